# Optimizing a Trainium2 kernel written in Bass

```python
import jax
import jax.numpy as jnp
from jax import lax
import numpy as np

D_MODEL = 1024
BATCH = 8
SEQ = 4096
DEPTH = 2

GRID_W = 64
CTX_LEN = 256
NORM_EPS = 1e-6
ROPE_THETA = 10000.0
Q_BLOCK = 128

GDN_HEADS = 8
GDN_DK = 64
GDN_DV = 64
GDN_CONV = 5
GDN_CHUNK = 64
GDN_QK_W = GDN_HEADS * GDN_DK
GDN_V_W = GDN_HEADS * GDN_DV

MLA_HEADS = 8
MLA_NOPE = 64
MLA_ROPE = 32
MLA_V = 64
MLA_Q_LORA = 256
MLA_KV_LORA = 128

NA_HEADS = 8
NA_DIM = 64
NA_WIN_H = 8
NA_WIN_W = 16
NA_W = NA_HEADS * NA_DIM

GQA_HEADS = 8
GQA_KV_HEADS = 2
GQA_DIM = 64

D_FF = -(-(8 * D_MODEL) // (3 * 256)) * 256

EVEN_SPLITS = (2 * GDN_QK_W + GDN_V_W, GDN_V_W, 2 * GDN_HEADS, 2 * GDN_HEADS, MLA_Q_LORA, MLA_KV_LORA, MLA_ROPE)
ODD_SPLITS = (NA_W, NA_W, NA_W, GQA_HEADS * GQA_DIM, GQA_KV_HEADS * GQA_DIM, GQA_KV_HEADS * GQA_DIM)
IN_EVEN = sum(EVEN_SPLITS)
IN_ODD = sum(ODD_SPLITS)
MIX_EVEN = GDN_V_W + MLA_HEADS * MLA_V
MIX_ODD = NA_W + GQA_HEADS * GQA_DIM
N_EVEN = (DEPTH + 1) // 2
N_ODD = DEPTH // 2

kernel_name = 'hybrid_gdn_mla_natten_gqa_prefix_dit'


def split_cols(z, sizes):
    offs = [int(o) for o in np.cumsum(sizes)[:-1]]
    return jnp.split(z, offs, axis=-1)


def to_heads(t, n):
    return t.reshape(t.shape[0], t.shape[1], n, -1)


def rms_norm(x, g):
    xf = x.astype(jnp.float32)
    y = xf * lax.rsqrt(jnp.mean(xf * xf, -1, keepdims=True) + NORM_EPS)
    return (y * g.astype(jnp.float32)).astype(x.dtype)


def l2_normalize(x):
    xf = x.astype(jnp.float32)
    return (xf * lax.rsqrt(jnp.sum(xf * xf, -1, keepdims=True) + NORM_EPS)).astype(x.dtype)


def modulate(h, shift, scale):
    return h * (1.0 + scale) + shift


def swiglu(u, w_gate, w_up, w_down):
    return (jax.nn.silu(u @ w_gate) * (u @ w_up)) @ w_down


def axial_rope(n_tokens, rot_dim):
    t = jnp.arange(n_tokens, dtype=jnp.int32)
    row = (t // GRID_W).astype(jnp.float32)
    col = (t % GRID_W).astype(jnp.float32)
    n_freq = rot_dim // 4
    inv_freq = ROPE_THETA ** (-jnp.arange(n_freq, dtype=jnp.float32) / n_freq)
    ang = jnp.concatenate([row[:, None] * inv_freq, col[:, None] * inv_freq], -1)
    return jnp.cos(ang), jnp.sin(ang)


def apply_rope(x, cos, sin):
    half = x.shape[-1] // 2
    x1, x2 = x[..., :half], x[..., half:]
    cs, sn = cos[None, :, None, :], sin[None, :, None, :]
    return jnp.concatenate([x1 * cs - x2 * sn, x2 * cs + x1 * sn], -1).astype(x.dtype)


def short_conv(x, w):
    n_tap = w.shape[0]
    pad = n_tap // 2
    t_len = x.shape[1]
    xp = jnp.pad(x, ((0, 0), (pad, pad), (0, 0)))
    out = xp[:, 0:t_len] * w[0]
    for j in range(1, n_tap):
        out = out + xp[:, j:j + t_len] * w[j]
    return out


def blocked_attention(q, k, v, scale):
    b, tq, h, d = q.shape
    hk = k.shape[2]
    grp = h // hk
    nb = tq // Q_BLOCK
    qb = q.reshape(b, nb, Q_BLOCK, hk, grp, d).transpose(1, 0, 2, 3, 4, 5)

    def one_block(q_i):
        s = jnp.einsum('bqkgd,bskd->bkgqs', q_i, k).astype(jnp.float32) * scale
        p = jax.nn.softmax(s, axis=-1).astype(v.dtype)
        return jnp.einsum('bkgqs,bskd->bqkgd', p, v)

    o = lax.map(one_block, qb)
    return o.transpose(1, 0, 2, 3, 4, 5).reshape(b, tq, h, v.shape[-1])


def delta_rule_chunked(q, k, v, g, beta, s0):
    f32 = jnp.float32
    b, t_len, h, dk = q.shape
    dv = v.shape[-1]
    n = t_len // GDN_CHUNK
    cl = GDN_CHUNK

    def chunks(t):
        return jnp.moveaxis(t.astype(f32).reshape(b, n, cl, h, *t.shape[3:]), 3, 1)

    qc, kc, vc, gch, bc = chunks(q), chunks(k), chunks(v), chunks(g), chunks(beta)
    gc = jnp.cumsum(gch, -1)
    tril = jnp.tril(jnp.ones((cl, cl), bool))
    strict = jnp.tril(jnp.ones((cl, cl), bool), -1)
    decay = jnp.exp(jnp.where(tril, gc[..., :, None] - gc[..., None, :], -jnp.inf))
    kb = kc * bc[..., None]
    vb = vc * bc[..., None]
    m = jnp.where(strict, jnp.einsum('bhncd,bhnsd->bhncs', kb, kc) * decay, 0.0)
    eye = jnp.eye(cl, dtype=f32)
    tmat = lax.linalg.triangular_solve(m + eye, jnp.broadcast_to(eye, m.shape), left_side=True,
                                       lower=True, unit_diagonal=True)
    u = jnp.einsum('bhncs,bhnsv->bhncv', tmat, vb)
    w = jnp.einsum('bhncs,bhnsk->bhnck', tmat, kb * jnp.exp(gc)[..., None])
    qk = jnp.einsum('bhncd,bhnsd->bhncs', qc, kc) * decay
    q_dec = qc * jnp.exp(gc)[..., None]
    k_dec = kc * jnp.exp(gc[..., -1:] - gc)[..., None]
    g_last = jnp.exp(gc[..., -1])
    xs = tuple(jnp.moveaxis(t, 2, 0) for t in (u, w, qk, q_dec, k_dec, g_last))

    def step(s, inp):
        u_i, w_i, qk_i, qd_i, kd_i, gl_i = inp
        v_new = u_i - jnp.einsum('bhck,bhkv->bhcv', w_i, s)
        o_i = jnp.einsum('bhck,bhkv->bhcv', qd_i, s) + jnp.einsum('bhcs,bhsv->bhcv', qk_i, v_new)
        s = s * gl_i[..., None, None] + jnp.einsum('bhck,bhcv->bhkv', kd_i, v_new)
        return s, o_i

    s_fin, o = lax.scan(step, s0.astype(f32), xs)
    return s_fin, o.transpose(1, 0, 3, 2, 4).reshape(b, t_len, h, dv)


def gdn_prepare(qkv, a, bt, conv_w, a_log, dt_bias):
    b, t_len, _ = qkv.shape
    qkv = jax.nn.silu(short_conv(qkv, conv_w))
    q, k, v = jnp.split(qkv, [GDN_QK_W, 2 * GDN_QK_W], axis=-1)
    q = l2_normalize(q.reshape(b, t_len, GDN_HEADS, GDN_DK)) * (GDN_DK ** -0.5)
    k = l2_normalize(k.reshape(b, t_len, GDN_HEADS, GDN_DK))
    v = v.reshape(b, t_len, GDN_HEADS, GDN_DV)
    a = a.reshape(b, t_len, 2, GDN_HEADS).astype(jnp.float32)
    bt = bt.reshape(b, t_len, 2, GDN_HEADS).astype(jnp.float32)
    g = -jnp.exp(a_log.astype(jnp.float32)) * jax.nn.softplus(a + dt_bias.astype(jnp.float32))
    return q, k, v, g, jax.nn.sigmoid(bt)


def gdn_bidirectional(lat, ctx):
    q_l, k_l, v_l, g_l, b_l = lat
    q_c, k_c, v_c, g_c, b_c = ctx
    s0 = jnp.zeros((q_l.shape[0], GDN_HEADS, GDN_DK, GDN_DV), jnp.float32)
    s_cf, oc_f = delta_rule_chunked(q_c, k_c, v_c, g_c[:, :, 0], b_c[:, :, 0], s0)
    _, ol_f = delta_rule_chunked(q_l, k_l, v_l, g_l[:, :, 0], b_l[:, :, 0], s_cf)
    fl = lambda t: jnp.flip(t, 1)
    s_cb, oc_b = delta_rule_chunked(fl(q_c), fl(k_c), fl(v_c), fl(g_c[:, :, 1]), fl(b_c[:, :, 1]), s0)
    _, ol_b = delta_rule_chunked(fl(q_l), fl(k_l), fl(v_l), fl(g_l[:, :, 1]), fl(b_l[:, :, 1]), s_cb)
    return ol_f + fl(ol_b), oc_f + fl(oc_b)


def gdn_output(o, gate, out_norm):
    b, t_len, h, dv = o.shape
    o = rms_norm(o.astype(gate.dtype), out_norm)
    return (o * jax.nn.silu(gate.reshape(b, t_len, h, dv))).reshape(b, t_len, h * dv)


def mla_q(q_down, q_norm, w_q_up, rope):
    b, t_len, _ = q_down.shape
    q = (rms_norm(q_down, q_norm) @ w_q_up).reshape(b, t_len, MLA_HEADS, MLA_NOPE + MLA_ROPE)
    if rope is None:
        return q
    return jnp.concatenate([q[..., :MLA_NOPE], apply_rope(q[..., MLA_NOPE:], *rope)], -1)


def mla_kv(kv_down, k_pe, kv_norm, w_kv_up, rope):
    b, t_len, _ = kv_down.shape
    kv = (rms_norm(kv_down, kv_norm) @ w_kv_up).reshape(b, t_len, MLA_HEADS, MLA_NOPE + MLA_V)
    k_pe = k_pe[:, :, None, :]
    if rope is not None:
        k_pe = apply_rope(k_pe, *rope)
    k = jnp.concatenate([kv[..., :MLA_NOPE], jnp.broadcast_to(k_pe, (b, t_len, MLA_HEADS, MLA_ROPE))], -1)
    return k, kv[..., MLA_NOPE:]


def neighbourhood_attention(q, k, v, k_ctx, v_ctx, rpb):
    b, s_len, h, d = q.shape
    rows = s_len // GRID_W
    kh = min(NA_WIN_H, rows)
    kw = NA_WIN_W
    nk = kh * kw
    r = jnp.arange(rows)
    col = jnp.arange(GRID_W)
    rs = jnp.clip(r - kh // 2, 0, rows - kh)
    cs = jnp.clip(col - kw // 2, 0, GRID_W - kw)
    kr = rs[:, None] + jnp.arange(kh)
    kc = cs[:, None] + jnp.arange(kw)
    idx = (kr[:, None, :, None] * GRID_W + kc[None, :, None, :]).reshape(rows, GRID_W * nk)
    dr = kr - r[:, None] + (NA_WIN_H - 1)
    dc = kc - col[:, None] + (NA_WIN_W - 1)
    bias = rpb[:, dr[:, None, :, None], dc[None, :, None, :]]
    bias = bias.reshape(h, rows, GRID_W, nk).transpose(1, 0, 2, 3).astype(jnp.float32)
    qr = q.reshape(b, rows, GRID_W, h, d).transpose(1, 0, 2, 3, 4)
    scale = d ** -0.5

    def row_block(inp):
        q_i, idx_i, b_i = inp
        kg = jnp.take(k, idx_i, axis=1).reshape(b, GRID_W, nk, h, d)
        vg = jnp.take(v, idx_i, axis=1).reshape(b, GRID_W, nk, h, d)
        s_loc = jnp.einsum('bqhd,bqnhd->bhqn', q_i, kg).astype(jnp.float32) * scale + b_i[None]
        s_ctx = jnp.einsum('bqhd,bshd->bhqs', q_i, k_ctx).astype(jnp.float32) * scale
        p = jax.nn.softmax(jnp.concatenate([s_loc, s_ctx], -1), axis=-1).astype(v.dtype)
        return (jnp.einsum('bhqn,bqnhd->bqhd', p[..., :nk], vg)
                + jnp.einsum('bhqs,bshd->bqhd', p[..., nk:], v_ctx))

    o = lax.map(row_block, (qr, idx, bias))
    return o.transpose(1, 0, 2, 3, 4).reshape(b, s_len, h, d)


def even_mixer(u_lat, u_ctx, need_ctx, w_in, w_out, conv_w, a_log, dt_bias, out_norm,
               q_norm, w_q_up, kv_norm, w_kv_up, rope):
    b, s_len, _ = u_lat.shape
    lat = split_cols(u_lat @ w_in, EVEN_SPLITS)
    ctx = split_cols(u_ctx @ w_in, EVEN_SPLITS)
    o_lat, o_ctx = gdn_bidirectional(gdn_prepare(lat[0], lat[2], lat[3], conv_w, a_log, dt_bias),
                                     gdn_prepare(ctx[0], ctx[2], ctx[3], conv_w, a_log, dt_bias))
    a_lat = gdn_output(o_lat, lat[1], out_norm)
    k_c, v_c = mla_kv(ctx[5], ctx[6], kv_norm, w_kv_up, None)
    k_l, v_l = mla_kv(lat[5], lat[6], kv_norm, w_kv_up, rope)
    q_l = mla_q(lat[4], q_norm, w_q_up, rope)
    scale = (MLA_NOPE + MLA_ROPE) ** -0.5
    b_lat = blocked_attention(q_l, jnp.concatenate([k_c, k_l], 1), jnp.concatenate([v_c, v_l], 1), scale)
    y_lat = jnp.concatenate([a_lat, b_lat.reshape(b, s_len, -1)], -1) @ w_out
    if not need_ctx:
        return y_lat, None
    l_len = u_ctx.shape[1]
    a_ctx = gdn_output(o_ctx, ctx[1], out_norm)
    b_ctx = blocked_attention(mla_q(ctx[4], q_norm, w_q_up, None), k_c, v_c, scale)
    y_ctx = jnp.concatenate([a_ctx, b_ctx.reshape(b, l_len, -1)], -1) @ w_out
    return y_lat, y_ctx


def odd_mixer(u_lat, u_ctx, need_ctx, w_in, w_out, rpb, q_norm, k_norm, rope):
    b, s_len, _ = u_lat.shape
    lat = split_cols(u_lat @ w_in, ODD_SPLITS)
    ctx = split_cols(u_ctx @ w_in, ODD_SPLITS)
    kn_c, vn_c = to_heads(ctx[1], NA_HEADS), to_heads(ctx[2], NA_HEADS)
    c_lat = neighbourhood_attention(to_heads(lat[0], NA_HEADS), to_heads(lat[1], NA_HEADS),
                                    to_heads(lat[2], NA_HEADS), kn_c, vn_c, rpb)
    kd_c = rms_norm(to_heads(ctx[4], GQA_KV_HEADS), k_norm)
    vd_c = to_heads(ctx[5], GQA_KV_HEADS)
    qd_l = apply_rope(rms_norm(to_heads(lat[3], GQA_HEADS), q_norm), *rope)
    kd_l = apply_rope(rms_norm(to_heads(lat[4], GQA_KV_HEADS), k_norm), *rope)
    scale = GQA_DIM ** -0.5
    d_lat = blocked_attention(qd_l, jnp.concatenate([kd_c, kd_l], 1),
                              jnp.concatenate([vd_c, to_heads(lat[5], GQA_KV_HEADS)], 1), scale)
    y_lat = jnp.concatenate([c_lat.reshape(b, s_len, -1), d_lat.reshape(b, s_len, -1)], -1) @ w_out
    if not need_ctx:
        return y_lat, None
    l_len = u_ctx.shape[1]
    c_ctx_o = blocked_attention(to_heads(ctx[0], NA_HEADS), kn_c, vn_c, NA_DIM ** -0.5)
    d_ctx = blocked_attention(rms_norm(to_heads(ctx[3], GQA_HEADS), q_norm), kd_c, vd_c, scale)
    y_ctx = jnp.concatenate([c_ctx_o.reshape(b, l_len, -1), d_ctx.reshape(b, l_len, -1)], -1) @ w_out
    return y_lat, y_ctx


def setup_inputs(seed: int = 0) -> dict:
    key = jax.random.key(seed)
    ks = iter(jax.random.split(key, 40))
    f32 = jnp.float32

    def nrm(shape, scale):
        return jax.random.normal(next(ks), shape, f32) * scale

    def gain(shape):
        return 1.0 + 0.05 * jax.random.normal(next(ks), shape, f32)

    d = D_MODEL
    x = nrm((BATCH, SEQ, d), 1.0)
    c = nrm((BATCH, d), 1.0)
    ctx = nrm((BATCH, CTX_LEN, d), 1.0)
    c_ctx = nrm((d,), 1.0)
    w_mod = nrm((DEPTH, d, 6 * d), 0.5 * d ** -0.5)
    b_mod = nrm((DEPTH, 6 * d), 0.01)
    g_pre_mix = gain((DEPTH, d))
    g_post_mix = gain((DEPTH, d))
    g_pre_ffn = gain((DEPTH, d))
    g_post_ffn = gain((DEPTH, d))
    w_ffn_gate = nrm((DEPTH, d, D_FF), d ** -0.5)
    w_ffn_up = nrm((DEPTH, d, D_FF), d ** -0.5)
    w_ffn_down = nrm((DEPTH, D_FF, d), D_FF ** -0.5)
    w_in_even = nrm((N_EVEN, d, IN_EVEN), d ** -0.5)
    w_out_even = nrm((N_EVEN, MIX_EVEN, d), MIX_EVEN ** -0.5)
    gdn_conv = nrm((N_EVEN, GDN_CONV, 2 * GDN_QK_W + GDN_V_W), GDN_CONV ** -0.5)
    gdn_a_log = jnp.log(jax.random.uniform(next(ks), (N_EVEN, 2, GDN_HEADS), f32, 1.0, 16.0))
    dt = jnp.exp(jax.random.uniform(next(ks), (N_EVEN, 2, GDN_HEADS), f32,
                                    float(np.log(1e-3)), float(np.log(1e-1))))
    gdn_dt_bias = dt + jnp.log(-jnp.expm1(-dt))
    gdn_out_norm = gain((N_EVEN, GDN_DV))
    mla_q_norm = gain((N_EVEN, MLA_Q_LORA))
    mla_w_q_up = nrm((N_EVEN, MLA_Q_LORA, MLA_HEADS * (MLA_NOPE + MLA_ROPE)), MLA_Q_LORA ** -0.5)
    mla_kv_norm = gain((N_EVEN, MLA_KV_LORA))
    mla_w_kv_up = nrm((N_EVEN, MLA_KV_LORA, MLA_HEADS * (MLA_NOPE + MLA_V)), MLA_KV_LORA ** -0.5)
    w_in_odd = nrm((N_ODD, d, IN_ODD), d ** -0.5)
    w_out_odd = nrm((N_ODD, MIX_ODD, d), MIX_ODD ** -0.5)
    na_rpb = nrm((N_ODD, NA_HEADS, 2 * NA_WIN_H - 1, 2 * NA_WIN_W - 1), 0.1)
    gqa_q_norm = gain((N_ODD, GQA_DIM))
    gqa_k_norm = gain((N_ODD, GQA_DIM))
    return {'x': x, 'c': c, 'ctx': ctx, 'c_ctx': c_ctx, 'w_mod': w_mod, 'b_mod': b_mod,
            'g_pre_mix': g_pre_mix, 'g_post_mix': g_post_mix, 'g_pre_ffn': g_pre_ffn,
            'g_post_ffn': g_post_ffn, 'w_ffn_gate': w_ffn_gate, 'w_ffn_up': w_ffn_up,
            'w_ffn_down': w_ffn_down, 'w_in_even': w_in_even, 'w_out_even': w_out_even,
            'gdn_conv': gdn_conv, 'gdn_a_log': gdn_a_log, 'gdn_dt_bias': gdn_dt_bias,
            'gdn_out_norm': gdn_out_norm, 'mla_q_norm': mla_q_norm, 'mla_w_q_up': mla_w_q_up,
            'mla_kv_norm': mla_kv_norm, 'mla_w_kv_up': mla_w_kv_up, 'w_in_odd': w_in_odd,
            'w_out_odd': w_out_odd, 'na_rpb': na_rpb, 'gqa_q_norm': gqa_q_norm, 'gqa_k_norm': gqa_k_norm}


def reference(x, c, ctx, c_ctx, w_mod, b_mod, g_pre_mix, g_post_mix, g_pre_ffn, g_post_ffn,
              w_ffn_gate, w_ffn_up, w_ffn_down, w_in_even, w_out_even, gdn_conv, gdn_a_log,
              gdn_dt_bias, gdn_out_norm, mla_q_norm, mla_w_q_up, mla_kv_norm, mla_w_kv_up,
              w_in_odd, w_out_odd, na_rpb, gqa_q_norm, gqa_k_norm):
    s_len = x.shape[1]
    rope_mla = axial_rope(s_len, MLA_ROPE)
    rope_gqa = axial_rope(s_len, GQA_DIM)
    h, hc = x, ctx
    for i in range(DEPTH):
        last = i == DEPTH - 1
        mod = (jax.nn.silu(c) @ w_mod[i] + b_mod[i])[:, None, :]
        mod_c = (jax.nn.silu(c_ctx) @ w_mod[i] + b_mod[i])[None, None, :]
        sh_m, sc_m, gt_m, sh_f, sc_f, gt_f = jnp.split(mod, 6, axis=-1)
        csh_m, csc_m, cgt_m, csh_f, csc_f, cgt_f = jnp.split(mod_c, 6, axis=-1)
        u = modulate(rms_norm(h, g_pre_mix[i]), sh_m, sc_m)
        uc = modulate(rms_norm(hc, g_pre_mix[i]), csh_m, csc_m)
        j = i // 2
        if i % 2 == 0:
            y, yc = even_mixer(u, uc, not last, w_in_even[j], w_out_even[j], gdn_conv[j], gdn_a_log[j],
                               gdn_dt_bias[j], gdn_out_norm[j], mla_q_norm[j], mla_w_q_up[j],
                               mla_kv_norm[j], mla_w_kv_up[j], rope_mla)
        else:
            y, yc = odd_mixer(u, uc, not last, w_in_odd[j], w_out_odd[j], na_rpb[j],
                              gqa_q_norm[j], gqa_k_norm[j], rope_gqa)
        h = h + gt_m * rms_norm(y, g_post_mix[i])
        f = swiglu(modulate(rms_norm(h, g_pre_ffn[i]), sh_f, sc_f), w_ffn_gate[i], w_ffn_up[i], w_ffn_down[i])
        h = h + gt_f * rms_norm(f, g_post_ffn[i])
        if not last:
            hc = hc + cgt_m * rms_norm(yc, g_post_mix[i])
            fc = swiglu(modulate(rms_norm(hc, g_pre_ffn[i]), csh_f, csc_f), w_ffn_gate[i], w_ffn_up[i], w_ffn_down[i])
            hc = hc + cgt_f * rms_norm(fc, g_post_ffn[i])
    return h
```

```python
import contextlib
import numpy as np
import concourse.bass as bass
import concourse.mybir as mybir
from concourse.bass_utils import run_bass_kernel_spmd

F32 = mybir.dt.float32
BF16 = mybir.dt.bfloat16
U8 = mybir.dt.uint8
AF = mybir.ActivationFunctionType
ALU = mybir.AluOpType
AX = mybir.AxisListType

ENGS = ("pe", "act", "dve", "pool", "sp")
N_HW_SEMS = 16
N_SW_SEMS = 8
N_DMA_SEMS = N_HW_SEMS + N_SW_SEMS
D = 1024
CTX = 256
DFF = 2816
EPS = 1e-6
NEG = -30000.0


class Ins:
    __slots__ = ("eng", "fn", "deps", "signal", "seq", "dma", "dsem", "dval", "prev_dma")

    def __init__(self, eng, fn, dma=False):
        self.eng = eng
        self.fn = fn
        self.deps = []
        self.signal = False
        self.seq = 0
        self.dma = dma
        self.dsem = -1
        self.dval = 0
        self.prev_dma = None


class Buf:
    __slots__ = ("name", "writer", "readers", "ws")

    def __init__(self, name=""):
        self.name = name
        self.writer = None
        self.readers = []
        self.ws = []


class Prog:
    def __init__(self, nc):
        self.nc = nc
        self.ins = []
        self.dma_rr = 0
        self.sw_rr = 0
        self.dma_last = [None] * N_DMA_SEMS
        self.last = {}
        self.open_dmas = []

    def op(self, eng, fn, reads=(), writes=(), dma=False, indep=False):
        i = Ins(eng, fn, dma)
        deps = []
        for b in reads:
            if b.writer is not None:
                deps.append(b.writer)
            deps.extend(b.ws)
        for b in writes:
            if b.writer is not None:
                deps.append(b.writer)
            deps.extend(b.readers)
            if not indep:
                deps.extend(b.ws)
        seen = set()
        for d in deps:
            if id(d) in seen or d is i:
                continue
            seen.add(id(d))
            if d.eng == "pe" and eng == "pe" and not d.dma and not dma:
                continue
            d.signal = True
            i.deps.append(d)
        for b in reads:
            b.readers.append(i)
        for b in writes:
            if indep:
                b.ws.append(i)
            else:
                b.writer = i
                b.readers = []
                b.ws = []
        if dma:
            if eng == "pool":
                k = N_HW_SEMS + self.sw_rr % N_SW_SEMS
                self.sw_rr += 1
            else:
                k = self.dma_rr % N_HW_SEMS
                self.dma_rr += 1
            i.dsem = k
            i.prev_dma = self.dma_last[k]
            self.dma_last[k] = i
            i.signal = True
            self.open_dmas.append(i)
        else:
            self.last[eng] = i
        self.ins.append(i)
        return i

    def dma(self, q, out, in_, reads=(), writes=(), slow=False):
        indep = str(out.space) == "DRAM"
        if slow:
            return self.op(q, I("dma_start", out=out, in_=in_, allow_slow_non_contiguous=True),
                           reads, writes, dma=True, indep=indep)
        return self.op(q, I("dma_start", out=out, in_=in_), reads, writes, dma=True, indep=indep)

    def barrier(self):
        lasts = [v for v in self.last.values()]
        for l in lasts:
            l.signal = True
        dmas = self.open_dmas
        self.open_dmas = []
        for e in ENGS:
            i = Ins(e, None)
            i.deps = lasts + dmas
            self.ins.append(i)

    def emit(self, final_wait_eng="sp"):
        nc = self.nc
        with contextlib.ExitStack() as st:
            esem = {e: st.enter_context(nc.semaphore("s_" + e)) for e in ENGS}
            dsem = [st.enter_context(nc.semaphore("d%d" % k)) for k in range(N_DMA_SEMS)]
            cnt = {e: 0 for e in ENGS}
            dcnt = [0] * N_DMA_SEMS
            for i in self.ins:
                if i.dma:
                    dcnt[i.dsem] += 16
                    i.dval = dcnt[i.dsem]
                elif i.signal:
                    cnt[i.eng] += 1
                    i.seq = cnt[i.eng]
            per = {e: [] for e in ENGS}
            for i in self.ins:
                per[i.eng].append(i)
            last_dmas = [d for d in self.dma_last if d is not None]
            block = st.enter_context(nc.Block())
            engobj = {"pe": "tensor", "act": "scalar", "dve": "vector", "pool": "gpsimd", "sp": "sync"}

            def run(ename, e):
                seen_e = {x: 0 for x in ENGS}
                seen_d = [0] * N_DMA_SEMS
                for i in per[ename]:
                    deps = i.deps
                    if i.dma and i.prev_dma is not None:
                        deps = deps + [i.prev_dma]
                    for d in deps:
                        if d.dma:
                            if seen_d[d.dsem] < d.dval:
                                e.wait_ge(dsem[d.dsem], d.dval)
                                seen_d[d.dsem] = d.dval
                        else:
                            if seen_e[d.eng] < d.seq:
                                e.wait_ge(esem[d.eng], d.seq)
                                seen_e[d.eng] = d.seq
                    if i.fn is None:
                        continue
                    r = i.fn(e)
                    if i.dma:
                        r.then_inc(dsem[i.dsem], 16)
                    elif i.signal:
                        r.then_inc(esem[i.eng], 1)
                if ename == final_wait_eng:
                    for d in last_dmas:
                        if seen_d[d.dsem] < d.dval:
                            e.wait_ge(dsem[d.dsem], d.dval)
                            seen_d[d.dsem] = d.dval

            for ename in ENGS:
                getattr(block, engobj[ename])(lambda e, ename=ename: run(ename, e))


class Tl:
    __slots__ = ("ap", "b")

    def __init__(self, ap, name=""):
        self.ap = ap
        self.b = Buf(name)


class KB:
    def __init__(self, nc, P, arena, psum, arena_bytes):
        self.nc, self.P, self.arena, self.psum = nc, P, arena, psum
        self.off = 0
        self.cap = arena_bytes
        self.pcache = {}

    def reset(self, to=0):
        self.off = to
        self.pcache = {}

    def sb(self, free, dt, name=""):
        if isinstance(free, int):
            free = [free]
        n = int(np.prod(free))
        sz = 2 if dt == BF16 else 4
        nbytes = ((n * sz + 63) // 64) * 64
        o = self.off
        self.off += nbytes
        assert self.off <= self.cap, "SBUF arena overflow at %s: %d" % (name, self.off)
        ap = self.arena[:, o:o + n * sz].bitcast(dt)
        if len(free) == 2:
            ap = ap.rearrange("p (a b) -> p a b", a=free[0])
        elif len(free) == 3:
            ap = ap.rearrange("p (a b c) -> p a b c", a=free[0], b=free[1])
        return Tl(ap, name)

    def ps(self, bank, cols=512, off=0, name=""):
        key = (bank, cols, off)
        if key not in self.pcache:
            self.pcache[key] = Tl(self.psum[:, bank * 512 + off: bank * 512 + off + cols], name)
        return self.pcache[key]


def bcast_mid(ap2, n):
    return ap2.unsqueeze(2).to_broadcast([ap2.shape[0], ap2.shape[1], n])


def bcast_head(ap2, a):
    return ap2.unsqueeze(1).to_broadcast([ap2.shape[0], a, ap2.shape[1]])


def I(m, *a, **kw):
    return lambda e: getattr(e, m)(*a, **kw)


class Builder:
    def __init__(self, SEQ, dbg=(), upto=None):
        self.SEQ = SEQ
        self.T = CTX + SEQ
        self.NT = self.T // 128
        self.dbg = set(dbg)
        self.upto = upto
        self.nc = bass.Bass("TRN2", target_bir_lowering=False)
        self.P = Prog(self.nc)
        self.din = {}
        self.dsc = {}
        self.hb = [Buf("hb%d" % n) for n in range(self.NT)]

    def inp(self, name, shape, dt=F32):
        self.din[name] = self.nc.dram_tensor(name, list(shape), dt, kind="ExternalInput").ap()
        return self.din[name]

    def scratch(self, name, shape, dt=F32):
        kind = "ExternalOutput" if name in self.dbg else "Internal"
        ap = self.nc.dram_tensor(name, list(shape), dt, kind=kind).ap()
        self.dsc[name] = Tl(ap, name)
        return self.dsc[name]

    def tok_blocks(self, bs=512):
        out = [(0, CTX)]
        c = CTX
        while c < self.T:
            out.append((c, min(bs, self.T - c)))
            c += bs
        return out

    def build(self):
        nc, P = self.nc, self.P
        SEQ, T, NT = self.SEQ, self.T, self.NT
        i_ = self.inp
        x = i_("x", [SEQ, D]); ctx = i_("ctx", [CTX, D]); c = i_("c", [D]); c_ctx = i_("c_ctx", [D])
        w_mod = i_("w_mod", [2, D, 6 * D]); b_mod = i_("b_mod", [2, 6 * D])
        g_pre_mix = i_("g_pre_mix", [2, D]); g_post_mix = i_("g_post_mix", [2, D])
        g_pre_ffn = i_("g_pre_ffn", [2, D]); g_post_ffn = i_("g_post_ffn", [2, D])
        w_ffn_gate = i_("w_ffn_gate", [2, D, DFF]); w_ffn_up = i_("w_ffn_up", [2, D, DFF])
        w_ffn_down = i_("w_ffn_down", [2, DFF, D])
        w_in_even = i_("w_in_even", [1, D, 2496]); w_out_even = i_("w_out_even", [1, D, D])
        gdn_conv = i_("gdn_conv", [1, 5, 1536]); gdn_a_log = i_("gdn_a_log", [1, 2, 8])
        gdn_dt_bias = i_("gdn_dt_bias", [1, 2, 8]); gdn_out_norm = i_("gdn_out_norm", [1, 64])
        mla_q_norm = i_("mla_q_norm", [1, 256]); mla_w_q_up = i_("mla_w_q_up", [1, 256, 768])
        mla_kv_norm = i_("mla_kv_norm", [1, 128]); mla_w_kv_up = i_("mla_w_kv_up", [1, 128, 1024])
        w_in_odd = i_("w_in_odd", [1, D, 2304]); w_out_odd = i_("w_out_odd", [1, D, D])
        na_bias = i_("na_bias", [5, 8, 128, 640])
        gqa_q_norm = i_("gqa_q_norm", [1, 64]); gqa_k_norm = i_("gqa_k_norm", [1, 64])
        k_identf = i_("k_identf", [128, 128]); k_identb = i_("k_identb", [128, 128], BF16)
        k_tril = i_("k_tril", [2, 128, 128])
        k_mstrict = i_("k_mstrict", [2, 128, 128])
        k_gmask = i_("k_gmask", [7, 128, 128])
        k_rope32 = i_("k_rope32", [2, 32, T])
        k_rope64 = i_("k_rope64", [2, 64, T])
        y = self.nc.dram_tensor("y", [SEQ, D], F32, kind="ExternalOutput").ap()
        self.y = Tl(y, "y")
        s_ = self.scratch
        s_("modv_d", [2, 2, 6 * D]); s_("uT_d", [8, 128, T], BF16); s_("hbuf_d", [T, D])
        s_("qkT_d", [16, 64, T]); s_("ktok_d", [T, 512]); s_("vtok_d", [T, 512])
        s_("gate_d", [T, 512]); s_("gb_d", [T, 32])
        s_("qn_d", [2, 128, T], BF16); s_("kvn_d", [128, T], BF16); s_("kpe_d", [32, T], BF16)
        s_("o_d", [2, T, 512]); s_("mixT_d", [8, 128, T], BF16)
        s_("oq_d", [24, 64, T], BF16); s_("ov_d", [T, 640], BF16); s_("gk_d", [2, 64, T], BF16)
        with contextlib.ExitStack() as st:
            AB = 204 * 1024
            arena = st.enter_context(nc.sbuf_tensor("arena", [128, AB], U8))
            psum = st.enter_context(nc.psum_tensor("psum", [128, 4096], F32))
            self.k = KB(nc, P, arena, psum, AB)
            stages = [
                ("mod", self.s_mod),
                ("inE", lambda: self.s_pre_in(0)),
                ("gdn", self.s_gdn),
                ("mla", self.s_mla),
                ("gdno", self.s_gdn_out),
                ("ffn0", lambda: self.s_outffn(0)),
                ("inO", lambda: self.s_pre_in(1)),
                ("na", self.s_na),
                ("gqa", self.s_gqa),
                ("ffn1", lambda: self.s_outffn(1)),
            ]
            for name, fn in stages:
                self.k.reset()
                fn()
                P.barrier()
                if self.upto == name:
                    break
            if self.upto is not None and self.upto != "ffn1":
                z = self.k.sb(D, F32)
                P.op("pool", I("memset", z.ap, 0.0), writes=[z.b])
                P.dma("sp", y[0:128, :], z.ap, reads=[z.b], writes=[self.y.b])
            P.emit()
        return nc

    def s_mod(self):
        k, P, d = self.k, self.P, self.din
        tmp = k.sb([8, 2], F32, "ctmp")
        csT = k.sb([8, 2], F32, "csT")
        P.dma("sp", tmp.ap[:, :, 0], d["c"].rearrange("(p k) -> p k", k=8), writes=[tmp.b], slow=True)
        P.dma("sp", tmp.ap[:, :, 1], d["c_ctx"].rearrange("(p k) -> p k", k=8), writes=[tmp.b], slow=True)
        P.op("act", I("activation", out=csT.ap, in_=tmp.ap, func=AF.Silu), reads=[tmp.b], writes=[csT.b])
        wb = [k.sb([8, 512], F32, "wmod%d" % i) for i in range(2)]
        bb = [k.sb(512, F32, "bmod%d" % i) for i in range(2)]
        ob = [k.sb(512, F32, "omod%d" % i) for i in range(2)]
        pss = [k.ps(0), k.ps(1)]
        modv = self.dsc["modv_d"]
        jobs = [(l, nb) for l in range(2) for nb in range(12)]

        def mod_load(it):
            l, nb = jobs[it]
            cs = slice(nb * 512, (nb + 1) * 512)
            wv = d["w_mod"][l].rearrange("(p k) n -> p k n", k=8)
            P.dma("sp", wb[it % 2].ap, wv[:, :, cs], writes=[wb[it % 2].b])
            P.dma("sp", bb[it % 2].ap[0:2, :], d["b_mod"][l, cs].partition_broadcast(2), writes=[bb[it % 2].b])

        mod_load(0)
        for it, (l, nb) in enumerate(jobs):
                w, b, o, ps = wb[it % 2], bb[it % 2], ob[it % 2], pss[it % 2]
                cs = slice(nb * 512, (nb + 1) * 512)
                for kk in range(8):
                    P.op("pe", I("matmul", ps.ap[0:2, :], lhsT=csT.ap[:, kk, :], rhs=w.ap[:, kk, :],
                                                                     start=(kk == 0), stop=(kk == 7)),
                         reads=[csT.b, w.b], writes=[ps.b])
                P.op("dve", I("tensor_tensor", out=o.ap[0:2, :], in0=ps.ap[0:2, :], in1=b.ap[0:2, :], op=ALU.add),
                     reads=[ps.b, b.b], writes=[o.b])
                if it + 1 < len(jobs):
                    mod_load(it + 1)
                P.dma("sp", modv.ap[l, :, cs], o.ap[0:2, :], reads=[o.b], writes=[modv.b])

    def load_mod_tiles(self, l, g_in, off_sh, off_sc):
        k, P = self.k, self.P
        modv = self.dsc["modv_d"]
        G = k.sb(D, F32, "G")
        P.dma("sp", G.ap, g_in[l].partition_broadcast(128), writes=[G.b])
        res = []
        for r in (1, 0):
            A = k.sb(D, F32, "A%d" % r)
            B = k.sb(D, F32, "B%d" % r)
            P.dma("sp", A.ap, modv.ap[l, r, off_sc:off_sc + D].partition_broadcast(128), reads=[modv.b], writes=[A.b])
            P.dma("sp", B.ap, modv.ap[l, r, off_sh:off_sh + D].partition_broadcast(128), reads=[modv.b], writes=[B.b])
            P.op("dve", I("scalar_tensor_tensor", out=A.ap, in0=A.ap, scalar=1.0, in1=G.ap, op0=ALU.add, op1=ALU.mult),
                 reads=[A.b, G.b], writes=[A.b])
            res += [A, B]
        return res + [G]

    def load_gate_tiles(self, l, g_in, off_gt):
        k, P = self.k, self.P
        modv = self.dsc["modv_d"]
        G = k.sb(D, F32, "Gp")
        P.dma("sp", G.ap, g_in[l].partition_broadcast(128), writes=[G.b])
        res = []
        for r in (1, 0):
            A = k.sb(D, F32, "Gt%d" % r)
            P.dma("sp", A.ap, modv.ap[l, r, off_gt:off_gt + D].partition_broadcast(128), reads=[modv.b], writes=[A.b])
            P.op("dve", I("tensor_tensor", out=A.ap, in0=A.ap, in1=G.ap, op=ALU.mult), reads=[A.b, G.b], writes=[A.b])
            res.append(A)
        return res + [G]

    def h_src(self, l, n):
        if l == 0:
            if n < 2:
                return self.din["ctx"][n * 128:(n + 1) * 128, :], None
            return self.din["x"][(n - 2) * 128:(n - 1) * 128, :], None
        hb = self.dsc["hbuf_d"]
        return hb.ap[n * 128:(n + 1) * 128, :], self.hb[n]

    def norm_mod_T(self, n, src_ap, src_b, tiles, bufs, uT_dst, it):
        k, P = self.k, self.P
        Ac, Bc, Al, Bl = tiles
        A, B = (Ac, Bc) if n < 2 else (Al, Bl)
        h, junk, st, t1, u, ps, identb = bufs["h"][it % 2], bufs["junk"], bufs["st"][it % 2], bufs["t1"][it % 2], \
            bufs["u"][it % 2], bufs["ps"][it % 2], bufs["identb"]
        P.dma("sp", h.ap, src_ap, reads=[src_b] if src_b else [], writes=[h.b])
        P.op("act", I("activation", out=junk.ap, in_=h.ap, func=AF.Square, accum_out=st.ap[:, 0:1]),
             reads=[h.b], writes=[junk.b, st.b])
        P.op("act", I("activation", out=st.ap[:, 1:2], in_=st.ap[:, 0:1], func=AF.Sqrt, scale=1.0 / D, bias=EPS),
             reads=[st.b], writes=[st.b])
        P.op("dve", I("reciprocal", out=st.ap[:, 2:3], in_=st.ap[:, 1:2]), reads=[st.b], writes=[st.b])
        P.op("dve", I("scalar_tensor_tensor", out=t1.ap, in0=h.ap, scalar=st.ap[:, 2:3], in1=A.ap, op0=ALU.mult, op1=ALU.mult),
             reads=[h.b, st.b, A.b], writes=[t1.b])
        P.op("pool", I("tensor_tensor", out=u.ap, in0=t1.ap, in1=B.ap, op=ALU.add), reads=[t1.b, B.b], writes=[u.b])
        psb = ps.ap.bitcast(BF16)
        for kk in range(8):
            P.op("pe", I("transpose", psb[:, kk * 128:(kk + 1) * 128], u.ap[:, kk * 128:(kk + 1) * 128], identb.ap),
                 reads=[u.b, identb.b], writes=[ps.b])
        P.op("act", I("activation", out=uT_dst.ap, in_=psb.rearrange("p (a b) -> p a b", a=8), func=AF.Copy),
             reads=[ps.b], writes=[uT_dst.b])
        return h

    def norm_bufs(self):
        k, P = self.k, self.P
        bufs = {
            "h": [k.sb(D, F32, "h%d" % i) for i in range(2)],
            "junk": k.sb(D, F32, "junk"),
            "st": [k.sb(8, F32, "st%d" % i) for i in range(2)],
            "t1": [k.sb(D, F32, "t1%d" % i) for i in range(2)],
            "u": [k.sb(D, BF16, "u%d" % i) for i in range(2)],
            "ps": [k.ps(6), k.ps(7)],
            "identb": k.sb(128, BF16, "identb"),
        }
        P.dma("sp", bufs["identb"].ap, self.din["k_identb"], writes=[bufs["identb"].b])
        return bufs

    def s_pre(self, l, sub, uT_sb=None):
        k, P = self.k, self.P
        tiles = self.load_mod_tiles(l, self.din["g_pre_mix"], 0, D)[:4]
        bufs = self.norm_bufs()
        for n in range(self.NT):
            src_ap, src_b = self.h_src(l, n)
            ut = Tl(uT_sb.ap[:, :, n * 128:(n + 1) * 128])
            ut.b = uT_sb.b
            self.norm_mod_T(n, src_ap, src_b, tiles, bufs, ut, n)

    def s_pre_in(self, l):
        k, P, d = self.k, self.P, self.din
        ncol = 2496 if l == 0 else 2304
        uT = k.sb([8, self.T], BF16, "uT")
        W = k.sb([8, ncol], BF16, "Win")
        wv = (d["w_in_even"] if l == 0 else d["w_in_odd"])[0].rearrange("(k p) n -> p k n", p=128)
        for kk in range(8):
            P.dma("pool", W.ap[:, kk, :], wv[:, kk, :], writes=[W.b])
        mark = k.off
        self.s_pre(l, "m", uT_sb=uT)
        P.barrier()
        k.reset(mark)
        if l == 0:
            self.s_inproj_even(pre=(uT, W))
        else:
            self.s_inproj_odd(pre=(uT, W))

    def s_inproj_even(self, pre):
        k, P, d, T, NT, SEQ = self.k, self.P, self.din, self.T, self.NT, self.SEQ
        sc = self.dsc
        uT, W = pre
        Wrot = k.sb([8, 32], BF16, "Wrot")
        P.op("act", I("activation", out=Wrot.ap[:, :, 0:16], in_=W.ap[:, :, 2480:2496], func=AF.Copy, scale=-1.0),
             reads=[W.b], writes=[Wrot.b])
        P.op("act", I("activation", out=Wrot.ap[:, :, 16:32], in_=W.ap[:, :, 2464:2480], func=AF.Copy),
             reads=[W.b], writes=[Wrot.b])
        identf = k.sb(128, F32, "identf")
        P.dma("sp", identf.ap, d["k_identf"], writes=[identf.b])
        ones = k.sb(128, F32, "ones")
        P.op("pool", I("memset", ones.ap, 1.0), writes=[ones.b])
        cwraw = k.sb(1536, F32, "cwraw")
        P.dma("sp", cwraw.ap[0:5, :], d["gdn_conv"][0], writes=[cwraw.b])
        cw = k.sb([12, 5], F32, "cw")
        pcw = k.ps(0)
        for pc in range(12):
            P.op("pe", I("transpose", out=pcw.ap[:, pc * 5:pc * 5 + 5], in_=cwraw.ap[0:5, pc * 128:(pc + 1) * 128], identity=identf.ap[0:5, 0:5]),
                 reads=[cwraw.b, identf.b], writes=[pcw.b])
        P.op("dve", I("tensor_copy", out=cw.ap, in_=pcw.ap[:, 0:60].rearrange("p (a b) -> p a b", a=12)),
             reads=[pcw.b], writes=[cw.b])
        bones = k.sb(128, F32, "bones")
        P.op("pool", I("memset", bones.ap, 0.0), writes=[bones.b])
        P.op("pool", I("memset", bones.ap[0:64, 0:64], 1.0), writes=[bones.b])
        P.op("pool", I("memset", bones.ap[64:128, 64:128], 1.0), writes=[bones.b])
        blocks = self.tok_blocks()
        mark = k.off
        import os
        sub = os.environ.get("KSUB", "")
        if sub == "a":
            return
        raws = [k.sb(T + 8, F32, "raw%d" % i) for i in range(2)]
        accs = [k.sb(T, F32, "acc%d" % i) for i in range(2)]
        for r_ in raws:
            P.op("pool", I("memset", r_.ap, 0.0), writes=[r_.b])
        sq = [k.sb(512, F32, "sq%d" % i) for i in range(2)]
        rn = [k.sb(512, F32, "rn%d" % i) for i in range(2)]
        stages = [k.sb([4, 128], F32, "tokstage%d" % i) for i in range(2)]
        pz = [k.ps(1), k.ps(2)]
        pss = [k.ps(3), k.ps(4)]
        ptr = [k.ps(5), k.ps(6)]
        qkT, ktok, vtok = sc["qkT_d"], sc["ktok_d"], sc["vtok_d"]

        def rawcol(c0):
            return c0 + 2 if c0 < CTX else c0 + 6

        cnt1 = {"it": 0}

        def front(pc):
            raw = raws[pc % 2]
            for bi, (c0, n) in enumerate(blocks):
                ps = pz[cnt1["it"] % 2]
                cnt1["it"] += 1
                for kk in range(8):
                    P.op("pe", I("matmul", ps.ap[:, 0:n], lhsT=W.ap[:, kk, pc * 128:(pc + 1) * 128], rhs=uT.ap[:, kk, c0:c0 + n],
                                 start=(kk == 0), stop=(kk == 7)), reads=[W.b, uT.b], writes=[ps.b])
                r0 = rawcol(c0)
                P.op("act", I("activation", out=raw.ap[:, r0:r0 + n], in_=ps.ap[:, 0:n], func=AF.Copy),
                     reads=[ps.b], writes=[raw.b])

        front(0)
        for pc in range(12):
            raw, acc = raws[pc % 2], accs[pc % 2]
            if pc + 1 < 12:
                front(pc + 1)
            for (a0, r0, L) in ((0, 0, CTX), (CTX, CTX + 4, SEQ)):
                P.op("dve", I("tensor_scalar", out=acc.ap[:, a0:a0 + L], in0=raw.ap[:, r0:r0 + L], scalar1=cw.ap[:, pc, 0:1], scalar2=None, op0=ALU.mult),
                     reads=[raw.b, cw.b], writes=[acc.b])
                for j in range(1, 5):
                    P.op("dve", I("scalar_tensor_tensor", out=acc.ap[:, a0:a0 + L], in0=raw.ap[:, r0 + j:r0 + j + L], scalar=cw.ap[:, pc, j:j + 1],
                                  in1=acc.ap[:, a0:a0 + L], op0=ALU.mult, op1=ALU.add), reads=[raw.b, cw.b, acc.b], writes=[acc.b])
            P.op("act", I("activation", out=acc.ap, in_=acc.ap, func=AF.Silu), reads=[acc.b], writes=[acc.b])
            if pc < 8:
                for bi, (c0, n) in enumerate(blocks):
                    s2, r2, ps = sq[bi % 2], rn[bi % 2], pss[bi % 2]
                    P.op("pool", I("tensor_tensor", out=s2.ap[:, 0:n], in0=acc.ap[:, c0:c0 + n], in1=acc.ap[:, c0:c0 + n], op=ALU.mult),
                         reads=[acc.b], writes=[s2.b])
                    P.op("pe", I("matmul", ps.ap[:, 0:n], lhsT=bones.ap, rhs=s2.ap[:, 0:n], start=True, stop=True), reads=[bones.b, s2.b], writes=[ps.b])
                    P.op("act", I("activation", out=r2.ap[:, 0:n], in_=ps.ap[:, 0:n], func=AF.Ln, bias=EPS), reads=[ps.b], writes=[r2.b])
                    P.op("act", I("activation", out=r2.ap[:, 0:n], in_=r2.ap[:, 0:n], func=AF.Exp, scale=-0.5), reads=[r2.b], writes=[r2.b])
                    scl = 0.125 if pc < 4 else 1.0
                    P.op("dve", I("scalar_tensor_tensor", out=acc.ap[:, c0:c0 + n], in0=acc.ap[:, c0:c0 + n], scalar=scl, in1=r2.ap[:, 0:n],
                                  op0=ALU.mult, op1=ALU.mult), reads=[acc.b, r2.b], writes=[acc.b])
                P.dma("sp", qkT.ap[2 * pc:2 * pc + 2].rearrange("h p t -> (h p) t"), acc.ap, reads=[acc.b], writes=[qkT.b])
            if pc >= 4:
                dst = ktok if pc < 8 else vtok
                p2 = pc % 4
                dstv = dst.ap[:, p2 * 128:(p2 + 1) * 128].rearrange("(n p) d -> p n d", p=128)
                for g0 in range(0, NT, 4):
                    ps = ptr[(g0 // 4) % 2]
                    stage = stages[(g0 // 4) % 2]
                    ng = min(4, NT - g0)
                    for i in range(ng):
                        n_ = g0 + i
                        P.op("pe", I("transpose", out=ps.ap[:, i * 128:(i + 1) * 128], in_=acc.ap[:, n_ * 128:(n_ + 1) * 128], identity=identf.ap),
                             reads=[acc.b, identf.b], writes=[ps.b])
                    P.op("act", I("activation", out=stage.ap[:, 0:ng, :], in_=ps.ap[:, 0:ng * 128].rearrange("p (a b) -> p a b", a=ng), func=AF.Copy),
                         reads=[ps.b], writes=[stage.b])
                    P.dma("act", dstv[:, g0:g0 + ng, :], stage.ap[:, 0:ng, :], reads=[stage.b], writes=[dst.b])
        if sub == "b":
            return
        P.barrier()
        k.reset(mark)
        dtb = k.sb(16, F32, "dtb")
        negA = k.sb(16, F32, "negA")
        P.dma("sp", dtb.ap, d["gdn_dt_bias"][0].rearrange("a b -> (a b)").partition_broadcast(128), writes=[dtb.b])
        P.dma("sp", negA.ap, d["gdn_a_log"][0].rearrange("a b -> (a b)").partition_broadcast(128), writes=[negA.b])
        P.op("act", I("activation", out=negA.ap, in_=negA.ap, func=AF.Exp), reads=[negA.b], writes=[negA.b])
        P.op("dve", I("tensor_scalar", out=negA.ap, in0=negA.ap, scalar1=-1.0, scalar2=None, op0=ALU.mult), reads=[negA.b], writes=[negA.b])
        gts = [k.sb(512, F32, "gt%d" % i) for i in range(2)]
        gbs = k.sb([NT, 32], F32, "gbs")
        tmp16 = [k.sb(16, F32, "tmp16%d" % i) for i in range(2)]
        pg = [k.ps(1), k.ps(2)]
        pab = [k.ps(3), k.ps(4)]
        gate_d, gb_d = sc["gate_d"], sc["gb_d"]
        for n in range(NT):
            p1, p2, gt, t16 = pg[n % 2], pab[n % 2], gts[n % 2], tmp16[n % 2]
            for kk in range(8):
                P.op("pe", I("matmul", p1.ap, lhsT=uT.ap[:, kk, n * 128:(n + 1) * 128], rhs=W.ap[:, kk, 1536:2048],
                                                                 start=(kk == 0), stop=(kk == 7)), reads=[W.b, uT.b], writes=[p1.b])
            for kk in range(8):
                P.op("pe", I("matmul", p2.ap[:, 0:32], lhsT=uT.ap[:, kk, n * 128:(n + 1) * 128], rhs=W.ap[:, kk, 2048:2080],
                                                                 start=(kk == 0), stop=(kk == 7)), reads=[W.b, uT.b], writes=[p2.b])
            P.op("act", I("activation", out=gt.ap, in_=p1.ap, func=AF.Silu), reads=[p1.b], writes=[gt.b])
            P.dma("sp", gate_d.ap[n * 128:(n + 1) * 128, :], gt.ap, reads=[gt.b], writes=[gate_d.b])
            P.op("dve", I("tensor_tensor", out=t16.ap, in0=p2.ap[:, 0:16], in1=dtb.ap, op=ALU.add),
                 reads=[p2.b, dtb.b], writes=[t16.b])
            P.op("act", I("activation", out=t16.ap, in_=t16.ap, func=AF.Exp), reads=[t16.b], writes=[t16.b])
            P.op("act", I("activation", out=t16.ap, in_=t16.ap, func=AF.Ln, bias=1.0), reads=[t16.b], writes=[t16.b])
            P.op("dve", I("tensor_tensor", out=gbs.ap[:, n, 0:16], in0=t16.ap, in1=negA.ap, op=ALU.mult),
                 reads=[t16.b, negA.b], writes=[gbs.b])
            P.op("act", I("activation", out=gbs.ap[:, n, 16:32], in_=p2.ap[:, 16:32], func=AF.Sigmoid),
                 reads=[p2.b], writes=[gbs.b])
        P.dma("sp", gb_d.ap.rearrange("(n p) d -> p n d", p=128), gbs.ap, reads=[gbs.b], writes=[gb_d.b])
        if sub == "c":
            return
        P.barrier()
        k.reset(mark)
        qg = k.sb(2, F32, "qg")
        kvg = k.sb(1, F32, "kvg")
        graw = k.sb(128, F32, "graw")
        P.dma("sp", graw.ap[0:2, :], d["mla_q_norm"][0].rearrange("(k p) -> k p", p=128), writes=[graw.b])
        P.dma("sp", graw.ap[2:3, :], d["mla_kv_norm"][0].rearrange("(k p) -> k p", p=128), writes=[graw.b])
        pgr = k.ps(7)
        P.op("pe", I("transpose", pgr.ap[:, 0:3], graw.ap[0:3, :], identf.ap[0:3, 0:3]), reads=[graw.b, identf.b], writes=[pgr.b])
        P.op("dve", I("tensor_copy", out=qg.ap, in_=pgr.ap[:, 0:2]), reads=[pgr.b], writes=[qg.b])
        P.op("dve", I("tensor_copy", out=kvg.ap, in_=pgr.ap[:, 2:3]), reads=[pgr.b], writes=[kvg.b])
        ropes = [k.sb([2, 512], F32, "rope32_%d" % i) for i in range(2)]
        zq = [k.sb([3, 512], F32, "zq%d" % i) for i in range(2)]
        sq3 = [k.sb([3, 512], F32, "sq3%d" % i) for i in range(2)]
        rs = [k.sb([2, 512], F32, "rs%d" % i) for i in range(2)]
        qo = [k.sb([3, 512], BF16, "qo%d" % i) for i in range(2)]
        kp = [k.sb([3, 512], F32, "kp%d" % i) for i in range(2)]
        kpo = [k.sb(512, BF16, "kpo%d" % i) for i in range(2)]
        pz3 = [k.ps(0), k.ps(1), k.ps(2)]
        pss2 = [k.ps(3), k.ps(4)]
        ppe = [k.ps(5), k.ps(6)]
        qn_d, kvn_d, kpe_d = sc["qn_d"], sc["kvn_d"], sc["kpe_d"]
        cols = [2080, 2208, 2336]
        for bi, (c0, n) in enumerate(blocks):
            z, s3, r, o = zq[bi % 2], sq3[bi % 2], rs[bi % 2], qo[bi % 2]
            for ci in range(3):
                ps = pz3[ci]
                for kk in range(8):
                    P.op("pe", I("matmul", ps.ap[:, 0:n], lhsT=W.ap[:, kk, cols[ci]:cols[ci] + 128], rhs=uT.ap[:, kk, c0:c0 + n],
                        start=(kk == 0), stop=(kk == 7)), reads=[W.b, uT.b], writes=[ps.b])
                P.op("act", I("activation", out=z.ap[:, ci, 0:n], in_=ps.ap[:, 0:n], func=AF.Copy),
                     reads=[ps.b], writes=[z.b])
                P.op("dve", I("tensor_tensor", out=s3.ap[:, ci, 0:n], in0=ps.ap[:, 0:n], in1=z.ap[:, ci, 0:n], op=ALU.mult),
                     reads=[ps.b, z.b], writes=[s3.b])
            p_q, p_kv = pss2[0], pss2[1]
            for ci in range(2):
                P.op("pe", I("matmul", p_q.ap[:, 0:n], lhsT=ones.ap, rhs=s3.ap[:, ci, 0:n], start=(ci == 0), stop=(ci == 1)),
                     reads=[ones.b, s3.b], writes=[p_q.b])
            P.op("pe", I("matmul", p_kv.ap[:, 0:n], lhsT=ones.ap, rhs=s3.ap[:, 2, 0:n], start=True, stop=True),
                 reads=[ones.b, s3.b], writes=[p_kv.b])
            P.op("act", I("activation", out=r.ap[:, 0, 0:n], in_=p_q.ap[:, 0:n], func=AF.Ln, scale=1.0 / 256, bias=EPS),
                 reads=[p_q.b], writes=[r.b])
            P.op("act", I("activation", out=r.ap[:, 1, 0:n], in_=p_kv.ap[:, 0:n], func=AF.Ln, scale=1.0 / 128, bias=EPS),
                 reads=[p_kv.b], writes=[r.b])
            P.op("act", I("activation", out=r.ap[:, :, 0:n], in_=r.ap[:, :, 0:n], func=AF.Exp, scale=-0.5), reads=[r.b], writes=[r.b])
            for ci in range(3):
                gcol = qg.ap[:, ci:ci + 1] if ci < 2 else kvg.ap[:, 0:1]
                gb_ = qg.b if ci < 2 else kvg.b
                P.op("dve", I("scalar_tensor_tensor", out=o.ap[:, ci, 0:n], in0=z.ap[:, ci, 0:n], scalar=gcol, in1=r.ap[:, 0 if ci < 2 else 1, 0:n],
                    op0=ALU.mult, op1=ALU.mult), reads=[z.b, r.b, gb_], writes=[o.b])
            P.dma("sp", qn_d.ap[0, :, c0:c0 + n], o.ap[:, 0, 0:n], reads=[o.b], writes=[qn_d.b])
            P.dma("sp", qn_d.ap[1, :, c0:c0 + n], o.ap[:, 1, 0:n], reads=[o.b], writes=[qn_d.b])
            P.dma("sp", kvn_d.ap[:, c0:c0 + n], o.ap[:, 2, 0:n], reads=[o.b], writes=[kvn_d.b])
            rope = ropes[bi % 2]
            P.dma("sp", rope.ap[0:32, :, 0:n], d["k_rope32"][:, :, c0:c0 + n].rearrange("a p n -> p a n"), writes=[rope.b])
            p_pe, p_rot = ppe[0], ppe[1]
            kq, ko = kp[bi % 2], kpo[bi % 2]
            for kk in range(8):
                P.op("pe", I("matmul", p_pe.ap[0:32, 0:n], lhsT=W.ap[:, kk, 2464:2496], rhs=uT.ap[:, kk, c0:c0 + n],
                                                                 start=(kk == 0), stop=(kk == 7)), reads=[W.b, uT.b], writes=[p_pe.b])
            for kk in range(8):
                P.op("pe", I("matmul", p_rot.ap[0:32, 0:n], lhsT=Wrot.ap[:, kk, :], rhs=uT.ap[:, kk, c0:c0 + n],
                                                                 start=(kk == 0), stop=(kk == 7)), reads=[Wrot.b, uT.b], writes=[p_rot.b])
            P.op("dve", I("tensor_tensor", out=kq.ap[0:32, 0, 0:n], in0=p_pe.ap[0:32, 0:n], in1=rope.ap[0:32, 0, 0:n], op=ALU.mult),
                 reads=[p_pe.b, rope.b], writes=[kq.b])
            P.op("dve", I("tensor_tensor", out=kq.ap[0:32, 1, 0:n], in0=p_rot.ap[0:32, 0:n], in1=rope.ap[0:32, 1, 0:n], op=ALU.mult),
                 reads=[p_rot.b, rope.b], writes=[kq.b])
            P.op("pool", I("tensor_tensor", out=ko.ap[0:32, 0:n], in0=kq.ap[0:32, 0, 0:n], in1=kq.ap[0:32, 1, 0:n], op=ALU.add),
                 reads=[kq.b], writes=[ko.b])
            P.dma("sp", kpe_d.ap[:, c0:c0 + n], ko.ap[0:32, 0:n], reads=[ko.b], writes=[kpe_d.b])
        if "dbg_z" in self.dbg:
            dz = self.scratch("dbg_z", [128, 3 * 512]); ds = self.scratch("dbg_s", [128, 3 * 512]); dr = self.scratch("dbg_r", [128, 2 * 512])
            P.dma("sp", dz.ap, zq[1].ap.rearrange("p a b -> p (a b)"), reads=[zq[1].b], writes=[dz.b])
            P.dma("sp", ds.ap, sq3[1].ap.rearrange("p a b -> p (a b)"), reads=[sq3[1].b], writes=[ds.b])
            P.dma("sp", dr.ap, rs[1].ap.rearrange("p a b -> p (a b)"), reads=[rs[1].b], writes=[dr.b])

    def s_gdn(self):
        k, P, d, T, NT = self.k, self.P, self.din, self.T, self.NT
        sc = self.dsc
        ident = k.sb(128, F32, "ident"); ones = k.sb(128, F32, "ones")
        P.dma("sp", ident.ap, d["k_identf"], writes=[ident.b])
        identb = k.sb(128, BF16, "identb")
        P.dma("sp", identb.ap, d["k_identb"], writes=[identb.b])
        P.op("pool", I("memset", ones.ap, 1.0), writes=[ones.b])
        TRI = [k.sb(128, F32, "tri%d" % i) for i in range(2)]
        MS = [k.sb(128, F32, "ms%d" % i) for i in range(2)]
        MI = [k.sb(128, F32, "mi%d" % i) for i in range(2)]
        for dr in range(2):
            P.dma("sp", TRI[dr].ap, d["k_tril"][dr], writes=[TRI[dr].b])
            P.dma("sp", MS[dr].ap, d["k_mstrict"][dr], writes=[MS[dr].b])
            P.op("dve", I("tensor_tensor", out=MI[dr].ap, in0=MS[dr].ap, in1=ident.ap, op=ALU.add), reads=[MS[dr].b, ident.b], writes=[MI[dr].b])
        GM = [k.sb(128, F32, "gm%d" % i) for i in range(7)]
        for i in range(7):
            P.dma("sp", GM[i].ap, d["k_gmask"][i], writes=[GM[i].b])
        GMb = [k.sb(128, BF16, "gmb%d" % i) for i in range(7)]
        for i in range(7):
            P.op("dve", I("tensor_copy", out=GMb[i].ap, in_=GM[i].ap), reads=[GM[i].b], writes=[GMb[i].b])
        pairs = [Tl(k.psum[:, i * 1024:(i + 1) * 1024]) for i in range(3)]
        singles = [Tl(k.psum[:, 3072 + i * 512:3072 + (i + 1) * 512]) for i in range(2)]
        rr = {"p": 0, "s": 0}

        def pair():
            rr["p"] += 1
            return pairs[rr["p"] % 3]

        def single():
            rr["s"] += 1
            return singles[rr["s"] % 2]

        def v3(t, a=8):
            return t.ap.rearrange("p (a b) -> p a b", a=a)

        def v3b(t):
            return t.ap.bitcast(BF16)[:, 0:1024].rearrange("p (a b) -> p a b", a=8)

        L = {}
        for dr in range(2):
            for par in range(2):
                L[dr, par] = dict(
                    ktok=k.sb([8, 64], F32), vtok=k.sb([8, 64], F32),
                    gb=k.sb(32, F32), QKmT=k.sb([8, 128], BF16), Us=k.sb([8, 64], F32), WTs=k.sb([8, 128], BF16),
                    Kd=k.sb([8, 64], BF16), egc=k.sb(8, F32), egl=k.sb(8, F32), qTb=k.sb([8, 128], BF16), kTb=k.sb([8, 128], BF16))
        SCR = []
        for dr in range(2):
            SCR.append(dict(
                gct=k.sb(16, F32), kdsc=k.sb(8, F32), begc=k.sb(8, F32),
                G2=k.sb([8, 128], F32), tD=k.sb([8, 128], F32), Dms=k.sb([8, 128], F32), Dmi=k.sb([8, 128], F32),
                tmpM=k.sb([8, 128], F32), QKm=k.sb([8, 128], BF16),
                Nb=[k.sb([8, 128], BF16) for _ in range(2)], NTb=[k.sb([8, 128], BF16) for _ in range(2)],
                R=k.sb([8, 128], BF16), Kbg=k.sb([8, 64], BF16), Vb=k.sb([8, 64], BF16),
                X=k.sb([8, 128], BF16), Mf=k.sb([8, 128], BF16), MTf=k.sb([8, 128], BF16)))
        S = [k.sb([8, 64], F32) for _ in range(2)]; S2 = [k.sb([8, 64], F32) for _ in range(2)]
        vnew = [k.sb([8, 64], BF16) for _ in range(2)]; tq = [k.sb([8, 64], F32) for _ in range(2)]
        Sb = [k.sb([8, 64], BF16) for _ in range(2)]
        ot = [[k.sb([8, 64], F32) for _ in range(2)] for _ in range(2)]
        for dr in range(2):
            P.op("pool", I("memset", S[dr].ap, 0.0), writes=[S[dr].b])
            P.op("pool", I("memset", Sb[dr].ap, 0.0), writes=[Sb[dr].b])
        qk_d, ktok_d, vtok_d, gb_d, o_d = sc["qkT_d"], sc["ktok_d"], sc["vtok_d"], sc["gb_d"], sc["o_d"]
        qv = qk_d.ap[0:8].rearrange("h p t -> p h t")
        kv = qk_d.ap[8:16].rearrange("h p t -> p h t")
        order = [list(range(NT)), [1, 0] + list(range(NT - 1, 1, -1))]

        def load(dr, par, n):
            b = L[dr, par]
            cs = slice(n * 128, (n + 1) * 128)
            P.dma("pool", b["qTb"].ap[0:64], qv[:, :, cs], reads=[qk_d.b], writes=[b["qTb"].b])
            P.dma("pool", b["kTb"].ap[0:64], kv[:, :, cs], reads=[qk_d.b], writes=[b["kTb"].b])
            P.dma("sp", b["ktok"].ap, ktok_d.ap[cs, :].rearrange("p (a b) -> p a b", a=8), reads=[ktok_d.b], writes=[b["ktok"].b])
            P.dma("sp", b["vtok"].ap, vtok_d.ap[cs, :].rearrange("p (a b) -> p a b", a=8), reads=[vtok_d.b], writes=[b["vtok"].b])
            P.dma("sp", b["gb"].ap, gb_d.ap[cs, :], reads=[gb_d.b], writes=[b["gb"].b])

        import os
        GSUB = os.environ.get("GSUB", "")

        def prep(dr, par):
            b = L[dr, par]
            sc_ = SCR[dr]
            gct, kdsc, begc, G2, tD, Dms, Dmi, tmpM, QKm = (sc_[n_] for n_ in ("gct", "kdsc", "begc", "G2", "tD", "Dms", "Dmi", "tmpM", "QKm"))
            Nb, NTb, R, Kbg, Vb, X, Mf, MTf = (sc_[n_] for n_ in ("Nb", "NTb", "R", "Kbg", "Vb", "X", "Mf", "MTf"))
            g8 = b["gb"].ap[:, dr * 8:dr * 8 + 8]
            be8 = b["gb"].ap[:, 16 + dr * 8:16 + dr * 8 + 8]
            gbb = b["gb"].b
            p1 = single()
            P.op("pe", I("matmul", p1.ap[:, 0:8], lhsT=TRI[dr].ap, rhs=g8, start=True, stop=True), reads=[TRI[dr].b, gbb], writes=[p1.b])
            P.op("pe", I("matmul", p1.ap[:, 8:16], lhsT=ones.ap, rhs=g8, start=True, stop=True), reads=[ones.b, gbb], writes=[p1.b])
            P.op("dve", I("tensor_copy", out=gct.ap, in_=p1.ap[:, 0:16]), reads=[p1.b], writes=[gct.b])
            P.op("act", I("activation", out=b["egc"].ap, in_=gct.ap[:, 0:8], func=AF.Exp), reads=[gct.b], writes=[b["egc"].b])
            P.op("act", I("activation", out=b["egl"].ap, in_=gct.ap[:, 8:16], func=AF.Exp), reads=[gct.b], writes=[b["egl"].b])
            P.op("dve", I("tensor_tensor", out=kdsc.ap, in0=gct.ap[:, 8:16], in1=gct.ap[:, 0:8], op=ALU.subtract), reads=[gct.b], writes=[kdsc.b])
            P.op("act", I("activation", out=kdsc.ap, in_=kdsc.ap, func=AF.Exp), reads=[kdsc.b], writes=[kdsc.b])
            P.op("dve", I("tensor_tensor", out=begc.ap, in0=be8, in1=b["egc"].ap, op=ALU.mult), reads=[gbb, b["egc"].b], writes=[begc.b])
            if GSUB == "1":
                return
            yield
            P.op("dve", I("tensor_tensor", out=G2.ap, in0=bcast_head(TRI[dr].ap, 8), in1=bcast_mid(g8, 128), op=ALU.mult),
                 reads=[TRI[dr].b, gbb], writes=[G2.b])
            pg = pair()
            for h in range(8):
                P.op("pe", I("matmul", pg.ap[:, h * 128:(h + 1) * 128], lhsT=ones.ap, rhs=G2.ap[:, h, :], start=True, stop=True),
                     reads=[ones.b, G2.b], writes=[pg.b])
            if GSUB == "2":
                return
            yield
            P.op("dve", I("tensor_tensor", out=tD.ap, in0=v3(pg), in1=bcast_mid(gct.ap[:, 0:8], 128), op=ALU.subtract),
                 reads=[pg.b, gct.b], writes=[tD.b])
            P.op("act", I("activation", out=tD.ap, in_=tD.ap, func=AF.Relu), reads=[tD.b], writes=[tD.b])
            P.op("act", I("activation", out=tD.ap, in_=tD.ap, func=AF.Exp, scale=-1.0), reads=[tD.b], writes=[tD.b])
            P.op("pool", I("tensor_tensor", out=Dms.ap, in0=tD.ap, in1=bcast_head(MS[dr].ap, 8), op=ALU.mult), reads=[tD.b, MS[dr].b], writes=[Dms.b])
            P.op("pool", I("tensor_tensor", out=Dmi.ap, in0=tD.ap, in1=bcast_head(MI[dr].ap, 8), op=ALU.mult), reads=[tD.b, MI[dr].b], writes=[Dmi.b])
            if GSUB == "3":
                return
            yield
            pG = pair()
            for h in range(8):
                P.op("pe", I("matmul", pG.ap[:, h * 128:(h + 1) * 128], lhsT=b["kTb"].ap[0:64, h, :], rhs=b["kTb"].ap[0:64, h, :], start=True, stop=True),
                     reads=[b["kTb"].b], writes=[pG.b])
            pA = pair()
            for h in range(8):
                P.op("pe", I("matmul", pA.ap[:, h * 128:(h + 1) * 128], lhsT=b["qTb"].ap[0:64, h, :], rhs=b["kTb"].ap[0:64, h, :], start=True, stop=True),
                     reads=[b["kTb"].b, b["qTb"].b], writes=[pA.b])
            P.op("dve", I("tensor_tensor", out=tmpM.ap, in0=v3(pG), in1=bcast_mid(be8, 128), op=ALU.mult), reads=[pG.b, gbb], writes=[tmpM.b])
            P.op("pool", I("tensor_tensor", out=Mf.ap, in0=tmpM.ap, in1=Dms.ap, op=ALU.mult), reads=[tmpM.b, Dms.b], writes=[Mf.b])
            P.op("dve", I("tensor_tensor", out=QKm.ap, in0=v3(pA), in1=Dmi.ap, op=ALU.mult), reads=[pA.b, Dmi.b], writes=[QKm.b])
            if GSUB == "4":
                return
            yield
            pT = pair()
            for h in range(8):
                P.op("pe", I("transpose", out=pT.ap.bitcast(BF16)[:, h * 128:(h + 1) * 128], in_=Mf.ap[:, h, :], identity=identb.ap), reads=[Mf.b, identb.b], writes=[pT.b])
            P.op("act", I("activation", out=MTf.ap, in_=v3b(pT), func=AF.Copy), reads=[pT.b], writes=[MTf.b])
            pQ = pair()
            for h in range(8):
                P.op("pe", I("transpose", out=pQ.ap.bitcast(BF16)[:, h * 128:(h + 1) * 128], in_=QKm.ap[:, h, :], identity=identb.ap), reads=[QKm.b, identb.b], writes=[pQ.b])
            P.op("act", I("activation", out=b["QKmT"].ap, in_=v3b(pQ), func=AF.Copy), reads=[pQ.b], writes=[b["QKmT"].b])
            if GSUB == "5":
                return
            yield
            def mm8(lhs, rhs):
                pp = pair()
                for h in range(8):
                    P.op("pe", I("matmul", pp.ap[:, h * 128:(h + 1) * 128], lhsT=lhs.ap[:, h, :], rhs=rhs.ap[:, h, :], start=True, stop=True),
                         reads=[lhs.b, rhs.b], writes=[pp.b])
                return pp

            P.op("dve", I("tensor_tensor", out=Nb[0].ap, in0=Mf.ap, in1=bcast_head(GMb[0].ap, 8), op=ALU.mult), reads=[Mf.b, GMb[0].b], writes=[Nb[0].b])
            P.op("dve", I("tensor_tensor", out=NTb[0].ap, in0=MTf.ap, in1=bcast_head(GMb[0].ap, 8), op=ALU.mult), reads=[MTf.b, GMb[0].b], writes=[NTb[0].b])
            P.op("dve", I("tensor_tensor", out=R.ap, in0=bcast_head(identb.ap, 8), in1=NTb[0].ap, op=ALU.subtract), reads=[NTb[0].b, identb.b], writes=[R.b])
            cur = 0
            for j in range(1, 4):
                Nc, NTc, Nn, NTn = Nb[cur], NTb[cur], Nb[1 - cur], NTb[1 - cur]
                px = mm8(NTc, Nc)
                P.op("act", I("activation", out=Nn.ap, in_=v3(px), func=AF.Copy), reads=[px.b], writes=[Nn.b])
                yield
                if j < 3:
                    py = mm8(Nc, NTc)
                    P.op("act", I("activation", out=NTn.ap, in_=v3(py), func=AF.Copy), reads=[py.b], writes=[NTn.b])
                    yield
                pz = mm8(Nn, R)
                P.op("dve", I("tensor_tensor", out=R.ap, in0=v3(pz), in1=R.ap, op=ALU.add), reads=[pz.b, R.b], writes=[R.b])
                cur = 1 - cur
                yield
            for li in range(3):
                C, Z1 = Nb[0], Nb[1]
                mC = GMb[1 + dr * 3 + li]
                ptx = pair()
                for h in range(8):
                    P.op("pe", I("transpose", out=ptx.ap.bitcast(BF16)[:, h * 128:(h + 1) * 128], in_=R.ap[:, h, :], identity=identb.ap),
                         reads=[R.b, identb.b], writes=[ptx.b])
                P.op("act", I("activation", out=X.ap, in_=v3b(ptx), func=AF.Copy), reads=[ptx.b], writes=[X.b])
                P.op("dve", I("tensor_tensor", out=C.ap, in0=Mf.ap, in1=bcast_head(mC.ap, 8), op=ALU.mult), reads=[Mf.b, mC.b], writes=[C.b])
                yield
                p1_ = mm8(C, R)
                P.op("act", I("activation", out=Z1.ap, in_=v3(p1_), func=AF.Copy), reads=[p1_.b], writes=[Z1.b])
                yield
                p3_ = mm8(X, Z1)
                P.op("dve", I("tensor_tensor", out=R.ap, in0=R.ap, in1=v3(p3_), op=ALU.subtract), reads=[p3_.b, R.b], writes=[R.b])
                yield
            if GSUB == "6":
                return
            yield
            P.op("pool", I("tensor_tensor", out=Vb.ap, in0=b["vtok"].ap, in1=bcast_mid(be8, 64), op=ALU.mult), reads=[b["vtok"].b, gbb], writes=[Vb.b])
            P.op("pool", I("tensor_tensor", out=Kbg.ap, in0=b["ktok"].ap, in1=bcast_mid(begc.ap, 64), op=ALU.mult), reads=[b["ktok"].b, begc.b], writes=[Kbg.b])
            P.op("pool", I("tensor_tensor", out=b["Kd"].ap, in0=b["ktok"].ap, in1=bcast_mid(kdsc.ap, 64), op=ALU.mult), reads=[b["ktok"].b, kdsc.b], writes=[b["Kd"].b])
            pU = single()
            for h in range(8):
                P.op("pe", I("matmul", pU.ap[:, h * 64:(h + 1) * 64], lhsT=R.ap[:, h, :], rhs=Vb.ap[:, h, :], start=True, stop=True),
                     reads=[R.b, Vb.b], writes=[pU.b])
            P.op("act", I("activation", out=b["Us"].ap, in_=v3(pU), func=AF.Copy), reads=[pU.b], writes=[b["Us"].b])
            pW = pair()
            for h in range(8):
                P.op("pe", I("matmul", pW.ap[0:64, h * 128:(h + 1) * 128], lhsT=Kbg.ap[:, h, :], rhs=R.ap[:, h, :], start=True, stop=True),
                     reads=[R.b, Kbg.b], writes=[pW.b])
            P.op("act", I("activation", out=b["WTs"].ap[0:64], in_=pW.ap[0:64, :].rearrange("p (a b) -> p a b", a=8), func=AF.Copy),
                 reads=[pW.b], writes=[b["WTs"].b])

        def step(dr, par, n, i):
            b = L[dr, par]
            Sd, S2d, vn, tqd, o = S[dr], S2[dr], vnew[dr], tq[dr], ot[dr][i % 2]
            pa, pq = single(), pair()
            for h in range(8):
                P.op("pe", I("matmul", pa.ap[:, h * 64:(h + 1) * 64], lhsT=b["WTs"].ap[0:64, h, :], rhs=Sb[dr].ap[0:64, h, :], start=True, stop=True),
                     reads=[b["WTs"].b, Sb[dr].b], writes=[pa.b])
            for h in range(8):
                P.op("pe", I("matmul", pq.ap[:, h * 64:(h + 1) * 64], lhsT=b["qTb"].ap[0:64, h, :], rhs=Sb[dr].ap[0:64, h, :], start=True, stop=True),
                     reads=[b["qTb"].b, Sb[dr].b], writes=[pq.b])
            P.op("dve", I("tensor_tensor", out=vn.ap, in0=b["Us"].ap, in1=pa.ap.rearrange("p (a b) -> p a b", a=8), op=ALU.subtract),
                 reads=[b["Us"].b, pa.b], writes=[vn.b])
            P.op("dve", I("tensor_tensor", out=tqd.ap, in0=pq.ap[:, 0:512].rearrange("p (a b) -> p a b", a=8), in1=bcast_mid(b["egc"].ap, 64), op=ALU.mult),
                 reads=[pq.b, b["egc"].b], writes=[tqd.b])
            yield
            po, ps_ = single(), pair()
            for h in range(8):
                P.op("pe", I("matmul", po.ap[:, h * 64:(h + 1) * 64], lhsT=b["QKmT"].ap[:, h, :], rhs=vn.ap[:, h, :], start=True, stop=True),
                     reads=[b["QKmT"].b, vn.b], writes=[po.b])
            for h in range(8):
                P.op("pe", I("matmul", ps_.ap[0:64, h * 64:(h + 1) * 64], lhsT=b["Kd"].ap[:, h, :], rhs=vn.ap[:, h, :], start=True, stop=True),
                     reads=[b["Kd"].b, vn.b], writes=[ps_.b])
            P.op("dve", I("tensor_tensor", out=o.ap, in0=tqd.ap, in1=po.ap.rearrange("p (a b) -> p a b", a=8), op=ALU.add),
                 reads=[tqd.b, po.b], writes=[o.b])
            P.dma("sp", o_d.ap[dr, n * 128:(n + 1) * 128, :], o.ap.rearrange("p a b -> p (a b)"), reads=[o.b], writes=[o_d.b])
            P.op("pool", I("tensor_tensor", out=S2d.ap[0:64], in0=Sd.ap[0:64], in1=bcast_mid(b["egl"].ap[0:64], 64), op=ALU.mult),
                 reads=[Sd.b, b["egl"].b], writes=[S2d.b])
            P.op("dve", I("tensor_tensor", out=Sd.ap[0:64], in0=S2d.ap[0:64], in1=ps_.ap[0:64, 0:512].rearrange("p (a b) -> p a b", a=8), op=ALU.add),
                 reads=[S2d.b, ps_.b], writes=[Sd.b])
            P.op("act", I("activation", out=Sb[dr].ap[0:64], in_=Sd.ap[0:64], func=AF.Copy), reads=[Sd.b], writes=[Sb[dr].b])

        for dr in range(2):
            load(dr, 0, order[dr][0])
        pending = []
        for i in range(NT + 1):
            gens = list(pending)
            pending = []
            if i < NT:
                for dr in range(2):
                    gens.append(prep(dr, i % 2))
            while gens:
                for g_ in list(gens):
                    try:
                        next(g_)
                    except StopIteration:
                        gens.remove(g_)
            if i < NT:
                if i + 1 < NT:
                    for dr in range(2):
                        load(dr, (i + 1) % 2, order[dr][i + 1])
                if not GSUB:
                    for dr in range(2):
                        pending.append(step(dr, i % 2, order[dr][i], i))

    def s_gdn_out(self):
        k, P, d, T, NT = self.k, self.P, self.din, self.T, self.NT
        sc = self.dsc
        self.ffn_weights_issue(0, self.ffn_weights(0))
        identb = k.sb(128, BF16, "identb")
        P.dma("sp", identb.ap, d["k_identb"], writes=[identb.b])
        gn = k.sb(64, F32, "gn")
        P.dma("sp", gn.ap, d["gdn_out_norm"][0].partition_broadcast(128), writes=[gn.b])
        of = [k.sb([8, 64], F32) for _ in range(2)]; ob = [k.sb([8, 64], F32) for _ in range(2)]
        gt = [k.sb([8, 64], F32) for _ in range(2)]
        o = [k.sb([8, 64], F32) for _ in range(2)]; sq = k.sb([8, 64], F32)
        st = [k.sb(16, F32) for _ in range(2)]
        ab = [k.sb(512, BF16) for _ in range(2)]
        aT = [k.sb([4, 128], BF16) for _ in range(2)]
        pt = [k.ps(0), k.ps(1)]
        o_d, gate_d, mix = sc["o_d"], sc["gate_d"], sc["mixT_d"]
        mixv = mix.ap[0:4].rearrange("k p t -> p k t")
        for n in range(NT):
            i = n % 2
            cs = slice(n * 128, (n + 1) * 128)
            P.dma("sp", of[i].ap, o_d.ap[0, cs, :].rearrange("p (a b) -> p a b", a=8), reads=[o_d.b], writes=[of[i].b])
            P.dma("sp", ob[i].ap, o_d.ap[1, cs, :].rearrange("p (a b) -> p a b", a=8), reads=[o_d.b], writes=[ob[i].b])
            P.dma("sp", gt[i].ap, gate_d.ap[cs, :].rearrange("p (a b) -> p a b", a=8), reads=[gate_d.b], writes=[gt[i].b])
            P.op("dve", I("tensor_tensor", out=o[i].ap, in0=of[i].ap, in1=ob[i].ap, op=ALU.add), reads=[of[i].b, ob[i].b], writes=[o[i].b])
            P.op("dve", I("tensor_tensor", out=sq.ap, in0=o[i].ap, in1=o[i].ap, op=ALU.mult), reads=[o[i].b], writes=[sq.b])
            P.op("dve", I("tensor_reduce", out=st[i].ap[:, 0:8], in_=sq.ap, axis=AX.X, op=ALU.add), reads=[sq.b], writes=[st[i].b])
            P.op("act", I("activation", out=st[i].ap[:, 8:16], in_=st[i].ap[:, 0:8], func=AF.Sqrt, scale=1.0 / 64, bias=EPS), reads=[st[i].b], writes=[st[i].b])
            P.op("dve", I("reciprocal", out=st[i].ap[:, 8:16], in_=st[i].ap[:, 8:16]), reads=[st[i].b], writes=[st[i].b])
            P.op("dve", I("tensor_tensor", out=o[i].ap, in0=o[i].ap, in1=bcast_mid(st[i].ap[:, 8:16], 64), op=ALU.mult), reads=[o[i].b, st[i].b], writes=[o[i].b])
            P.op("dve", I("tensor_tensor", out=o[i].ap, in0=o[i].ap, in1=bcast_head(gn.ap, 8), op=ALU.mult), reads=[o[i].b, gn.b], writes=[o[i].b])
            P.op("dve", I("tensor_tensor", out=ab[i].ap.rearrange("p (a b) -> p a b", a=8), in0=o[i].ap, in1=gt[i].ap, op=ALU.mult),
                 reads=[o[i].b, gt[i].b], writes=[ab[i].b])
            psb = pt[i].ap.bitcast(BF16)
            for j in range(4):
                P.op("pe", I("transpose", out=psb[:, j * 128:(j + 1) * 128], in_=ab[i].ap[:, j * 128:(j + 1) * 128], identity=identb.ap),
                     reads=[ab[i].b, identb.b], writes=[pt[i].b])
            P.op("act", I("activation", out=aT[i].ap, in_=psb[:, 0:512].rearrange("p (a b) -> p a b", a=4), func=AF.Copy), reads=[pt[i].b], writes=[aT[i].b])
            P.dma("act", mixv[:, :, cs], aT[i].ap, reads=[aT[i].b], writes=[mix.b])

    def post_epilogue(self, ps, h, Gt, st, tmp, hn):
        P = self.P
        P.op("act", I("activation", out=tmp.ap, in_=ps.ap, func=AF.Square, accum_out=st.ap[:, 0:1]), reads=[ps.b], writes=[tmp.b, st.b])
        P.op("act", I("activation", out=st.ap[:, 1:2], in_=st.ap[:, 0:1], func=AF.Sqrt, scale=1.0 / D, bias=EPS), reads=[st.b], writes=[st.b])
        P.op("dve", I("reciprocal", out=st.ap[:, 2:3], in_=st.ap[:, 1:2]), reads=[st.b], writes=[st.b])
        P.op("dve", I("scalar_tensor_tensor", out=tmp.ap, in0=ps.ap, scalar=st.ap[:, 2:3], in1=Gt.ap, op0=ALU.mult, op1=ALU.mult),
             reads=[ps.b, st.b, Gt.b], writes=[tmp.b])
        P.op("pool", I("tensor_tensor", out=hn.ap, in0=tmp.ap, in1=h.ap, op=ALU.add), reads=[tmp.b, h.b], writes=[hn.b])

    def s_outproj(self, l, after_loads=None):
        k, P, d, T, NT = self.k, self.P, self.din, self.T, self.NT
        sc = self.dsc
        Gc, Gl, _ = self.load_gate_tiles(l, d["g_post_mix"], 2 * D)
        W = k.sb([8, D], BF16, "Wout")
        wsrc = (d["w_out_even"] if l == 0 else d["w_out_odd"])[0].rearrange("(k p) n -> p k n", p=128)
        for kk in range(8):
            P.dma("pool", W.ap[:, kk, :], wsrc[:, kk, :], writes=[W.b])
        if after_loads is not None:
            after_loads()
        mix = sc["mixT_d"]
        mixv = mix.ap.rearrange("k p t -> p k t")
        mt = [k.sb([8, 128], BF16) for _ in range(2)]
        h = [k.sb(D, F32) for _ in range(2)]; hn = [k.sb(D, F32) for _ in range(2)]
        tmp = k.sb(D, F32); st = [k.sb(8, F32) for _ in range(2)]
        pss = [Tl(k.psum[:, 0:1024]), Tl(k.psum[:, 1024:2048])]
        tiles = range(NT) if l == 0 else range(2, NT)
        hb = sc["hbuf_d"]
        for it, n in enumerate(tiles):
            i = it % 2
            cs = slice(n * 128, (n + 1) * 128)
            P.dma("sp", mt[i].ap, mixv[:, :, cs], reads=[mix.b], writes=[mt[i].b])
            src_ap, src_b = self.h_src(l, n)
            if l == 1:
                src_b = self.hb[n]
            P.dma("sp", h[i].ap, src_ap, reads=[src_b] if src_b else [], writes=[h[i].b])
            ps = pss[i]
            for half in range(2):
                for kk in range(8):
                    P.op("pe", I("matmul", ps.ap[:, half * 512:(half + 1) * 512], lhsT=mt[i].ap[:, kk, :], rhs=W.ap[:, kk, half * 512:(half + 1) * 512],
                                 start=(kk == 0), stop=(kk == 7)), reads=[mt[i].b, W.b], writes=[ps.b])
            self.post_epilogue(ps, h[i], Gc if n < 2 else Gl, st[i], tmp, hn[i])
            P.dma("pool", hb.ap[cs, :], hn[i].ap, reads=[hn[i].b], writes=[self.hb[n]])

    def ffn_weights(self, l):
        k = self.k
        return k.sb([8, DFF], BF16, "Wg"), k.sb([8, DFF], BF16, "Wu"), k.sb([22, D], BF16, "Wd")

    def ffn_weights_issue(self, l, Wts):
        P, d = self.P, self.din
        Wg, Wu, Wd = Wts
        wg = d["w_ffn_gate"][l].rearrange("(k p) n -> p k n", p=128)
        wu = d["w_ffn_up"][l].rearrange("(k p) n -> p k n", p=128)
        wd = d["w_ffn_down"][l].rearrange("(k p) n -> p k n", p=128)
        for kk in range(8):
            P.dma("pool", Wg.ap[:, kk, :], wg[:, kk, :], writes=[Wg.b])
            P.dma("pool", Wu.ap[:, kk, :], wu[:, kk, :], writes=[Wu.b])
        for kk in range(22):
            P.dma("pool", Wd.ap[:, kk, :], wd[:, kk, :], writes=[Wd.b])

    def s_outffn(self, l):
        k, P = self.k, self.P
        Wts = self.ffn_weights(l)
        mark = k.off
        self.s_outproj(l)
        P.barrier()
        k.reset(mark)
        self.s_ffn(l, Wts)

    def s_ffn(self, l, Wts):
        k, P, d, T, NT = self.k, self.P, self.din, self.T, self.NT
        sc = self.dsc
        Wg, Wu, Wd = Wts
        modv = sc["modv_d"]
        A = k.sb(D, F32, "fA"); B = k.sb(D, F32, "fB"); Gt = k.sb(D, F32, "fGt")
        t1 = k.sb(D, F32, "t1"); tmp = k.sb(D, F32, "tmp")

        def load_AB(r):
            P.dma("sp", t1.ap, d["g_pre_ffn"][l].partition_broadcast(128), writes=[t1.b])
            P.dma("sp", A.ap, modv.ap[l, r, 4 * D:5 * D].partition_broadcast(128), reads=[modv.b], writes=[A.b])
            P.dma("sp", B.ap, modv.ap[l, r, 3 * D:4 * D].partition_broadcast(128), reads=[modv.b], writes=[B.b])
            P.op("dve", I("scalar_tensor_tensor", out=A.ap, in0=A.ap, scalar=1.0, in1=t1.ap, op0=ALU.add, op1=ALU.mult), reads=[A.b, t1.b], writes=[A.b])

        def load_Gt(r):
            P.dma("sp", tmp.ap, d["g_post_ffn"][l].partition_broadcast(128), writes=[tmp.b])
            P.dma("sp", Gt.ap, modv.ap[l, r, 5 * D:6 * D].partition_broadcast(128), reads=[modv.b], writes=[Gt.b])
            P.op("dve", I("tensor_tensor", out=Gt.ap, in0=Gt.ap, in1=tmp.ap, op=ALU.mult), reads=[Gt.b, tmp.b], writes=[Gt.b])

        NB = 2
        identb = k.sb(128, BF16, "identb")
        P.dma("sp", identb.ap, d["k_identb"], writes=[identb.b])
        hblk = [[k.sb(D, F32) for _ in range(NB)] for _ in range(2)]
        junk = [k.sb(D, BF16) for _ in range(2)]
        st = [k.sb(8, F32) for _ in range(4)]
        u = [[k.sb(D, BF16) for _ in range(NB)] for _ in range(2)]
        uTblk = k.sb([8, NB * 128], BF16)
        a = k.sb([22, NB * 128], BF16)
        sg = [k.sb(NB * 128, F32) for _ in range(2)]
        ptr = [k.ps(6), k.ps(7)]
        pgs = [k.ps(4), k.ps(5)]
        pus = [k.ps(2), k.ps(3)]
        py = Tl(k.psum[:, 0:1024])
        hbd = sc["hbuf_d"]
        tiles = list(range(NT)) if l == 0 else list(range(2, NT))
        blks = [tiles[b0:b0 + NB] for b0 in range(0, len(tiles), NB)]
        state = {"ab": None, "gt": None, "it": 0}

        def norm_front(bix):
            blk = blks[bix]
            want = 1 if blk[0] < 2 else 0
            if want != state["ab"]:
                load_AB(want)
                state["ab"] = want
            for j, n in enumerate(blk):
                it = state["it"]
                state["it"] += 1
                h, s_, uu, jk = hblk[bix % 2][j], st[it % 2], u[bix % 2][j], junk[it % 2]
                P.dma("sp", h.ap, hbd.ap[n * 128:(n + 1) * 128, :], reads=[self.hb[n]], writes=[h.b])
                P.op("act", I("activation", out=jk.ap, in_=h.ap, func=AF.Square, accum_out=s_.ap[:, 0:1]), reads=[h.b], writes=[jk.b, s_.b])
                P.op("act", I("activation", out=s_.ap[:, 1:2], in_=s_.ap[:, 0:1], func=AF.Sqrt, scale=1.0 / D, bias=EPS), reads=[s_.b], writes=[s_.b])
                P.op("dve", I("reciprocal", out=s_.ap[:, 2:3], in_=s_.ap[:, 1:2]), reads=[s_.b], writes=[s_.b])
                P.op("dve", I("scalar_tensor_tensor", out=t1.ap, in0=h.ap, scalar=s_.ap[:, 2:3], in1=A.ap, op0=ALU.mult, op1=ALU.mult),
                     reads=[h.b, s_.b, A.b], writes=[t1.b])
                P.op("pool", I("tensor_tensor", out=uu.ap, in0=t1.ap, in1=B.ap, op=ALU.add), reads=[t1.b, B.b], writes=[uu.b])

        def transposes(bix):
            for j, n in enumerate(blks[bix]):
                uu, ps = u[bix % 2][j], ptr[j % 2]
                psb = ps.ap.bitcast(BF16)
                for kk in range(8):
                    P.op("pe", I("transpose", out=psb[:, kk * 128:(kk + 1) * 128], in_=uu.ap[:, kk * 128:(kk + 1) * 128], identity=identb.ap),
                         reads=[uu.b, identb.b], writes=[ps.b])
                P.op("act", I("activation", out=uTblk.ap[:, :, j * 128:(j + 1) * 128], in_=psb.rearrange("p (a b) -> p a b", a=8), func=AF.Copy), reads=[ps.b], writes=[uTblk.b])

        def gate_up(bix):
            nb = len(blks[bix])
            for ffc in range(22):
                pg, pu, sgt = pgs[ffc % 2], pus[ffc % 2], sg[ffc % 2]
                fs = slice(ffc * 128, (ffc + 1) * 128)
                for kk in range(8):
                    P.op("pe", I("matmul", pg.ap[:, 0:nb * 128], lhsT=Wg.ap[:, kk, fs], rhs=uTblk.ap[:, kk, 0:nb * 128], start=(kk == 0), stop=(kk == 7)),
                         reads=[Wg.b, uTblk.b], writes=[pg.b])
                for kk in range(8):
                    P.op("pe", I("matmul", pu.ap[:, 0:nb * 128], lhsT=Wu.ap[:, kk, fs], rhs=uTblk.ap[:, kk, 0:nb * 128], start=(kk == 0), stop=(kk == 7)),
                         reads=[Wu.b, uTblk.b], writes=[pu.b])
                P.op("act", I("activation", out=sgt.ap[:, 0:nb * 128], in_=pg.ap[:, 0:nb * 128], func=AF.Silu), reads=[pg.b], writes=[sgt.b])
                P.op("dve", I("tensor_tensor", out=a.ap[:, ffc, 0:nb * 128], in0=sgt.ap[:, 0:nb * 128], in1=pu.ap[:, 0:nb * 128], op=ALU.mult),
                     reads=[sgt.b, pu.b], writes=[a.b])

        def down_epi(bix):
            blk = blks[bix]
            want = 1 if blk[0] < 2 else 0
            if want != state["gt"]:
                load_Gt(want)
                state["gt"] = want
            for j, n in enumerate(blk):
                h = hblk[bix % 2][j]
                for half in range(2):
                    for ffc in range(22):
                        P.op("pe", I("matmul", py.ap[:, half * 512:(half + 1) * 512], lhsT=a.ap[:, ffc, j * 128:(j + 1) * 128], rhs=Wd.ap[:, ffc, half * 512:(half + 1) * 512],
                                     start=(ffc == 0), stop=(ffc == 21)), reads=[a.b, Wd.b], writes=[py.b])
                s2 = st[2 + j % 2]
                jk = junk[j % 2]
                P.op("act", I("activation", out=jk.ap, in_=py.ap, func=AF.Square, accum_out=s2.ap[:, 0:1]), reads=[py.b], writes=[jk.b, s2.b])
                P.op("act", I("activation", out=s2.ap[:, 1:2], in_=s2.ap[:, 0:1], func=AF.Sqrt, scale=1.0 / D, bias=EPS), reads=[s2.b], writes=[s2.b])
                P.op("dve", I("reciprocal", out=s2.ap[:, 2:3], in_=s2.ap[:, 1:2]), reads=[s2.b], writes=[s2.b])
                P.op("dve", I("scalar_tensor_tensor", out=tmp.ap, in0=py.ap, scalar=s2.ap[:, 2:3], in1=Gt.ap, op0=ALU.mult, op1=ALU.mult),
                     reads=[py.b, s2.b, Gt.b], writes=[tmp.b])
                P.op("pool", I("tensor_tensor", out=h.ap, in0=tmp.ap, in1=h.ap, op=ALU.add), reads=[tmp.b, h.b], writes=[h.b])
                if l == 0:
                    P.dma("pool", hbd.ap[n * 128:(n + 1) * 128, :], h.ap, reads=[h.b], writes=[self.hb[n]])
                else:
                    P.dma("pool", self.y.ap[(n - 2) * 128:(n - 1) * 128, :], h.ap, reads=[h.b], writes=[self.y.b])

        norm_front(0)
        for bix in range(len(blks)):
            transposes(bix)
            gate_up(bix)
            if bix + 1 < len(blks):
                norm_front(bix + 1)
            down_epi(bix)

    def attn_setup(self):
        k, P = self.k, self.P
        A = {}
        A["pt"] = [k.sb(512, BF16) for _ in range(3)]
        A["osb"] = [k.sb(512, F32) for _ in range(2)]
        A["om"] = [k.sb(512, BF16) for _ in range(2)]
        A["ones"] = k.sb(128, F32)
        P.op("pool", I("memset", A["ones"].ap, 1.0), writes=[A["ones"].b])
        A["ps_s"] = [k.ps(0), k.ps(1), k.ps(2)]
        A["ps_o"] = [k.ps(3), k.ps(4)]
        A["ps_b"] = k.ps(5)
        A["cnt"] = 0
        A["qcnt"] = 0
        return A

    def attend(self, A, KT, QT, kd, vfn, vb, ktiles, c0, nq, scale, dst_ap, dst_b):
        P = self.P
        qi = A["qcnt"]
        A["qcnt"] += 1
        po = A["ps_o"][qi % 2]
        osb, om = A["osb"][qi % 2], A["om"][qi % 2]
        nk = len(ktiles)
        slots = []

        def qk(j):
            i = A["cnt"]
            A["cnt"] += 1
            ps, pt = A["ps_s"][i % 3], A["pt"][i % 3]
            kt = ktiles[j]
            P.op("pe", I("matmul", ps.ap[:, 0:nq], lhsT=KT.ap[0:kd, kt * 128:(kt + 1) * 128], rhs=QT.ap[0:kd, c0:c0 + nq], start=True, stop=True),
                 reads=[KT.b, QT.b], writes=[ps.b])
            P.op("act", I("activation", out=pt.ap[:, 0:nq], in_=ps.ap[:, 0:nq], func=AF.Exp, scale=scale), reads=[ps.b], writes=[pt.b])
            slots.append(pt)

        LA = 2
        for j in range(min(LA, nk)):
            qk(j)
        for j in range(nk):
            if j + LA < nk:
                qk(j + LA)
            pt = slots[j]
            P.op("pe", I("matmul", po.ap[:, 0:nq], lhsT=vfn(ktiles[j]), rhs=pt.ap[:, 0:nq], start=(j == 0), stop=(j == nk - 1)),
                 reads=[vb, pt.b], writes=[po.b])
        P.op("act", I("activation", out=osb.ap[0:65, 0:nq], in_=po.ap[0:65, 0:nq], func=AF.Copy), reads=[po.b], writes=[osb.b])
        P.op("dve", I("reciprocal", out=osb.ap[64:65, 0:nq], in_=osb.ap[64:65, 0:nq]), reads=[osb.b], writes=[osb.b])
        pb = A["ps_b"]
        P.op("pe", I("matmul", pb.ap[0:64, 0:nq], lhsT=A["ones"].ap[64:65, 0:64], rhs=osb.ap[64:65, 0:nq], start=True, stop=True),
             reads=[A["ones"].b, osb.b], writes=[pb.b])
        P.op("dve", I("tensor_tensor", out=om.ap[0:64, 0:nq], in0=osb.ap[0:64, 0:nq], in1=pb.ap[0:64, 0:nq], op=ALU.mult),
             reads=[osb.b, pb.b], writes=[om.b])
        P.dma("sp", dst_ap, om.ap[0:64, 0:nq], reads=[om.b], writes=[dst_b])

    def s_mla(self):
        k, P, d, T, NT, SEQ = self.k, self.P, self.din, self.T, self.NT, self.SEQ
        sc = self.dsc
        qn = k.sb([2, T], BF16, "qn"); kvn = k.sb(T, BF16, "kvn")
        for kk in range(2):
            P.dma("sp", qn.ap[:, kk, :], sc["qn_d"].ap[kk], reads=[sc["qn_d"].b], writes=[qn.b])
        P.dma("sp", kvn.ap, sc["kvn_d"].ap, reads=[sc["kvn_d"].b], writes=[kvn.b])
        Wq = k.sb([2, 768], BF16, "Wq"); Wkv = k.sb(1024, BF16, "Wkv"); Wqr = k.sb([2, 256], BF16, "Wqr")
        wqv = d["mla_w_q_up"][0].rearrange("(k p) n -> p k n", p=128)
        for kk in range(2):
            P.dma("pool", Wq.ap[:, kk, :], wqv[:, kk, :], writes=[Wq.b])
        P.dma("pool", Wkv.ap, d["mla_w_kv_up"][0], writes=[Wkv.b])
        wq4 = Wq.ap.rearrange("p k (h c) -> p k h c", h=8)
        wr4 = Wqr.ap.rearrange("p k (h c) -> p k h c", h=8)
        P.op("act", I("activation", out=wr4[:, :, :, 0:16], in_=wq4[:, :, :, 80:96], func=AF.Copy, scale=-1.0), reads=[Wq.b], writes=[Wqr.b])
        P.op("act", I("activation", out=wr4[:, :, :, 16:32], in_=wq4[:, :, :, 64:80], func=AF.Copy), reads=[Wq.b], writes=[Wqr.b])
        rope = k.sb([2, T], F32, "rope")
        for i in range(2):
            P.dma("sp", rope.ap[64:96, i, :], d["k_rope32"][i], writes=[rope.b])
        KT = [k.sb(T, BF16, "KT%d" % i) for i in range(2)]
        QT = [k.sb(T, BF16, "QT%d" % i) for i in range(2)]
        for i in range(2):
            P.dma("sp", KT[i].ap[64:96, :], sc["kpe_d"].ap, reads=[sc["kpe_d"].b], writes=[KT[i].b])
        Vall = k.sb([NT, 8 * 128], BF16, "Vall")
        V4 = Vall.ap.rearrange("p n (h c) -> p n h c", h=8)
        P.op("pool", I("memset", Vall.ap, 1.0), writes=[Vall.b])
        A = self.attn_setup()
        pv = [k.ps(6), k.ps(7)]
        import os
        if os.environ.get("MSUB", "") == "a":
            return
        wkv3 = Wkv.ap.rearrange("p (h c) -> p h c", h=8)
        for t in range(NT):
            ps = pv[t % 2]
            P.op("pe", I("matmul", ps.ap.rearrange("p (h c) -> p h c", h=8), lhsT=kvn.ap[:, t * 128:(t + 1) * 128], rhs=wkv3[:, :, 64:128], start=True, stop=True),
                 reads=[kvn.b, Wkv.b], writes=[ps.b])
            P.op("dve", I("tensor_copy", out=V4[:, t, :, 0:64], in_=ps.ap.rearrange("p (h c) -> p h c", h=8)), reads=[ps.b], writes=[Vall.b])
        if os.environ.get("MSUB", "") == "b":
            return
        t1 = [k.sb(512, F32) for _ in range(2)]; t2 = [k.sb(512, F32) for _ in range(2)]
        pk, pqn, pqp, pqr = pv[0], pv[1], pv[0], pv[1]
        import os
        MSUB = os.environ.get("MSUB", "")
        blocks = self.tok_blocks()
        mix = sc["mixT_d"]
        scale = float(96 ** -0.5)
        for h in range(8):
            kt_, qt_ = KT[h % 2], QT[h % 2]
            for bi, (c0, n) in enumerate(blocks):
                P.op("pe", I("matmul", pk.ap[0:64, 0:n], lhsT=wkv3[:, h, 0:64], rhs=kvn.ap[:, c0:c0 + n], start=True, stop=True),
                     reads=[Wkv.b, kvn.b], writes=[pk.b])
                P.op("act", I("activation", out=kt_.ap[0:64, c0:c0 + n], in_=pk.ap[0:64, 0:n], func=AF.Copy), reads=[pk.b], writes=[kt_.b])
                for kk in range(2):
                    P.op("pe", I("matmul", pqn.ap[0:64, 0:n], lhsT=wq4[:, kk, h, 0:64], rhs=qn.ap[:, kk, c0:c0 + n], start=(kk == 0), stop=(kk == 1)),
                         reads=[Wq.b, qn.b], writes=[pqn.b])
                for kk in range(2):
                    P.op("pe", I("matmul", pqp.ap[64:96, 0:n], lhsT=wq4[:, kk, h, 64:96], rhs=qn.ap[:, kk, c0:c0 + n], start=(kk == 0), stop=(kk == 1)),
                         reads=[Wq.b, qn.b], writes=[pqp.b])
                P.op("dve", I("tensor_copy", out=qt_.ap[0:64, c0:c0 + n], in_=pqn.ap[0:64, 0:n]), reads=[pqn.b], writes=[qt_.b])
                a1, a2 = t1[bi % 2], t2[bi % 2]
                P.op("dve", I("tensor_tensor", out=a1.ap[64:96, 0:n], in0=pqp.ap[64:96, 0:n], in1=rope.ap[64:96, 0, c0:c0 + n], op=ALU.mult),
                     reads=[pqp.b, rope.b], writes=[a1.b])
                for kk in range(2):
                    P.op("pe", I("matmul", pqr.ap[64:96, 0:n], lhsT=wr4[:, kk, h, :], rhs=qn.ap[:, kk, c0:c0 + n], start=(kk == 0), stop=(kk == 1)),
                         reads=[Wqr.b, qn.b], writes=[pqr.b])
                P.op("dve", I("tensor_tensor", out=a2.ap[64:96, 0:n], in0=pqr.ap[64:96, 0:n], in1=rope.ap[64:96, 1, c0:c0 + n], op=ALU.mult),
                     reads=[pqr.b, rope.b], writes=[a2.b])
                P.op("pool", I("tensor_tensor", out=qt_.ap[64:96, c0:c0 + n], in0=a1.ap[64:96, 0:n], in1=a2.ap[64:96, 0:n], op=ALU.add),
                     reads=[a1.b, a2.b], writes=[qt_.b])
            if MSUB == "c":
                continue
            dst = mix.ap[4 + h // 2, (h % 2) * 64:(h % 2) * 64 + 64, :]
            vfn = lambda kt, h=h: V4[:, kt, h, :]
            for bi, (c0, n) in enumerate(blocks):
                ktl = [0, 1] if c0 < CTX else list(range(NT))
                self.attend(A, kt_, qt_, 96, vfn, Vall.b, ktl, c0, n, scale, dst[:, c0:c0 + n], mix.b)

    def s_inproj_odd(self, pre):
        k, P, d, T, NT, SEQ = self.k, self.P, self.din, self.T, self.NT, self.SEQ
        sc = self.dsc
        uT, W = pre
        Wr = k.sb([8, 640], BF16, "Wr")
        w4 = W.ap[:, :, 1536:2176].rearrange("p k (h c) -> p k h c", h=10)
        r4 = Wr.ap.rearrange("p k (h c) -> p k h c", h=10)
        P.op("act", I("activation", out=r4[:, :, :, 0:32], in_=w4[:, :, :, 32:64], func=AF.Copy, scale=-1.0), reads=[W.b], writes=[Wr.b])
        P.op("act", I("activation", out=r4[:, :, :, 32:64], in_=w4[:, :, :, 0:32], func=AF.Copy), reads=[W.b], writes=[Wr.b])
        identf = k.sb(128, F32, "identf")
        P.dma("sp", identf.ap, d["k_identf"], writes=[identf.b])
        ones = k.sb(128, F32, "ones")
        P.op("pool", I("memset", ones.ap, 1.0), writes=[ones.b])
        grow = k.sb(64, F32, "grow")
        for i, nm in enumerate(("gqa_q_norm", "gqa_k_norm")):
            g = d[nm][0]
            P.dma("sp", grow.ap[2 * i:2 * i + 1, :], g.rearrange("(a n) -> a n", a=1), writes=[grow.b])
            P.dma("sp", grow.ap[2 * i + 1:2 * i + 2, 0:32], g[32:64].rearrange("(a n) -> a n", a=1), writes=[grow.b])
            P.dma("sp", grow.ap[2 * i + 1:2 * i + 2, 32:64], g[0:32].rearrange("(a n) -> a n", a=1), writes=[grow.b])
        gcol = k.sb(4, F32, "gcol")
        pgc = k.ps(7)
        P.op("pe", I("transpose", out=pgc.ap[0:64, 0:4], in_=grow.ap[0:4, :], identity=identf.ap[0:4, 0:4]), reads=[grow.b, identf.b], writes=[pgc.b])
        P.op("dve", I("tensor_copy", out=gcol.ap[0:64], in_=pgc.ap[0:64, 0:4]), reads=[pgc.b], writes=[gcol.b])
        rope = k.sb([2, T], F32, "rope64")
        for i in range(2):
            P.dma("sp", rope.ap[0:64, i, :], d["k_rope64"][i], writes=[rope.b])
        blocks = self.tok_blocks()
        stage = [k.sb(T, BF16, "hstage%d" % i) for i in range(2)]
        pz = [k.ps(0), k.ps(1)]
        oq = sc["oq_d"]
        import os
        OSUB = os.environ.get("OSUB", "")
        if OSUB == "a":
            return
        it = 0
        for hh in range(16):
            sg = stage[hh % 2]
            for (c0, n) in blocks:
                ps = pz[it % 2]
                it += 1
                for kk in range(8):
                    P.op("pe", I("matmul", ps.ap[0:64, 0:n], lhsT=W.ap[:, kk, hh * 64:(hh + 1) * 64], rhs=uT.ap[:, kk, c0:c0 + n], start=(kk == 0), stop=(kk == 7)),
                         reads=[W.b, uT.b], writes=[ps.b])
                P.op("act", I("activation", out=sg.ap[0:64, c0:c0 + n], in_=ps.ap[0:64, 0:n], func=AF.Copy), reads=[ps.b], writes=[sg.b])
            P.dma("sp", oq.ap[hh], sg.ap[0:64, :], reads=[sg.b], writes=[oq.b])
        if OSUB == "b":
            return
        gk = sc["gk_d"]
        pr = [k.ps(2), k.ps(3)]
        pss = [k.ps(4), k.ps(5)]
        sq = [k.sb(512, F32) for _ in range(2)]; rn = [k.sb(512, F32) for _ in range(2)]
        zg = [k.sb(512, F32) for _ in range(2)]; zr = [k.sb(512, F32) for _ in range(2)]
        items = [(hh, bi, c0, n) for hh in range(10) for bi, (c0, n) in enumerate(blocks)]

        def g_front(idx):
            hh, bi, c0, n = items[idx]
            ps, ps2 = pz[idx % 2], pr[idx % 2]
            wc = 1536 + hh * 64
            for kk in range(8):
                P.op("pe", I("matmul", ps.ap[0:64, 0:n], lhsT=W.ap[:, kk, wc:wc + 64], rhs=uT.ap[:, kk, c0:c0 + n], start=(kk == 0), stop=(kk == 7)),
                     reads=[W.b, uT.b], writes=[ps.b])
            for kk in range(8):
                P.op("pe", I("matmul", ps2.ap[0:64, 0:n], lhsT=Wr.ap[:, kk, hh * 64:(hh + 1) * 64], rhs=uT.ap[:, kk, c0:c0 + n], start=(kk == 0), stop=(kk == 7)),
                     reads=[Wr.b, uT.b], writes=[ps2.b])

        def g_back(idx):
            hh, bi, c0, n = items[idx]
            sg = stage[hh % 2]
            gi = 0 if hh < 8 else 2
            ps, ps2, p3 = pz[idx % 2], pr[idx % 2], pss[idx % 2]
            s2, r2, a1, a2 = sq[idx % 2], rn[idx % 2], zg[idx % 2], zr[idx % 2]
            P.op("act", I("activation", out=s2.ap[0:64, 0:n], in_=ps.ap[0:64, 0:n], func=AF.Square), reads=[ps.b], writes=[s2.b])
            P.op("pe", I("matmul", p3.ap[0:64, 0:n], lhsT=ones.ap[0:64, 0:64], rhs=s2.ap[0:64, 0:n], start=True, stop=True), reads=[ones.b, s2.b], writes=[p3.b])
            P.op("act", I("activation", out=r2.ap[0:64, 0:n], in_=p3.ap[0:64, 0:n], func=AF.Sqrt, scale=1.0 / 64, bias=EPS), reads=[p3.b], writes=[r2.b])
            P.op("dve", I("reciprocal", out=r2.ap[0:64, 0:n], in_=r2.ap[0:64, 0:n]), reads=[r2.b], writes=[r2.b])
            P.op("dve", I("scalar_tensor_tensor", out=a1.ap[0:64, 0:n], in0=ps.ap[0:64, 0:n], scalar=gcol.ap[0:64, gi:gi + 1], in1=rope.ap[0:64, 0, c0:c0 + n],
                          op0=ALU.mult, op1=ALU.mult), reads=[ps.b, gcol.b, rope.b], writes=[a1.b])
            P.op("dve", I("scalar_tensor_tensor", out=a2.ap[0:64, 0:n], in0=ps2.ap[0:64, 0:n], scalar=gcol.ap[0:64, gi + 1:gi + 2], in1=rope.ap[0:64, 1, c0:c0 + n],
                          op0=ALU.mult, op1=ALU.mult), reads=[ps2.b, gcol.b, rope.b], writes=[a2.b])
            P.op("pool", I("tensor_tensor", out=a1.ap[0:64, 0:n], in0=a1.ap[0:64, 0:n], in1=a2.ap[0:64, 0:n], op=ALU.add), reads=[a1.b, a2.b], writes=[a1.b])
            P.op("pool", I("tensor_tensor", out=sg.ap[0:64, c0:c0 + n], in0=a1.ap[0:64, 0:n], in1=r2.ap[0:64, 0:n], op=ALU.mult), reads=[a1.b, r2.b], writes=[sg.b])
            if bi == len(blocks) - 1:
                if hh < 8:
                    P.dma("sp", oq.ap[16 + hh], sg.ap[0:64, :], reads=[sg.b], writes=[oq.b])
                else:
                    P.dma("sp", gk.ap[hh - 8], sg.ap[0:64, :], reads=[sg.b], writes=[gk.b])

        g_front(0)
        for idx in range(len(items)):
            if idx + 1 < len(items):
                g_front(idx + 1)
            g_back(idx)
        if OSUB == "c":
            return
        ov = sc["ov_d"]
        vst = [k.sb(640, BF16) for _ in range(2)]
        pv1 = [k.ps(0), k.ps(1)]; pv2 = [k.ps(2), k.ps(3)]
        for n in range(NT):
            p1, p2, vs = pv1[n % 2], pv2[n % 2], vst[n % 2]
            for kk in range(8):
                P.op("pe", I("matmul", p1.ap, lhsT=uT.ap[:, kk, n * 128:(n + 1) * 128], rhs=W.ap[:, kk, 1024:1536], start=(kk == 0), stop=(kk == 7)),
                     reads=[W.b, uT.b], writes=[p1.b])
            for kk in range(8):
                P.op("pe", I("matmul", p2.ap[:, 0:128], lhsT=uT.ap[:, kk, n * 128:(n + 1) * 128], rhs=W.ap[:, kk, 2176:2304], start=(kk == 0), stop=(kk == 7)),
                     reads=[W.b, uT.b], writes=[p2.b])
            P.op("act", I("activation", out=vs.ap[:, 0:512], in_=p1.ap, func=AF.Copy), reads=[p1.b], writes=[vs.b])
            P.op("dve", I("tensor_copy", out=vs.ap[:, 512:640], in_=p2.ap[:, 0:128]), reads=[p2.b], writes=[vs.b])
            P.dma("sp", ov.ap[n * 128:(n + 1) * 128, :], vs.ap, reads=[vs.b], writes=[ov.b])

    def s_na(self):
        k, P, d, T, NT, SEQ = self.k, self.P, self.din, self.T, self.NT, self.SEQ
        sc = self.dsc
        nqt = SEQ // 128
        oq, ov, mix = sc["oq_d"], sc["ov_d"], sc["mixT_d"]
        QT = [k.sb(T, BF16) for _ in range(2)]; KT = [k.sb(T, BF16) for _ in range(2)]
        Vh = [k.sb([NT, 128], BF16) for _ in range(2)]
        Bs = [k.sb([5, 640], F32) for _ in range(2)]
        for i in range(2):
            P.op("pool", I("memset", Vh[i].ap, 1.0), writes=[Vh[i].b])
        sb = [k.sb(640, F32) for _ in range(2)]
        pt = [k.sb(896, BF16) for _ in range(2)]
        osb = [k.sb(512, F32) for _ in range(2)]; om = [k.sb(512, BF16) for _ in range(2)]
        ones = k.sb(128, F32)
        P.op("pool", I("memset", ones.ap, 1.0), writes=[ones.b])
        pS = [Tl(k.psum[:, 0:1024]), Tl(k.psum[:, 1024:2048])]
        pO = [k.ps(4), k.ps(5)]
        pB = k.ps(6)
        scale = 0.125
        ovv = ov.ap.rearrange("(n p) c -> p n c", p=128)
        cnt = {"it": 0}
        def na_load(h):
            q_, k_, v_, b_ = QT[h % 2], KT[h % 2], Vh[h % 2], Bs[h % 2]
            P.dma("sp", q_.ap[0:64, :], oq.ap[h], reads=[oq.b], writes=[q_.b])
            P.dma("sp", k_.ap[0:64, :], oq.ap[8 + h], reads=[oq.b], writes=[k_.b])
            P.dma("sp", v_.ap[:, :, 0:64], ovv[:, :, h * 64:(h + 1) * 64], reads=[ov.b], writes=[v_.b])
            P.dma("sp", b_.ap, d["na_bias"][:, h].rearrange("c p n -> p c n"), writes=[b_.b])

        na_load(0)
        for h in range(8):
            q_, k_, v_, b_ = QT[h % 2], KT[h % 2], Vh[h % 2], Bs[h % 2]
            if h + 1 < 8:
                na_load(h + 1)
            dst = mix.ap[h // 2, (h % 2) * 64:(h % 2) * 64 + 64, :]
            def na_qk(qt):
                i = cnt["it"]
                cnt["it"] += 1
                cls = na_class(qt, nqt)
                kt0 = int(np.clip(qt - 2, 0, nqt - 5))
                tiles = [2 + kt0 + j for j in range(5)] + [0, 1]
                ps, s_, p_ = pS[i % 2], sb[i % 2], pt[i % 2]
                qc = CTX + qt * 128
                for j, t in enumerate(tiles):
                    P.op("pe", I("matmul", ps.ap[:, j * 128:(j + 1) * 128], lhsT=k_.ap[0:64, t * 128:(t + 1) * 128], rhs=q_.ap[0:64, qc:qc + 128], start=True, stop=True),
                         reads=[k_.b, q_.b], writes=[ps.b])
                P.op("dve", I("scalar_tensor_tensor", out=s_.ap, in0=ps.ap[:, 0:640], scalar=scale, in1=b_.ap[:, cls, :], op0=ALU.mult, op1=ALU.add),
                     reads=[ps.b, b_.b], writes=[s_.b])
                P.op("act", I("activation", out=p_.ap[:, 0:640], in_=s_.ap, func=AF.Exp), reads=[s_.b], writes=[p_.b])
                P.op("act", I("activation", out=p_.ap[:, 640:896], in_=ps.ap[:, 640:896], func=AF.Exp, scale=scale), reads=[ps.b], writes=[p_.b])
                return tiles, p_

            pend = na_qk(0)
            for g0 in range(0, nqt, 4):
                gi = (h * (nqt // 4) + g0 // 4)
                po, os_, om_ = pO[gi % 2], osb[gi % 2], om[gi % 2]
                for qq in range(4):
                    qt = g0 + qq
                    tiles, p_ = pend
                    if qt + 1 < nqt:
                        pend = na_qk(qt + 1)
                    for j, t in enumerate(tiles):
                        P.op("pe", I("matmul", po.ap[:, qq * 128:(qq + 1) * 128], lhsT=v_.ap[:, t, :], rhs=p_.ap[:, j * 128:(j + 1) * 128], start=(j == 0), stop=(j == 6)),
                             reads=[v_.b, p_.b], writes=[po.b])
                c0 = CTX + g0 * 128
                P.op("act", I("activation", out=os_.ap[0:65, :], in_=po.ap[0:65, :], func=AF.Copy), reads=[po.b], writes=[os_.b])
                P.op("dve", I("reciprocal", out=os_.ap[64:65, :], in_=os_.ap[64:65, :]), reads=[os_.b], writes=[os_.b])
                P.op("pe", I("matmul", pB.ap[0:64, :], lhsT=ones.ap[64:65, 0:64], rhs=os_.ap[64:65, :], start=True, stop=True), reads=[ones.b, os_.b], writes=[pB.b])
                P.op("dve", I("tensor_tensor", out=om_.ap[0:64, :], in0=os_.ap[0:64, :], in1=pB.ap[0:64, :], op=ALU.mult), reads=[os_.b, pB.b], writes=[om_.b])
                P.dma("sp", dst[:, c0:c0 + 512], om_.ap[0:64, :], reads=[om_.b], writes=[mix.b])

    def s_gqa(self):
        k, P, d, T, NT, SEQ = self.k, self.P, self.din, self.T, self.NT, self.SEQ
        sc = self.dsc
        ffn_w = self.ffn_weights(1)
        oq, ov, mix, gk = sc["oq_d"], sc["ov_d"], sc["mixT_d"], sc["gk_d"]
        KT = [k.sb(T, BF16) for _ in range(2)]
        for i in range(2):
            P.dma("sp", KT[i].ap[0:64, :], gk.ap[i], reads=[gk.b], writes=[KT[i].b])
        V = k.sb([NT, 2 * 128], BF16)
        V4 = V.ap.rearrange("p n (h c) -> p n h c", h=2)
        P.op("pool", I("memset", V.ap, 1.0), writes=[V.b])
        ovv = ov.ap.rearrange("(n p) c -> p n c", p=128)
        for i in range(2):
            P.dma("sp", V4[:, :, i, 0:64], ovv[:, :, 512 + i * 64:512 + (i + 1) * 64], reads=[ov.b], writes=[V.b])
        QT = [k.sb(T, BF16) for _ in range(2)]
        A = self.attn_setup()
        self.ffn_weights_issue(1, ffn_w)
        blocks = [b for b in self.tok_blocks() if b[0] >= CTX]
        P.dma("sp", QT[0].ap[0:64, :], oq.ap[16], reads=[oq.b], writes=[QT[0].b])
        for h in range(8):
            q_ = QT[h % 2]
            if h + 1 < 8:
                P.dma("sp", QT[(h + 1) % 2].ap[0:64, :], oq.ap[16 + h + 1], reads=[oq.b], writes=[QT[(h + 1) % 2].b])
            kv = h // 4
            dst = mix.ap[4 + h // 2, (h % 2) * 64:(h % 2) * 64 + 64, :]
            vfn = lambda kt, kv=kv: V4[:, kt, kv, :]
            for (c0, n) in blocks:
                self.attend(A, KT[kv], q_, 64, vfn, V.b, list(range(NT)), c0, n, 0.125, dst[:, c0:c0 + n], mix.b)


def _rope_tables(SEQ, rot_dim):
    t = np.arange(SEQ)
    row = (t // 64).astype(np.float32)
    col = (t % 64).astype(np.float32)
    nf = rot_dim // 4
    inv = (np.float32(10000.0) ** (-np.arange(nf, dtype=np.float32) / nf)).astype(np.float32)
    ang = np.concatenate([row[:, None] * inv, col[:, None] * inv], -1)
    half = rot_dim // 2
    idx = np.arange(rot_dim) % half
    cos = np.ones((rot_dim, CTX + SEQ), np.float32)
    sin = np.zeros((rot_dim, CTX + SEQ), np.float32)
    cos[:, CTX:] = np.cos(ang)[:, idx].T
    sin[:, CTX:] = np.sin(ang)[:, idx].T
    return np.stack([cos, sin]).astype(np.float32)


def _na_bias(rpb, SEQ):
    rows = SEQ // 64
    nqt = rows // 2
    out = np.full((5, 8, 128, 640), NEG, np.float32)
    reps = {0: 0, 1: 1, 2: min(2, nqt - 3), 3: nqt - 2, 4: nqt - 1}
    for cls, qt in reps.items():
        kt0 = int(np.clip(qt - 2, 0, nqt - 5))
        for qi in range(128):
            r, cc = 2 * qt + qi // 64, qi % 64
            rs = int(np.clip(r - 4, 0, rows - 8))
            cs = int(np.clip(cc - 8, 0, 64 - 16))
            for kr in range(rs, rs + 8):
                j = kr // 2 - kt0
                p0 = (kr % 2) * 64
                kc = np.arange(cs, cs + 16)
                out[cls, :, p0 + kc, j * 128 + qi] = rpb[:, kr - r + 7, kc - cc + 15].T
    return out


def host_consts(SEQ):
    import ml_dtypes
    i = np.arange(128)
    tril = np.stack([(i[:, None] <= i[None, :]), (i[:, None] >= i[None, :])]).astype(np.float32)
    mstrict = np.stack([(i[None, :] < i[:, None]), (i[None, :] > i[:, None])]).astype(np.float32)
    gm = np.zeros((7, 128, 128), np.float32)
    cc, ss = i[:, None], i[None, :]
    gm[0] = (cc // 16 == ss // 16)
    for li, B in enumerate((32, 64, 128)):
        H = B // 2
        m = (cc // B == ss // B) & (cc % B >= H) & (ss % B < H)
        gm[1 + li] = m
        gm[4 + li] = m.T
    return {
        "k_gmask": gm,
        "k_identf": np.eye(128, dtype=np.float32),
        "k_identb": np.eye(128, dtype=np.float32).astype(ml_dtypes.bfloat16),
        "k_tril": tril, "k_mstrict": mstrict,
        "k_rope32": _rope_tables(SEQ, 32), "k_rope64": _rope_tables(SEQ, 64),
    }


def na_class(qt, nqt):
    if qt < 2:
        return qt
    if qt >= nqt - 2:
        return 4 - (nqt - 1 - qt)
    return 2


_CACHE = {}


def core_inputs(inputs, b, SEQ):
    m = {}
    m["x"] = np.ascontiguousarray(inputs["x"][b])
    m["ctx"] = np.ascontiguousarray(inputs["ctx"][b])
    m["c"] = np.ascontiguousarray(inputs["c"][b])
    for k_ in ("c_ctx", "w_mod", "b_mod", "g_pre_mix", "g_post_mix", "g_pre_ffn", "g_post_ffn", "w_ffn_gate", "w_ffn_up",
               "w_ffn_down", "w_in_even", "w_out_even", "gdn_conv", "gdn_a_log", "gdn_dt_bias", "gdn_out_norm", "mla_q_norm",
               "mla_w_q_up", "mla_kv_norm", "mla_w_kv_up", "w_in_odd", "w_out_odd", "gqa_q_norm", "gqa_k_norm"):
        m[k_] = np.ascontiguousarray(np.asarray(inputs[k_], dtype=np.float32))
    return m


def kernel(**inputs):
    inputs = {k_: np.asarray(v) for k_, v in inputs.items()}
    B, SEQ = inputs["x"].shape[0], inputs["x"].shape[1]
    if SEQ not in _CACHE:
        _CACHE[SEQ] = Builder(SEQ).build()
    nc = _CACHE[SEQ]
    consts = host_consts(SEQ)
    nab = _na_bias(np.asarray(inputs["na_rpb"][0], np.float32), SEQ)
    in_maps = []
    for b in range(B):
        m = core_inputs(inputs, b, SEQ)
        m.update(consts)
        m["na_bias"] = nab
        in_maps.append(m)
    res = run_bass_kernel_spmd(nc, in_maps, core_ids=list(range(B)))
    return np.stack([np.asarray(r["y"], dtype=np.float32) for r in res.results], 0)
```

```python
import contextlib
import numpy as np
import concourse.bass as bass
import concourse.mybir as mybir
from concourse.bass_utils import run_bass_kernel_spmd

F32 = mybir.dt.float32
BF16 = mybir.dt.bfloat16
U8 = mybir.dt.uint8
AF = mybir.ActivationFunctionType
ALU = mybir.AluOpType
AX = mybir.AxisListType

ENGS = ("pe", "act", "dve", "pool", "sp")
N_HW_SEMS = 24
N_SW_SEMS = 48
N_DMA_SEMS = N_HW_SEMS + N_SW_SEMS
D = 1024
CTX = 256
DFF = 2816
EPS = 1e-6
NEG = -30000.0


class Ins:
    __slots__ = ("eng", "fn", "deps", "signal", "seq", "dma", "dsem", "dval", "prev_dma")

    def __init__(self, eng, fn, dma=False):
        self.eng = eng
        self.fn = fn
        self.deps = []
        self.signal = False
        self.seq = 0
        self.dma = dma
        self.dsem = -1
        self.dval = 0
        self.prev_dma = None


class Buf:
    __slots__ = ("name", "writer", "readers", "ws")

    def __init__(self, name=""):
        self.name = name
        self.writer = None
        self.readers = []
        self.ws = []


class Prog:
    def __init__(self, nc):
        self.nc = nc
        self.ins = []
        self.dma_rr = 0
        self.sw_rr = 0
        self.dma_last = [None] * N_DMA_SEMS
        self.last = {}
        self.open_dmas = []

    def op(self, eng, fn, reads=(), writes=(), dma=False, indep=False):
        i = Ins(eng, fn, dma)
        deps = []
        for b in reads:
            if b.writer is not None:
                deps.append(b.writer)
            deps.extend(b.ws)
        for b in writes:
            if b.writer is not None:
                deps.append(b.writer)
            deps.extend(b.readers)
            if not indep:
                deps.extend(b.ws)
        seen = set()
        for d in deps:
            if id(d) in seen or d is i:
                continue
            seen.add(id(d))
            if d.eng == "pe" and eng == "pe" and not d.dma and not dma:
                continue
            d.signal = True
            i.deps.append(d)
        for b in reads:
            b.readers.append(i)
        for b in writes:
            if indep:
                b.ws.append(i)
            else:
                b.writer = i
                b.readers = []
                b.ws = []
        if dma:
            if eng == "pool":
                k = N_HW_SEMS + self.sw_rr % N_SW_SEMS
                self.sw_rr += 1
            else:
                k = self.dma_rr % N_HW_SEMS
                self.dma_rr += 1
            i.dsem = k
            i.prev_dma = self.dma_last[k]
            self.dma_last[k] = i
            i.signal = True
            self.open_dmas.append(i)
        else:
            self.last[eng] = i
        self.ins.append(i)
        return i

    def dma(self, q, out, in_, reads=(), writes=(), slow=False):
        indep = str(out.space) == "DRAM"
        if slow:
            return self.op(q, I("dma_start", out=out, in_=in_, allow_slow_non_contiguous=True),
                           reads, writes, dma=True, indep=indep)
        return self.op(q, I("dma_start", out=out, in_=in_), reads, writes, dma=True, indep=indep)

    def barrier(self):
        lasts = [v for v in self.last.values()]
        for l in lasts:
            l.signal = True
        dmas = self.open_dmas
        self.open_dmas = []
        for e in ENGS:
            i = Ins(e, None)
            i.deps = lasts + dmas
            self.ins.append(i)

    def emit(self, final_wait_eng="sp"):
        nc = self.nc
        with contextlib.ExitStack() as st:
            esem = {e: st.enter_context(nc.semaphore("s_" + e)) for e in ENGS}
            dsem = [st.enter_context(nc.semaphore("d%d" % k)) for k in range(N_DMA_SEMS)]
            cnt = {e: 0 for e in ENGS}
            dcnt = [0] * N_DMA_SEMS
            for i in self.ins:
                if i.dma:
                    dcnt[i.dsem] += 16
                    i.dval = dcnt[i.dsem]
                elif i.signal:
                    cnt[i.eng] += 1
                    i.seq = cnt[i.eng]
            per = {e: [] for e in ENGS}
            for i in self.ins:
                per[i.eng].append(i)
            last_dmas = [d for d in self.dma_last if d is not None]
            block = st.enter_context(nc.Block())
            engobj = {"pe": "tensor", "act": "scalar", "dve": "vector", "pool": "gpsimd", "sp": "sync"}

            def run(ename, e):
                seen_e = {x: 0 for x in ENGS}
                seen_d = [0] * N_DMA_SEMS
                for i in per[ename]:
                    deps = i.deps
                    if i.dma and i.prev_dma is not None:
                        deps = deps + [i.prev_dma]
                    for d in deps:
                        if d.dma:
                            if seen_d[d.dsem] < d.dval:
                                e.wait_ge(dsem[d.dsem], d.dval)
                                seen_d[d.dsem] = d.dval
                        else:
                            if seen_e[d.eng] < d.seq:
                                e.wait_ge(esem[d.eng], d.seq)
                                seen_e[d.eng] = d.seq
                    if i.fn is None:
                        continue
                    r = i.fn(e)
                    if i.dma:
                        r.then_inc(dsem[i.dsem], 16)
                    elif i.signal:
                        r.then_inc(esem[i.eng], 1)
                if ename == final_wait_eng:
                    for d in last_dmas:
                        if seen_d[d.dsem] < d.dval:
                            e.wait_ge(dsem[d.dsem], d.dval)
                            seen_d[d.dsem] = d.dval

            for ename in ENGS:
                getattr(block, engobj[ename])(lambda e, ename=ename: run(ename, e))


class Tl:
    __slots__ = ("ap", "b")

    def __init__(self, ap, name=""):
        self.ap = ap
        self.b = Buf(name)


class KB:
    def __init__(self, nc, P, arena, psum, arena_bytes):
        self.nc, self.P, self.arena, self.psum = nc, P, arena, psum
        self.off = 0
        self.cap = arena_bytes
        self.pcache = {}

    def reset(self, to=0):
        self.off = to
        self.pcache = {}

    def sb(self, free, dt, name=""):
        if isinstance(free, int):
            free = [free]
        n = int(np.prod(free))
        sz = 2 if dt == BF16 else 4
        nbytes = ((n * sz + 63) // 64) * 64
        o = self.off
        self.off += nbytes
        assert self.off <= self.cap, "SBUF arena overflow at %s: %d" % (name, self.off)
        ap = self.arena[:, o:o + n * sz].bitcast(dt)
        if len(free) == 2:
            ap = ap.rearrange("p (a b) -> p a b", a=free[0])
        elif len(free) == 3:
            ap = ap.rearrange("p (a b c) -> p a b c", a=free[0], b=free[1])
        return Tl(ap, name)

    def ps(self, bank, cols=512, off=0, name=""):
        key = (bank, cols, off)
        if key not in self.pcache:
            self.pcache[key] = Tl(self.psum[:, bank * 512 + off: bank * 512 + off + cols], name)
        return self.pcache[key]


def bcast_mid(ap2, n):
    return ap2.unsqueeze(2).to_broadcast([ap2.shape[0], ap2.shape[1], n])


def bcast_head(ap2, a):
    return ap2.unsqueeze(1).to_broadcast([ap2.shape[0], a, ap2.shape[1]])


def I(m, *a, **kw):
    return lambda e: getattr(e, m)(*a, **kw)


class Builder:
    def __init__(self, SEQ, dbg=(), upto=None):
        self.SEQ = SEQ
        self.T = CTX + SEQ
        self.NT = self.T // 128
        self.dbg = set(dbg)
        self.upto = upto
        self.nc = bass.Bass("TRN2", target_bir_lowering=False)
        self.P = Prog(self.nc)
        self.din = {}
        self.dsc = {}
        self.hb = [Buf("hb%d" % n) for n in range(self.NT)]

    def inp(self, name, shape, dt=F32):
        self.din[name] = self.nc.dram_tensor(name, list(shape), dt, kind="ExternalInput").ap()
        return self.din[name]

    def scratch(self, name, shape, dt=F32):
        kind = "ExternalOutput" if name in self.dbg else "Internal"
        ap = self.nc.dram_tensor(name, list(shape), dt, kind=kind).ap()
        self.dsc[name] = Tl(ap, name)
        return self.dsc[name]

    def tok_blocks(self, bs=512):
        out = [(0, CTX)]
        c = CTX
        while c < self.T:
            out.append((c, min(bs, self.T - c)))
            c += bs
        return out

    def build(self):
        nc, P = self.nc, self.P
        SEQ, T, NT = self.SEQ, self.T, self.NT
        i_ = self.inp
        x = i_("x", [SEQ, D]); ctx = i_("ctx", [CTX, D]); c = i_("c", [D]); c_ctx = i_("c_ctx", [D])
        w_mod = i_("w_mod", [2, D, 6 * D]); b_mod = i_("b_mod", [2, 6 * D])
        g_pre_mix = i_("g_pre_mix", [2, D]); g_post_mix = i_("g_post_mix", [2, D])
        g_pre_ffn = i_("g_pre_ffn", [2, D]); g_post_ffn = i_("g_post_ffn", [2, D])
        w_ffn_gate = i_("w_ffn_gate", [2, D, DFF]); w_ffn_up = i_("w_ffn_up", [2, D, DFF])
        w_ffn_down = i_("w_ffn_down", [2, DFF, D])
        w_in_even = i_("w_in_even", [1, D, 2496]); w_out_even = i_("w_out_even", [1, D, D])
        gdn_conv = i_("gdn_conv", [1, 5, 1536]); gdn_a_log = i_("gdn_a_log", [1, 2, 8])
        gdn_dt_bias = i_("gdn_dt_bias", [1, 2, 8]); gdn_out_norm = i_("gdn_out_norm", [1, 64])
        mla_q_norm = i_("mla_q_norm", [1, 256]); mla_w_q_up = i_("mla_w_q_up", [1, 256, 768])
        mla_kv_norm = i_("mla_kv_norm", [1, 128]); mla_w_kv_up = i_("mla_w_kv_up", [1, 128, 1024])
        w_in_odd = i_("w_in_odd", [1, D, 2304]); w_out_odd = i_("w_out_odd", [1, D, D])
        na_bias = i_("na_bias", [5, 8, 128, 640])
        gqa_q_norm = i_("gqa_q_norm", [1, 64]); gqa_k_norm = i_("gqa_k_norm", [1, 64])
        k_identf = i_("k_identf", [128, 128]); k_identb = i_("k_identb", [128, 128], BF16)
        k_tril = i_("k_tril", [2, 128, 128])
        k_mstrict = i_("k_mstrict", [2, 128, 128])
        k_gmask = i_("k_gmask", [7, 128, 128])
        k_rope32 = i_("k_rope32", [2, 32, T])
        k_rope64 = i_("k_rope64", [2, 64, T])
        y = self.nc.dram_tensor("y", [SEQ, D], F32, kind="ExternalOutput").ap()
        self.y = Tl(y, "y")
        s_ = self.scratch
        s_("modv_d", [2, 2, 6 * D]); s_("uT_d", [8, 128, T], BF16); s_("hbuf_d", [T, D])
        s_("qkT_d", [16, 64, T]); s_("ktok_d", [T, 512]); s_("vtok_d", [T, 512])
        s_("gate_d", [T, 512]); s_("gb_d", [T, 32])
        s_("qn_d", [2, 128, T], BF16); s_("kvn_d", [128, T], BF16); s_("kpe_d", [32, T], BF16)
        s_("o_d", [2, T, 512]); s_("mixT_d", [8, 128, T], BF16)
        s_("oq_d", [24, 64, T], BF16); s_("ov_d", [T, 640], BF16); s_("gk_d", [2, 64, T], BF16)
        with contextlib.ExitStack() as st:
            AB = 204 * 1024
            arena = st.enter_context(nc.sbuf_tensor("arena", [128, AB], U8))
            psum = st.enter_context(nc.psum_tensor("psum", [128, 4096], F32))
            self.k = KB(nc, P, arena, psum, AB)
            stages = [
                ("mod", self.s_mod),
                ("inE", lambda: self.s_pre_in(0)),
                ("gdn", self.s_gdn),
                ("mla", self.s_mla),
                ("gdno", self.s_gdn_out),
                ("ffn0", lambda: self.s_outffn(0)),
                ("inO", lambda: self.s_pre_in(1)),
                ("na", self.s_na),
                ("gqa", self.s_gqa),
                ("ffn1", lambda: self.s_outffn(1)),
            ]
            for name, fn in stages:
                self.k.reset()
                fn()
                P.barrier()
                if self.upto == name:
                    break
            if self.upto is not None and self.upto != "ffn1":
                z = self.k.sb(D, F32)
                P.op("pool", I("memset", z.ap, 0.0), writes=[z.b])
                P.dma("sp", y[0:128, :], z.ap, reads=[z.b], writes=[self.y.b])
            P.emit()
        return nc

    def s_mod(self):
        k, P, d = self.k, self.P, self.din
        tmp = k.sb([8, 2], F32, "ctmp")
        csT = k.sb([8, 2], F32, "csT")
        P.dma("sp", tmp.ap[:, :, 0], d["c"].rearrange("(p k) -> p k", k=8), writes=[tmp.b], slow=True)
        P.dma("sp", tmp.ap[:, :, 1], d["c_ctx"].rearrange("(p k) -> p k", k=8), writes=[tmp.b], slow=True)
        P.op("act", I("activation", out=csT.ap, in_=tmp.ap, func=AF.Silu), reads=[tmp.b], writes=[csT.b])
        wb = [k.sb([8, 512], F32, "wmod%d" % i) for i in range(2)]
        bb = [k.sb(512, F32, "bmod%d" % i) for i in range(2)]
        ob = [k.sb(512, F32, "omod%d" % i) for i in range(2)]
        pss = [k.ps(0), k.ps(1)]
        modv = self.dsc["modv_d"]
        jobs = [(l, nb) for l in range(2) for nb in range(12)]

        def mod_load(it):
            l, nb = jobs[it]
            cs = slice(nb * 512, (nb + 1) * 512)
            wv = d["w_mod"][l].rearrange("(p k) n -> p k n", k=8)
            P.dma("sp", wb[it % 2].ap, wv[:, :, cs], writes=[wb[it % 2].b])
            P.dma("sp", bb[it % 2].ap[0:2, :], d["b_mod"][l, cs].partition_broadcast(2), writes=[bb[it % 2].b])

        mod_load(0)
        for it, (l, nb) in enumerate(jobs):
                w, b, o, ps = wb[it % 2], bb[it % 2], ob[it % 2], pss[it % 2]
                cs = slice(nb * 512, (nb + 1) * 512)
                for kk in range(8):
                    P.op("pe", I("matmul", ps.ap[0:2, :], lhsT=csT.ap[:, kk, :], rhs=w.ap[:, kk, :],
                                                                     start=(kk == 0), stop=(kk == 7)),
                         reads=[csT.b, w.b], writes=[ps.b])
                P.op("dve", I("tensor_tensor", out=o.ap[0:2, :], in0=ps.ap[0:2, :], in1=b.ap[0:2, :], op=ALU.add),
                     reads=[ps.b, b.b], writes=[o.b])
                if it + 1 < len(jobs):
                    mod_load(it + 1)
                P.dma("sp", modv.ap[l, :, cs], o.ap[0:2, :], reads=[o.b], writes=[modv.b])

    def load_mod_tiles(self, l, g_in, off_sh, off_sc):
        k, P = self.k, self.P
        modv = self.dsc["modv_d"]
        G = k.sb(D, F32, "G")
        P.dma("sp", G.ap, g_in[l].partition_broadcast(128), writes=[G.b])
        res = []
        for r in (1, 0):
            A = k.sb(D, F32, "A%d" % r)
            B = k.sb(D, F32, "B%d" % r)
            P.dma("sp", A.ap, modv.ap[l, r, off_sc:off_sc + D].partition_broadcast(128), reads=[modv.b], writes=[A.b])
            P.dma("sp", B.ap, modv.ap[l, r, off_sh:off_sh + D].partition_broadcast(128), reads=[modv.b], writes=[B.b])
            P.op("dve", I("scalar_tensor_tensor", out=A.ap, in0=A.ap, scalar=1.0, in1=G.ap, op0=ALU.add, op1=ALU.mult),
                 reads=[A.b, G.b], writes=[A.b])
            res += [A, B]
        return res + [G]

    def load_gate_tiles(self, l, g_in, off_gt):
        k, P = self.k, self.P
        modv = self.dsc["modv_d"]
        G = k.sb(D, F32, "Gp")
        P.dma("sp", G.ap, g_in[l].partition_broadcast(128), writes=[G.b])
        res = []
        for r in (1, 0):
            A = k.sb(D, F32, "Gt%d" % r)
            P.dma("sp", A.ap, modv.ap[l, r, off_gt:off_gt + D].partition_broadcast(128), reads=[modv.b], writes=[A.b])
            P.op("dve", I("tensor_tensor", out=A.ap, in0=A.ap, in1=G.ap, op=ALU.mult), reads=[A.b, G.b], writes=[A.b])
            res.append(A)
        return res + [G]

    def h_src(self, l, n):
        if l == 0:
            if n < 2:
                return self.din["ctx"][n * 128:(n + 1) * 128, :], None
            return self.din["x"][(n - 2) * 128:(n - 1) * 128, :], None
        hb = self.dsc["hbuf_d"]
        return hb.ap[n * 128:(n + 1) * 128, :], self.hb[n]

    def norm_mod_T(self, n, src_ap, src_b, tiles, bufs, uT_dst, it):
        k, P = self.k, self.P
        Ac, Bc, Al, Bl = tiles
        A, B = (Ac, Bc) if n < 2 else (Al, Bl)
        h, junk, st, t1, u, ps, identb = bufs["h"][it % 2], bufs["junk"], bufs["st"][it % 2], bufs["t1"][it % 2], \
            bufs["u"][it % 2], bufs["ps"][it % 2], bufs["identb"]
        P.dma("sp", h.ap, src_ap, reads=[src_b] if src_b else [], writes=[h.b])
        P.op("act", I("activation", out=junk.ap, in_=h.ap, func=AF.Square, accum_out=st.ap[:, 0:1]),
             reads=[h.b], writes=[junk.b, st.b])
        P.op("act", I("activation", out=st.ap[:, 1:2], in_=st.ap[:, 0:1], func=AF.Sqrt, scale=1.0 / D, bias=EPS),
             reads=[st.b], writes=[st.b])
        P.op("dve", I("reciprocal", out=st.ap[:, 2:3], in_=st.ap[:, 1:2]), reads=[st.b], writes=[st.b])
        P.op("dve", I("scalar_tensor_tensor", out=t1.ap, in0=h.ap, scalar=st.ap[:, 2:3], in1=A.ap, op0=ALU.mult, op1=ALU.mult),
             reads=[h.b, st.b, A.b], writes=[t1.b])
        P.op("pool", I("tensor_tensor", out=u.ap, in0=t1.ap, in1=B.ap, op=ALU.add), reads=[t1.b, B.b], writes=[u.b])
        psb = ps.ap.bitcast(BF16)
        for kk in range(8):
            P.op("pe", I("transpose", psb[:, kk * 128:(kk + 1) * 128], u.ap[:, kk * 128:(kk + 1) * 128], identb.ap),
                 reads=[u.b, identb.b], writes=[ps.b])
        P.op("act", I("activation", out=uT_dst.ap, in_=psb.rearrange("p (a b) -> p a b", a=8), func=AF.Copy),
             reads=[ps.b], writes=[uT_dst.b])
        return h

    def norm_bufs(self):
        k, P = self.k, self.P
        bufs = {
            "h": [k.sb(D, F32, "h%d" % i) for i in range(2)],
            "junk": k.sb(D, F32, "junk"),
            "st": [k.sb(8, F32, "st%d" % i) for i in range(2)],
            "t1": [k.sb(D, F32, "t1%d" % i) for i in range(2)],
            "u": [k.sb(D, BF16, "u%d" % i) for i in range(2)],
            "ps": [k.ps(6), k.ps(7)],
            "identb": k.sb(128, BF16, "identb"),
        }
        P.dma("sp", bufs["identb"].ap, self.din["k_identb"], writes=[bufs["identb"].b])
        return bufs

    def s_pre(self, l, sub, uT_sb=None):
        k, P = self.k, self.P
        tiles = self.load_mod_tiles(l, self.din["g_pre_mix"], 0, D)[:4]
        bufs = self.norm_bufs()
        for n in range(self.NT):
            src_ap, src_b = self.h_src(l, n)
            ut = Tl(uT_sb.ap[:, :, n * 128:(n + 1) * 128])
            ut.b = uT_sb.b
            self.norm_mod_T(n, src_ap, src_b, tiles, bufs, ut, n)

    def s_pre_in(self, l):
        k, P, d = self.k, self.P, self.din
        ncol = 2496 if l == 0 else 2304
        uT = k.sb([8, self.T], BF16, "uT")
        W = k.sb([8, ncol], BF16, "Win")
        wv = (d["w_in_even"] if l == 0 else d["w_in_odd"])[0].rearrange("(k p) n -> p k n", p=128)
        for kk in range(8):
            P.dma("pool", W.ap[:, kk, :], wv[:, kk, :], writes=[W.b])
        mark = k.off
        self.s_pre(l, "m", uT_sb=uT)
        P.barrier()
        k.reset(mark)
        if l == 0:
            self.s_inproj_even(pre=(uT, W))
        else:
            self.s_inproj_odd(pre=(uT, W))

    def s_inproj_even(self, pre):
        k, P, d, T, NT, SEQ = self.k, self.P, self.din, self.T, self.NT, self.SEQ
        sc = self.dsc
        uT, W = pre
        Wrot = k.sb([8, 32], BF16, "Wrot")
        P.op("act", I("activation", out=Wrot.ap[:, :, 0:16], in_=W.ap[:, :, 2480:2496], func=AF.Copy, scale=-1.0),
             reads=[W.b], writes=[Wrot.b])
        P.op("act", I("activation", out=Wrot.ap[:, :, 16:32], in_=W.ap[:, :, 2464:2480], func=AF.Copy),
             reads=[W.b], writes=[Wrot.b])
        identf = k.sb(128, F32, "identf")
        P.dma("sp", identf.ap, d["k_identf"], writes=[identf.b])
        ones = k.sb(128, F32, "ones")
        P.op("pool", I("memset", ones.ap, 1.0), writes=[ones.b])
        cwraw = k.sb(1536, F32, "cwraw")
        P.dma("sp", cwraw.ap[0:5, :], d["gdn_conv"][0], writes=[cwraw.b])
        cw = k.sb([12, 5], F32, "cw")
        pcw = k.ps(0)
        for pc in range(12):
            P.op("pe", I("transpose", out=pcw.ap[:, pc * 5:pc * 5 + 5], in_=cwraw.ap[0:5, pc * 128:(pc + 1) * 128], identity=identf.ap[0:5, 0:5]),
                 reads=[cwraw.b, identf.b], writes=[pcw.b])
        P.op("dve", I("tensor_copy", out=cw.ap, in_=pcw.ap[:, 0:60].rearrange("p (a b) -> p a b", a=12)),
             reads=[pcw.b], writes=[cw.b])
        bones = k.sb(128, F32, "bones")
        P.op("pool", I("memset", bones.ap, 0.0), writes=[bones.b])
        P.op("pool", I("memset", bones.ap[0:64, 0:64], 1.0), writes=[bones.b])
        P.op("pool", I("memset", bones.ap[64:128, 64:128], 1.0), writes=[bones.b])
        blocks = self.tok_blocks()
        mark = k.off
        import os
        sub = os.environ.get("KSUB", "")
        if sub == "a":
            return
        raws = [k.sb(T + 8, F32, "raw%d" % i) for i in range(2)]
        accs = [k.sb(T, F32, "acc%d" % i) for i in range(2)]
        for r_ in raws:
            P.op("pool", I("memset", r_.ap, 0.0), writes=[r_.b])
        sq = [k.sb(512, F32, "sq%d" % i) for i in range(2)]
        rn = [k.sb(512, F32, "rn%d" % i) for i in range(2)]
        stages = [k.sb([4, 128], F32, "tokstage%d" % i) for i in range(2)]
        pz = [k.ps(1), k.ps(2)]
        pss = [k.ps(3), k.ps(4)]
        ptr = [k.ps(5), k.ps(6)]
        qkT, ktok, vtok = sc["qkT_d"], sc["ktok_d"], sc["vtok_d"]

        def rawcol(c0):
            return c0 + 2 if c0 < CTX else c0 + 6

        cnt1 = {"it": 0}

        def front(pc):
            raw = raws[pc % 2]
            for bi, (c0, n) in enumerate(blocks):
                ps = pz[cnt1["it"] % 2]
                cnt1["it"] += 1
                for kk in range(8):
                    P.op("pe", I("matmul", ps.ap[:, 0:n], lhsT=W.ap[:, kk, pc * 128:(pc + 1) * 128], rhs=uT.ap[:, kk, c0:c0 + n],
                                 start=(kk == 0), stop=(kk == 7)), reads=[W.b, uT.b], writes=[ps.b])
                r0 = rawcol(c0)
                P.op("act", I("activation", out=raw.ap[:, r0:r0 + n], in_=ps.ap[:, 0:n], func=AF.Copy),
                     reads=[ps.b], writes=[raw.b])

        front(0)
        for pc in range(12):
            raw, acc = raws[pc % 2], accs[pc % 2]
            if pc + 1 < 12:
                front(pc + 1)
            for (a0, r0, L) in ((0, 0, CTX), (CTX, CTX + 4, SEQ)):
                P.op("dve", I("tensor_scalar", out=acc.ap[:, a0:a0 + L], in0=raw.ap[:, r0:r0 + L], scalar1=cw.ap[:, pc, 0:1], scalar2=None, op0=ALU.mult),
                     reads=[raw.b, cw.b], writes=[acc.b])
                for j in range(1, 5):
                    P.op("dve", I("scalar_tensor_tensor", out=acc.ap[:, a0:a0 + L], in0=raw.ap[:, r0 + j:r0 + j + L], scalar=cw.ap[:, pc, j:j + 1],
                                  in1=acc.ap[:, a0:a0 + L], op0=ALU.mult, op1=ALU.add), reads=[raw.b, cw.b, acc.b], writes=[acc.b])
            P.op("act", I("activation", out=acc.ap, in_=acc.ap, func=AF.Silu), reads=[acc.b], writes=[acc.b])
            if pc < 8:
                for bi, (c0, n) in enumerate(blocks):
                    s2, r2, ps = sq[bi % 2], rn[bi % 2], pss[bi % 2]
                    P.op("pool", I("tensor_tensor", out=s2.ap[:, 0:n], in0=acc.ap[:, c0:c0 + n], in1=acc.ap[:, c0:c0 + n], op=ALU.mult),
                         reads=[acc.b], writes=[s2.b])
                    P.op("pe", I("matmul", ps.ap[:, 0:n], lhsT=bones.ap, rhs=s2.ap[:, 0:n], start=True, stop=True), reads=[bones.b, s2.b], writes=[ps.b])
                    P.op("act", I("activation", out=r2.ap[:, 0:n], in_=ps.ap[:, 0:n], func=AF.Ln, bias=EPS), reads=[ps.b], writes=[r2.b])
                    P.op("act", I("activation", out=r2.ap[:, 0:n], in_=r2.ap[:, 0:n], func=AF.Exp, scale=-0.5), reads=[r2.b], writes=[r2.b])
                    scl = 0.125 if pc < 4 else 1.0
                    P.op("dve", I("scalar_tensor_tensor", out=acc.ap[:, c0:c0 + n], in0=acc.ap[:, c0:c0 + n], scalar=scl, in1=r2.ap[:, 0:n],
                                  op0=ALU.mult, op1=ALU.mult), reads=[acc.b, r2.b], writes=[acc.b])
                P.dma("sp", qkT.ap[2 * pc:2 * pc + 2].rearrange("h p t -> (h p) t"), acc.ap, reads=[acc.b], writes=[qkT.b])
            if pc >= 4:
                dst = ktok if pc < 8 else vtok
                p2 = pc % 4
                dstv = dst.ap[:, p2 * 128:(p2 + 1) * 128].rearrange("(n p) d -> p n d", p=128)
                for g0 in range(0, NT, 4):
                    ps = ptr[(g0 // 4) % 2]
                    stage = stages[(g0 // 4) % 2]
                    ng = min(4, NT - g0)
                    for i in range(ng):
                        n_ = g0 + i
                        P.op("pe", I("transpose", out=ps.ap[:, i * 128:(i + 1) * 128], in_=acc.ap[:, n_ * 128:(n_ + 1) * 128], identity=identf.ap),
                             reads=[acc.b, identf.b], writes=[ps.b])
                    P.op("act", I("activation", out=stage.ap[:, 0:ng, :], in_=ps.ap[:, 0:ng * 128].rearrange("p (a b) -> p a b", a=ng), func=AF.Copy),
                         reads=[ps.b], writes=[stage.b])
                    P.dma("act", dstv[:, g0:g0 + ng, :], stage.ap[:, 0:ng, :], reads=[stage.b], writes=[dst.b])
        if sub == "b":
            return
        P.barrier()
        k.reset(mark)
        dtb = k.sb(16, F32, "dtb")
        negA = k.sb(16, F32, "negA")
        P.dma("sp", dtb.ap, d["gdn_dt_bias"][0].rearrange("a b -> (a b)").partition_broadcast(128), writes=[dtb.b])
        P.dma("sp", negA.ap, d["gdn_a_log"][0].rearrange("a b -> (a b)").partition_broadcast(128), writes=[negA.b])
        P.op("act", I("activation", out=negA.ap, in_=negA.ap, func=AF.Exp), reads=[negA.b], writes=[negA.b])
        P.op("dve", I("tensor_scalar", out=negA.ap, in0=negA.ap, scalar1=-1.0, scalar2=None, op0=ALU.mult), reads=[negA.b], writes=[negA.b])
        gts = [k.sb(512, F32, "gt%d" % i) for i in range(2)]
        gbs = k.sb([NT, 32], F32, "gbs")
        tmp16 = [k.sb(16, F32, "tmp16%d" % i) for i in range(2)]
        pg = [k.ps(1), k.ps(2)]
        pab = [k.ps(3), k.ps(4)]
        gate_d, gb_d = sc["gate_d"], sc["gb_d"]
        for n in range(NT):
            p1, p2, gt, t16 = pg[n % 2], pab[n % 2], gts[n % 2], tmp16[n % 2]
            for kk in range(8):
                P.op("pe", I("matmul", p1.ap, lhsT=uT.ap[:, kk, n * 128:(n + 1) * 128], rhs=W.ap[:, kk, 1536:2048],
                                                                 start=(kk == 0), stop=(kk == 7)), reads=[W.b, uT.b], writes=[p1.b])
            for kk in range(8):
                P.op("pe", I("matmul", p2.ap[:, 0:32], lhsT=uT.ap[:, kk, n * 128:(n + 1) * 128], rhs=W.ap[:, kk, 2048:2080],
                                                                 start=(kk == 0), stop=(kk == 7)), reads=[W.b, uT.b], writes=[p2.b])
            P.op("act", I("activation", out=gt.ap, in_=p1.ap, func=AF.Silu), reads=[p1.b], writes=[gt.b])
            P.dma("sp", gate_d.ap[n * 128:(n + 1) * 128, :], gt.ap, reads=[gt.b], writes=[gate_d.b])
            P.op("dve", I("tensor_tensor", out=t16.ap, in0=p2.ap[:, 0:16], in1=dtb.ap, op=ALU.add),
                 reads=[p2.b, dtb.b], writes=[t16.b])
            P.op("act", I("activation", out=t16.ap, in_=t16.ap, func=AF.Exp), reads=[t16.b], writes=[t16.b])
            P.op("act", I("activation", out=t16.ap, in_=t16.ap, func=AF.Ln, bias=1.0), reads=[t16.b], writes=[t16.b])
            P.op("dve", I("tensor_tensor", out=gbs.ap[:, n, 0:16], in0=t16.ap, in1=negA.ap, op=ALU.mult),
                 reads=[t16.b, negA.b], writes=[gbs.b])
            P.op("act", I("activation", out=gbs.ap[:, n, 16:32], in_=p2.ap[:, 16:32], func=AF.Sigmoid),
                 reads=[p2.b], writes=[gbs.b])
        P.dma("sp", gb_d.ap.rearrange("(n p) d -> p n d", p=128), gbs.ap, reads=[gbs.b], writes=[gb_d.b])
        if sub == "c":
            return
        P.barrier()
        k.reset(mark)
        qg = k.sb(2, F32, "qg")
        kvg = k.sb(1, F32, "kvg")
        graw = k.sb(128, F32, "graw")
        P.dma("sp", graw.ap[0:2, :], d["mla_q_norm"][0].rearrange("(k p) -> k p", p=128), writes=[graw.b])
        P.dma("sp", graw.ap[2:3, :], d["mla_kv_norm"][0].rearrange("(k p) -> k p", p=128), writes=[graw.b])
        pgr = k.ps(7)
        P.op("pe", I("transpose", pgr.ap[:, 0:3], graw.ap[0:3, :], identf.ap[0:3, 0:3]), reads=[graw.b, identf.b], writes=[pgr.b])
        P.op("dve", I("tensor_copy", out=qg.ap, in_=pgr.ap[:, 0:2]), reads=[pgr.b], writes=[qg.b])
        P.op("dve", I("tensor_copy", out=kvg.ap, in_=pgr.ap[:, 2:3]), reads=[pgr.b], writes=[kvg.b])
        ropes = [k.sb([2, 512], F32, "rope32_%d" % i) for i in range(2)]
        zq = [k.sb([3, 512], F32, "zq%d" % i) for i in range(2)]
        sq3 = [k.sb([3, 512], F32, "sq3%d" % i) for i in range(2)]
        rs = [k.sb([2, 512], F32, "rs%d" % i) for i in range(2)]
        qo = [k.sb([3, 512], BF16, "qo%d" % i) for i in range(2)]
        kp = [k.sb([3, 512], F32, "kp%d" % i) for i in range(2)]
        kpo = [k.sb(512, BF16, "kpo%d" % i) for i in range(2)]
        pz3 = [k.ps(0), k.ps(1), k.ps(2)]
        pss2 = [k.ps(3), k.ps(4)]
        ppe = [k.ps(5), k.ps(6)]
        qn_d, kvn_d, kpe_d = sc["qn_d"], sc["kvn_d"], sc["kpe_d"]
        cols = [2080, 2208, 2336]
        for bi, (c0, n) in enumerate(blocks):
            z, s3, r, o = zq[bi % 2], sq3[bi % 2], rs[bi % 2], qo[bi % 2]
            for ci in range(3):
                ps = pz3[ci]
                for kk in range(8):
                    P.op("pe", I("matmul", ps.ap[:, 0:n], lhsT=W.ap[:, kk, cols[ci]:cols[ci] + 128], rhs=uT.ap[:, kk, c0:c0 + n],
                        start=(kk == 0), stop=(kk == 7)), reads=[W.b, uT.b], writes=[ps.b])
                P.op("act", I("activation", out=z.ap[:, ci, 0:n], in_=ps.ap[:, 0:n], func=AF.Copy),
                     reads=[ps.b], writes=[z.b])
                P.op("dve", I("tensor_tensor", out=s3.ap[:, ci, 0:n], in0=ps.ap[:, 0:n], in1=z.ap[:, ci, 0:n], op=ALU.mult),
                     reads=[ps.b, z.b], writes=[s3.b])
            p_q, p_kv = pss2[0], pss2[1]
            for ci in range(2):
                P.op("pe", I("matmul", p_q.ap[:, 0:n], lhsT=ones.ap, rhs=s3.ap[:, ci, 0:n], start=(ci == 0), stop=(ci == 1)),
                     reads=[ones.b, s3.b], writes=[p_q.b])
            P.op("pe", I("matmul", p_kv.ap[:, 0:n], lhsT=ones.ap, rhs=s3.ap[:, 2, 0:n], start=True, stop=True),
                 reads=[ones.b, s3.b], writes=[p_kv.b])
            P.op("act", I("activation", out=r.ap[:, 0, 0:n], in_=p_q.ap[:, 0:n], func=AF.Ln, scale=1.0 / 256, bias=EPS),
                 reads=[p_q.b], writes=[r.b])
            P.op("act", I("activation", out=r.ap[:, 1, 0:n], in_=p_kv.ap[:, 0:n], func=AF.Ln, scale=1.0 / 128, bias=EPS),
                 reads=[p_kv.b], writes=[r.b])
            P.op("act", I("activation", out=r.ap[:, :, 0:n], in_=r.ap[:, :, 0:n], func=AF.Exp, scale=-0.5), reads=[r.b], writes=[r.b])
            for ci in range(3):
                gcol = qg.ap[:, ci:ci + 1] if ci < 2 else kvg.ap[:, 0:1]
                gb_ = qg.b if ci < 2 else kvg.b
                P.op("dve", I("scalar_tensor_tensor", out=o.ap[:, ci, 0:n], in0=z.ap[:, ci, 0:n], scalar=gcol, in1=r.ap[:, 0 if ci < 2 else 1, 0:n],
                    op0=ALU.mult, op1=ALU.mult), reads=[z.b, r.b, gb_], writes=[o.b])
            P.dma("sp", qn_d.ap[0, :, c0:c0 + n], o.ap[:, 0, 0:n], reads=[o.b], writes=[qn_d.b])
            P.dma("sp", qn_d.ap[1, :, c0:c0 + n], o.ap[:, 1, 0:n], reads=[o.b], writes=[qn_d.b])
            P.dma("sp", kvn_d.ap[:, c0:c0 + n], o.ap[:, 2, 0:n], reads=[o.b], writes=[kvn_d.b])
            rope = ropes[bi % 2]
            P.dma("sp", rope.ap[0:32, :, 0:n], d["k_rope32"][:, :, c0:c0 + n].rearrange("a p n -> p a n"), writes=[rope.b])
            p_pe, p_rot = ppe[0], ppe[1]
            kq, ko = kp[bi % 2], kpo[bi % 2]
            for kk in range(8):
                P.op("pe", I("matmul", p_pe.ap[0:32, 0:n], lhsT=W.ap[:, kk, 2464:2496], rhs=uT.ap[:, kk, c0:c0 + n],
                                                                 start=(kk == 0), stop=(kk == 7)), reads=[W.b, uT.b], writes=[p_pe.b])
            for kk in range(8):
                P.op("pe", I("matmul", p_rot.ap[0:32, 0:n], lhsT=Wrot.ap[:, kk, :], rhs=uT.ap[:, kk, c0:c0 + n],
                                                                 start=(kk == 0), stop=(kk == 7)), reads=[Wrot.b, uT.b], writes=[p_rot.b])
            P.op("dve", I("tensor_tensor", out=kq.ap[0:32, 0, 0:n], in0=p_pe.ap[0:32, 0:n], in1=rope.ap[0:32, 0, 0:n], op=ALU.mult),
                 reads=[p_pe.b, rope.b], writes=[kq.b])
            P.op("dve", I("tensor_tensor", out=kq.ap[0:32, 1, 0:n], in0=p_rot.ap[0:32, 0:n], in1=rope.ap[0:32, 1, 0:n], op=ALU.mult),
                 reads=[p_rot.b, rope.b], writes=[kq.b])
            P.op("pool", I("tensor_tensor", out=ko.ap[0:32, 0:n], in0=kq.ap[0:32, 0, 0:n], in1=kq.ap[0:32, 1, 0:n], op=ALU.add),
                 reads=[kq.b], writes=[ko.b])
            P.dma("sp", kpe_d.ap[:, c0:c0 + n], ko.ap[0:32, 0:n], reads=[ko.b], writes=[kpe_d.b])
        if "dbg_z" in self.dbg:
            dz = self.scratch("dbg_z", [128, 3 * 512]); ds = self.scratch("dbg_s", [128, 3 * 512]); dr = self.scratch("dbg_r", [128, 2 * 512])
            P.dma("sp", dz.ap, zq[1].ap.rearrange("p a b -> p (a b)"), reads=[zq[1].b], writes=[dz.b])
            P.dma("sp", ds.ap, sq3[1].ap.rearrange("p a b -> p (a b)"), reads=[sq3[1].b], writes=[ds.b])
            P.dma("sp", dr.ap, rs[1].ap.rearrange("p a b -> p (a b)"), reads=[rs[1].b], writes=[dr.b])

    def s_gdn(self):
        k, P, d, T, NT = self.k, self.P, self.din, self.T, self.NT
        sc = self.dsc
        ident = k.sb(128, F32, "ident"); ones = k.sb(128, F32, "ones")
        P.dma("sp", ident.ap, d["k_identf"], writes=[ident.b])
        identb = k.sb(128, BF16, "identb")
        P.dma("sp", identb.ap, d["k_identb"], writes=[identb.b])
        P.op("pool", I("memset", ones.ap, 1.0), writes=[ones.b])
        TRI = [k.sb(128, F32, "tri%d" % i) for i in range(2)]
        MS = [k.sb(128, F32, "ms%d" % i) for i in range(2)]
        MI = [k.sb(128, F32, "mi%d" % i) for i in range(2)]
        for dr in range(2):
            P.dma("sp", TRI[dr].ap, d["k_tril"][dr], writes=[TRI[dr].b])
            P.dma("sp", MS[dr].ap, d["k_mstrict"][dr], writes=[MS[dr].b])
            P.op("dve", I("tensor_tensor", out=MI[dr].ap, in0=MS[dr].ap, in1=ident.ap, op=ALU.add), reads=[MS[dr].b, ident.b], writes=[MI[dr].b])
        GM = [k.sb(128, F32, "gm%d" % i) for i in range(7)]
        for i in range(7):
            P.dma("sp", GM[i].ap, d["k_gmask"][i], writes=[GM[i].b])
        GMb = [k.sb(128, BF16, "gmb%d" % i) for i in range(7)]
        for i in range(7):
            P.op("dve", I("tensor_copy", out=GMb[i].ap, in_=GM[i].ap), reads=[GM[i].b], writes=[GMb[i].b])
        pairs = [Tl(k.psum[:, i * 1024:(i + 1) * 1024]) for i in range(3)]
        singles = [Tl(k.psum[:, 3072 + i * 512:3072 + (i + 1) * 512]) for i in range(2)]
        rr = {"p": 0, "s": 0}

        def pair():
            rr["p"] += 1
            return pairs[rr["p"] % 3]

        def single():
            rr["s"] += 1
            return singles[rr["s"] % 2]

        def v3(t, a=8):
            return t.ap.rearrange("p (a b) -> p a b", a=a)

        def v3b(t):
            return t.ap.bitcast(BF16)[:, 0:1024].rearrange("p (a b) -> p a b", a=8)

        L = {}
        for dr in range(2):
            for par in range(2):
                L[dr, par] = dict(
                    ktok=k.sb([8, 64], F32), vtok=k.sb([8, 64], F32),
                    gb=k.sb(32, F32), QKmT=k.sb([8, 128], BF16), Us=k.sb([8, 64], F32), WTs=k.sb([8, 128], BF16),
                    Kd=k.sb([8, 64], BF16), egc=k.sb(8, F32), egl=k.sb(8, F32), qTb=k.sb([8, 128], BF16), kTb=k.sb([8, 128], BF16))
        SCR = []
        for dr in range(2):
            SCR.append(dict(
                gct=k.sb(16, F32), kdsc=k.sb(8, F32), begc=k.sb(8, F32),
                G2=k.sb([8, 128], F32), tD=k.sb([8, 128], F32), Dms=k.sb([8, 128], F32), Dmi=k.sb([8, 128], F32),
                tmpM=k.sb([8, 128], F32), QKm=k.sb([8, 128], BF16),
                Nb=[k.sb([8, 128], BF16) for _ in range(2)], NTb=[k.sb([8, 128], BF16) for _ in range(2)],
                R=k.sb([8, 128], BF16), Kbg=k.sb([8, 64], BF16), Vb=k.sb([8, 64], BF16),
                X=k.sb([8, 128], BF16), Mf=k.sb([8, 128], BF16), MTf=k.sb([8, 128], BF16)))
        S = [k.sb([8, 64], F32) for _ in range(2)]; S2 = [k.sb([8, 64], F32) for _ in range(2)]
        vnew = [k.sb([8, 64], BF16) for _ in range(2)]; tq = [k.sb([8, 64], F32) for _ in range(2)]
        Sb = [k.sb([8, 64], BF16) for _ in range(2)]
        ot = [[k.sb([8, 64], F32) for _ in range(2)] for _ in range(2)]
        for dr in range(2):
            P.op("pool", I("memset", S[dr].ap, 0.0), writes=[S[dr].b])
            P.op("pool", I("memset", Sb[dr].ap, 0.0), writes=[Sb[dr].b])
        qk_d, ktok_d, vtok_d, gb_d, o_d = sc["qkT_d"], sc["ktok_d"], sc["vtok_d"], sc["gb_d"], sc["o_d"]
        qv = qk_d.ap[0:8].rearrange("h p t -> p h t")
        kv = qk_d.ap[8:16].rearrange("h p t -> p h t")
        order = [list(range(NT)), [1, 0] + list(range(NT - 1, 1, -1))]

        def load(dr, par, n):
            b = L[dr, par]
            cs = slice(n * 128, (n + 1) * 128)
            P.dma("pool", b["qTb"].ap[0:64], qv[:, :, cs], reads=[qk_d.b], writes=[b["qTb"].b])
            P.dma("pool", b["kTb"].ap[0:64], kv[:, :, cs], reads=[qk_d.b], writes=[b["kTb"].b])
            P.dma("sp", b["ktok"].ap, ktok_d.ap[cs, :].rearrange("p (a b) -> p a b", a=8), reads=[ktok_d.b], writes=[b["ktok"].b])
            P.dma("sp", b["vtok"].ap, vtok_d.ap[cs, :].rearrange("p (a b) -> p a b", a=8), reads=[vtok_d.b], writes=[b["vtok"].b])
            P.dma("sp", b["gb"].ap, gb_d.ap[cs, :], reads=[gb_d.b], writes=[b["gb"].b])

        import os
        GSUB = os.environ.get("GSUB", "")

        def prep(dr, par):
            b = L[dr, par]
            sc_ = SCR[dr]
            gct, kdsc, begc, G2, tD, Dms, Dmi, tmpM, QKm = (sc_[n_] for n_ in ("gct", "kdsc", "begc", "G2", "tD", "Dms", "Dmi", "tmpM", "QKm"))
            Nb, NTb, R, Kbg, Vb, X, Mf, MTf = (sc_[n_] for n_ in ("Nb", "NTb", "R", "Kbg", "Vb", "X", "Mf", "MTf"))
            g8 = b["gb"].ap[:, dr * 8:dr * 8 + 8]
            be8 = b["gb"].ap[:, 16 + dr * 8:16 + dr * 8 + 8]
            gbb = b["gb"].b
            p1 = single()
            P.op("pe", I("matmul", p1.ap[:, 0:8], lhsT=TRI[dr].ap, rhs=g8, start=True, stop=True), reads=[TRI[dr].b, gbb], writes=[p1.b])
            P.op("pe", I("matmul", p1.ap[:, 8:16], lhsT=ones.ap, rhs=g8, start=True, stop=True), reads=[ones.b, gbb], writes=[p1.b])
            P.op("dve", I("tensor_copy", out=gct.ap, in_=p1.ap[:, 0:16]), reads=[p1.b], writes=[gct.b])
            P.op("act", I("activation", out=b["egc"].ap, in_=gct.ap[:, 0:8], func=AF.Exp), reads=[gct.b], writes=[b["egc"].b])
            P.op("act", I("activation", out=b["egl"].ap, in_=gct.ap[:, 8:16], func=AF.Exp), reads=[gct.b], writes=[b["egl"].b])
            P.op("dve", I("tensor_tensor", out=kdsc.ap, in0=gct.ap[:, 8:16], in1=gct.ap[:, 0:8], op=ALU.subtract), reads=[gct.b], writes=[kdsc.b])
            P.op("act", I("activation", out=kdsc.ap, in_=kdsc.ap, func=AF.Exp), reads=[kdsc.b], writes=[kdsc.b])
            P.op("dve", I("tensor_tensor", out=begc.ap, in0=be8, in1=b["egc"].ap, op=ALU.mult), reads=[gbb, b["egc"].b], writes=[begc.b])
            if GSUB == "1":
                return
            yield
            P.op("dve", I("tensor_tensor", out=G2.ap, in0=bcast_head(TRI[dr].ap, 8), in1=bcast_mid(g8, 128), op=ALU.mult),
                 reads=[TRI[dr].b, gbb], writes=[G2.b])
            pg = pair()
            for h in range(8):
                P.op("pe", I("matmul", pg.ap[:, h * 128:(h + 1) * 128], lhsT=ones.ap, rhs=G2.ap[:, h, :], start=True, stop=True),
                     reads=[ones.b, G2.b], writes=[pg.b])
            if GSUB == "2":
                return
            yield
            P.op("dve", I("tensor_tensor", out=tD.ap, in0=v3(pg), in1=bcast_mid(gct.ap[:, 0:8], 128), op=ALU.subtract),
                 reads=[pg.b, gct.b], writes=[tD.b])
            P.op("act", I("activation", out=tD.ap, in_=tD.ap, func=AF.Relu), reads=[tD.b], writes=[tD.b])
            P.op("act", I("activation", out=tD.ap, in_=tD.ap, func=AF.Exp, scale=-1.0), reads=[tD.b], writes=[tD.b])
            P.op("pool", I("tensor_tensor", out=Dms.ap, in0=tD.ap, in1=bcast_head(MS[dr].ap, 8), op=ALU.mult), reads=[tD.b, MS[dr].b], writes=[Dms.b])
            P.op("pool", I("tensor_tensor", out=Dmi.ap, in0=tD.ap, in1=bcast_head(MI[dr].ap, 8), op=ALU.mult), reads=[tD.b, MI[dr].b], writes=[Dmi.b])
            if GSUB == "3":
                return
            yield
            pG = pair()
            for h in range(8):
                P.op("pe", I("matmul", pG.ap[:, h * 128:(h + 1) * 128], lhsT=b["kTb"].ap[0:64, h, :], rhs=b["kTb"].ap[0:64, h, :], start=True, stop=True),
                     reads=[b["kTb"].b], writes=[pG.b])
            pA = pair()
            for h in range(8):
                P.op("pe", I("matmul", pA.ap[:, h * 128:(h + 1) * 128], lhsT=b["qTb"].ap[0:64, h, :], rhs=b["kTb"].ap[0:64, h, :], start=True, stop=True),
                     reads=[b["kTb"].b, b["qTb"].b], writes=[pA.b])
            P.op("dve", I("tensor_tensor", out=tmpM.ap, in0=v3(pG), in1=bcast_mid(be8, 128), op=ALU.mult), reads=[pG.b, gbb], writes=[tmpM.b])
            P.op("pool", I("tensor_tensor", out=Mf.ap, in0=tmpM.ap, in1=Dms.ap, op=ALU.mult), reads=[tmpM.b, Dms.b], writes=[Mf.b])
            P.op("dve", I("tensor_tensor", out=QKm.ap, in0=v3(pA), in1=Dmi.ap, op=ALU.mult), reads=[pA.b, Dmi.b], writes=[QKm.b])
            if GSUB == "4":
                return
            yield
            pT = pair()
            for h in range(8):
                P.op("pe", I("transpose", out=pT.ap.bitcast(BF16)[:, h * 128:(h + 1) * 128], in_=Mf.ap[:, h, :], identity=identb.ap), reads=[Mf.b, identb.b], writes=[pT.b])
            P.op("act", I("activation", out=MTf.ap, in_=v3b(pT), func=AF.Copy), reads=[pT.b], writes=[MTf.b])
            pQ = pair()
            for h in range(8):
                P.op("pe", I("transpose", out=pQ.ap.bitcast(BF16)[:, h * 128:(h + 1) * 128], in_=QKm.ap[:, h, :], identity=identb.ap), reads=[QKm.b, identb.b], writes=[pQ.b])
            P.op("act", I("activation", out=b["QKmT"].ap, in_=v3b(pQ), func=AF.Copy), reads=[pQ.b], writes=[b["QKmT"].b])
            if GSUB == "5":
                return
            yield
            def mm8(lhs, rhs):
                pp = pair()
                for h in range(8):
                    P.op("pe", I("matmul", pp.ap[:, h * 128:(h + 1) * 128], lhsT=lhs.ap[:, h, :], rhs=rhs.ap[:, h, :], start=True, stop=True),
                         reads=[lhs.b, rhs.b], writes=[pp.b])
                return pp

            P.op("dve", I("tensor_tensor", out=Nb[0].ap, in0=Mf.ap, in1=bcast_head(GMb[0].ap, 8), op=ALU.mult), reads=[Mf.b, GMb[0].b], writes=[Nb[0].b])
            P.op("dve", I("tensor_tensor", out=NTb[0].ap, in0=MTf.ap, in1=bcast_head(GMb[0].ap, 8), op=ALU.mult), reads=[MTf.b, GMb[0].b], writes=[NTb[0].b])
            P.op("dve", I("tensor_tensor", out=R.ap, in0=bcast_head(identb.ap, 8), in1=NTb[0].ap, op=ALU.subtract), reads=[NTb[0].b, identb.b], writes=[R.b])
            cur = 0
            for j in range(1, 4):
                Nc, NTc, Nn, NTn = Nb[cur], NTb[cur], Nb[1 - cur], NTb[1 - cur]
                px = mm8(NTc, Nc)
                P.op("act", I("activation", out=Nn.ap, in_=v3(px), func=AF.Copy), reads=[px.b], writes=[Nn.b])
                yield
                if j < 3:
                    py = mm8(Nc, NTc)
                    P.op("act", I("activation", out=NTn.ap, in_=v3(py), func=AF.Copy), reads=[py.b], writes=[NTn.b])
                    yield
                pz = mm8(Nn, R)
                P.op("dve", I("tensor_tensor", out=R.ap, in0=v3(pz), in1=R.ap, op=ALU.add), reads=[pz.b, R.b], writes=[R.b])
                cur = 1 - cur
                yield
            for li in range(3):
                C, Z1 = Nb[0], Nb[1]
                mC = GMb[1 + dr * 3 + li]
                ptx = pair()
                for h in range(8):
                    P.op("pe", I("transpose", out=ptx.ap.bitcast(BF16)[:, h * 128:(h + 1) * 128], in_=R.ap[:, h, :], identity=identb.ap),
                         reads=[R.b, identb.b], writes=[ptx.b])
                P.op("act", I("activation", out=X.ap, in_=v3b(ptx), func=AF.Copy), reads=[ptx.b], writes=[X.b])
                P.op("dve", I("tensor_tensor", out=C.ap, in0=Mf.ap, in1=bcast_head(mC.ap, 8), op=ALU.mult), reads=[Mf.b, mC.b], writes=[C.b])
                yield
                p1_ = mm8(C, R)
                P.op("act", I("activation", out=Z1.ap, in_=v3(p1_), func=AF.Copy), reads=[p1_.b], writes=[Z1.b])
                yield
                p3_ = mm8(X, Z1)
                P.op("dve", I("tensor_tensor", out=R.ap, in0=R.ap, in1=v3(p3_), op=ALU.subtract), reads=[p3_.b, R.b], writes=[R.b])
                yield
            if GSUB == "6":
                return
            yield
            P.op("pool", I("tensor_tensor", out=Vb.ap, in0=b["vtok"].ap, in1=bcast_mid(be8, 64), op=ALU.mult), reads=[b["vtok"].b, gbb], writes=[Vb.b])
            P.op("pool", I("tensor_tensor", out=Kbg.ap, in0=b["ktok"].ap, in1=bcast_mid(begc.ap, 64), op=ALU.mult), reads=[b["ktok"].b, begc.b], writes=[Kbg.b])
            P.op("pool", I("tensor_tensor", out=b["Kd"].ap, in0=b["ktok"].ap, in1=bcast_mid(kdsc.ap, 64), op=ALU.mult), reads=[b["ktok"].b, kdsc.b], writes=[b["Kd"].b])
            pU = single()
            for h in range(8):
                P.op("pe", I("matmul", pU.ap[:, h * 64:(h + 1) * 64], lhsT=R.ap[:, h, :], rhs=Vb.ap[:, h, :], start=True, stop=True),
                     reads=[R.b, Vb.b], writes=[pU.b])
            P.op("act", I("activation", out=b["Us"].ap, in_=v3(pU), func=AF.Copy), reads=[pU.b], writes=[b["Us"].b])
            pW = pair()
            for h in range(8):
                P.op("pe", I("matmul", pW.ap[0:64, h * 128:(h + 1) * 128], lhsT=Kbg.ap[:, h, :], rhs=R.ap[:, h, :], start=True, stop=True),
                     reads=[R.b, Kbg.b], writes=[pW.b])
            P.op("act", I("activation", out=b["WTs"].ap[0:64], in_=pW.ap[0:64, :].rearrange("p (a b) -> p a b", a=8), func=AF.Copy),
                 reads=[pW.b], writes=[b["WTs"].b])

        def step(dr, par, n, i):
            b = L[dr, par]
            Sd, S2d, vn, tqd, o = S[dr], S2[dr], vnew[dr], tq[dr], ot[dr][i % 2]
            pa, pq = single(), pair()
            for h in range(8):
                P.op("pe", I("matmul", pa.ap[:, h * 64:(h + 1) * 64], lhsT=b["WTs"].ap[0:64, h, :], rhs=Sb[dr].ap[0:64, h, :], start=True, stop=True),
                     reads=[b["WTs"].b, Sb[dr].b], writes=[pa.b])
            for h in range(8):
                P.op("pe", I("matmul", pq.ap[:, h * 64:(h + 1) * 64], lhsT=b["qTb"].ap[0:64, h, :], rhs=Sb[dr].ap[0:64, h, :], start=True, stop=True),
                     reads=[b["qTb"].b, Sb[dr].b], writes=[pq.b])
            P.op("dve", I("tensor_tensor", out=vn.ap, in0=b["Us"].ap, in1=pa.ap.rearrange("p (a b) -> p a b", a=8), op=ALU.subtract),
                 reads=[b["Us"].b, pa.b], writes=[vn.b])
            P.op("dve", I("tensor_tensor", out=tqd.ap, in0=pq.ap[:, 0:512].rearrange("p (a b) -> p a b", a=8), in1=bcast_mid(b["egc"].ap, 64), op=ALU.mult),
                 reads=[pq.b, b["egc"].b], writes=[tqd.b])
            yield
            po, ps_ = single(), pair()
            for h in range(8):
                P.op("pe", I("matmul", po.ap[:, h * 64:(h + 1) * 64], lhsT=b["QKmT"].ap[:, h, :], rhs=vn.ap[:, h, :], start=True, stop=True),
                     reads=[b["QKmT"].b, vn.b], writes=[po.b])
            for h in range(8):
                P.op("pe", I("matmul", ps_.ap[0:64, h * 64:(h + 1) * 64], lhsT=b["Kd"].ap[:, h, :], rhs=vn.ap[:, h, :], start=True, stop=True),
                     reads=[b["Kd"].b, vn.b], writes=[ps_.b])
            P.op("dve", I("tensor_tensor", out=o.ap, in0=tqd.ap, in1=po.ap.rearrange("p (a b) -> p a b", a=8), op=ALU.add),
                 reads=[tqd.b, po.b], writes=[o.b])
            P.dma("sp", o_d.ap[dr, n * 128:(n + 1) * 128, :], o.ap.rearrange("p a b -> p (a b)"), reads=[o.b], writes=[o_d.b])
            P.op("pool", I("tensor_tensor", out=S2d.ap[0:64], in0=Sd.ap[0:64], in1=bcast_mid(b["egl"].ap[0:64], 64), op=ALU.mult),
                 reads=[Sd.b, b["egl"].b], writes=[S2d.b])
            P.op("dve", I("tensor_tensor", out=Sd.ap[0:64], in0=S2d.ap[0:64], in1=ps_.ap[0:64, 0:512].rearrange("p (a b) -> p a b", a=8), op=ALU.add),
                 reads=[S2d.b, ps_.b], writes=[Sd.b])
            P.op("act", I("activation", out=Sb[dr].ap[0:64], in_=Sd.ap[0:64], func=AF.Copy), reads=[Sd.b], writes=[Sb[dr].b])

        for dr in range(2):
            load(dr, 0, order[dr][0])
        pending = []
        for i in range(NT + 1):
            gens = list(pending)
            pending = []
            if i < NT:
                for dr in range(2):
                    gens.append(prep(dr, i % 2))
            while gens:
                for g_ in list(gens):
                    try:
                        next(g_)
                    except StopIteration:
                        gens.remove(g_)
            if i < NT:
                if i + 1 < NT:
                    for dr in range(2):
                        load(dr, (i + 1) % 2, order[dr][i + 1])
                if not GSUB:
                    for dr in range(2):
                        pending.append(step(dr, i % 2, order[dr][i], i))

    def s_gdn_out(self):
        k, P, d, T, NT = self.k, self.P, self.din, self.T, self.NT
        sc = self.dsc
        self.ffn_weights_issue(0, self.ffn_weights(0))
        identb = k.sb(128, BF16, "identb")
        P.dma("sp", identb.ap, d["k_identb"], writes=[identb.b])
        gn = k.sb(64, F32, "gn")
        P.dma("sp", gn.ap, d["gdn_out_norm"][0].partition_broadcast(128), writes=[gn.b])
        of = [k.sb([8, 64], F32) for _ in range(2)]; ob = [k.sb([8, 64], F32) for _ in range(2)]
        gt = [k.sb([8, 64], F32) for _ in range(2)]
        o = [k.sb([8, 64], F32) for _ in range(2)]; sq = k.sb([8, 64], F32)
        st = [k.sb(16, F32) for _ in range(2)]
        ab = [k.sb(512, BF16) for _ in range(2)]
        aT = [k.sb([4, 128], BF16) for _ in range(2)]
        pt = [k.ps(0), k.ps(1)]
        o_d, gate_d, mix = sc["o_d"], sc["gate_d"], sc["mixT_d"]
        mixv = mix.ap[0:4].rearrange("k p t -> p k t")
        for n in range(NT):
            i = n % 2
            cs = slice(n * 128, (n + 1) * 128)
            P.dma("sp", of[i].ap, o_d.ap[0, cs, :].rearrange("p (a b) -> p a b", a=8), reads=[o_d.b], writes=[of[i].b])
            P.dma("sp", ob[i].ap, o_d.ap[1, cs, :].rearrange("p (a b) -> p a b", a=8), reads=[o_d.b], writes=[ob[i].b])
            P.dma("sp", gt[i].ap, gate_d.ap[cs, :].rearrange("p (a b) -> p a b", a=8), reads=[gate_d.b], writes=[gt[i].b])
            P.op("dve", I("tensor_tensor", out=o[i].ap, in0=of[i].ap, in1=ob[i].ap, op=ALU.add), reads=[of[i].b, ob[i].b], writes=[o[i].b])
            P.op("dve", I("tensor_tensor", out=sq.ap, in0=o[i].ap, in1=o[i].ap, op=ALU.mult), reads=[o[i].b], writes=[sq.b])
            P.op("dve", I("tensor_reduce", out=st[i].ap[:, 0:8], in_=sq.ap, axis=AX.X, op=ALU.add), reads=[sq.b], writes=[st[i].b])
            P.op("act", I("activation", out=st[i].ap[:, 8:16], in_=st[i].ap[:, 0:8], func=AF.Sqrt, scale=1.0 / 64, bias=EPS), reads=[st[i].b], writes=[st[i].b])
            P.op("dve", I("reciprocal", out=st[i].ap[:, 8:16], in_=st[i].ap[:, 8:16]), reads=[st[i].b], writes=[st[i].b])
            P.op("dve", I("tensor_tensor", out=o[i].ap, in0=o[i].ap, in1=bcast_mid(st[i].ap[:, 8:16], 64), op=ALU.mult), reads=[o[i].b, st[i].b], writes=[o[i].b])
            P.op("dve", I("tensor_tensor", out=o[i].ap, in0=o[i].ap, in1=bcast_head(gn.ap, 8), op=ALU.mult), reads=[o[i].b, gn.b], writes=[o[i].b])
            P.op("dve", I("tensor_tensor", out=ab[i].ap.rearrange("p (a b) -> p a b", a=8), in0=o[i].ap, in1=gt[i].ap, op=ALU.mult),
                 reads=[o[i].b, gt[i].b], writes=[ab[i].b])
            psb = pt[i].ap.bitcast(BF16)
            for j in range(4):
                P.op("pe", I("transpose", out=psb[:, j * 128:(j + 1) * 128], in_=ab[i].ap[:, j * 128:(j + 1) * 128], identity=identb.ap),
                     reads=[ab[i].b, identb.b], writes=[pt[i].b])
            P.op("act", I("activation", out=aT[i].ap, in_=psb[:, 0:512].rearrange("p (a b) -> p a b", a=4), func=AF.Copy), reads=[pt[i].b], writes=[aT[i].b])
            P.dma("act", mixv[:, :, cs], aT[i].ap, reads=[aT[i].b], writes=[mix.b])

    def post_epilogue(self, ps, h, Gt, st, tmp, hn):
        P = self.P
        P.op("act", I("activation", out=tmp.ap, in_=ps.ap, func=AF.Square, accum_out=st.ap[:, 0:1]), reads=[ps.b], writes=[tmp.b, st.b])
        P.op("act", I("activation", out=st.ap[:, 1:2], in_=st.ap[:, 0:1], func=AF.Sqrt, scale=1.0 / D, bias=EPS), reads=[st.b], writes=[st.b])
        P.op("dve", I("reciprocal", out=st.ap[:, 2:3], in_=st.ap[:, 1:2]), reads=[st.b], writes=[st.b])
        P.op("dve", I("scalar_tensor_tensor", out=tmp.ap, in0=ps.ap, scalar=st.ap[:, 2:3], in1=Gt.ap, op0=ALU.mult, op1=ALU.mult),
             reads=[ps.b, st.b, Gt.b], writes=[tmp.b])
        P.op("pool", I("tensor_tensor", out=hn.ap, in0=tmp.ap, in1=h.ap, op=ALU.add), reads=[tmp.b, h.b], writes=[hn.b])

    def s_outproj(self, l, after_loads=None):
        k, P, d, T, NT = self.k, self.P, self.din, self.T, self.NT
        sc = self.dsc
        Gc, Gl, _ = self.load_gate_tiles(l, d["g_post_mix"], 2 * D)
        W = k.sb([8, D], BF16, "Wout")
        wsrc = (d["w_out_even"] if l == 0 else d["w_out_odd"])[0].rearrange("(k p) n -> p k n", p=128)
        for kk in range(8):
            P.dma("pool", W.ap[:, kk, :], wsrc[:, kk, :], writes=[W.b])
        if after_loads is not None:
            after_loads()
        mix = sc["mixT_d"]
        mixv = mix.ap.rearrange("k p t -> p k t")
        mt = [k.sb([8, 128], BF16) for _ in range(2)]
        h = [k.sb(D, F32) for _ in range(2)]; hn = [k.sb(D, F32) for _ in range(2)]
        tmp = k.sb(D, F32); st = [k.sb(8, F32) for _ in range(2)]
        pss = [Tl(k.psum[:, 0:1024]), Tl(k.psum[:, 1024:2048])]
        tiles = range(NT) if l == 0 else range(2, NT)
        hb = sc["hbuf_d"]
        for it, n in enumerate(tiles):
            i = it % 2
            cs = slice(n * 128, (n + 1) * 128)
            P.dma("sp", mt[i].ap, mixv[:, :, cs], reads=[mix.b], writes=[mt[i].b])
            src_ap, src_b = self.h_src(l, n)
            if l == 1:
                src_b = self.hb[n]
            P.dma("sp", h[i].ap, src_ap, reads=[src_b] if src_b else [], writes=[h[i].b])
            ps = pss[i]
            for half in range(2):
                for kk in range(8):
                    P.op("pe", I("matmul", ps.ap[:, half * 512:(half + 1) * 512], lhsT=mt[i].ap[:, kk, :], rhs=W.ap[:, kk, half * 512:(half + 1) * 512],
                                 start=(kk == 0), stop=(kk == 7)), reads=[mt[i].b, W.b], writes=[ps.b])
            self.post_epilogue(ps, h[i], Gc if n < 2 else Gl, st[i], tmp, hn[i])
            P.dma("pool", hb.ap[cs, :], hn[i].ap, reads=[hn[i].b], writes=[self.hb[n]])

    def ffn_weights(self, l):
        k = self.k
        return k.sb([8, DFF], BF16, "Wg"), k.sb([8, DFF], BF16, "Wu"), k.sb([22, D], BF16, "Wd")

    def ffn_weights_issue(self, l, Wts):
        P, d = self.P, self.din
        Wg, Wu, Wd = Wts
        wg = d["w_ffn_gate"][l].rearrange("(k p) n -> p k n", p=128)
        wu = d["w_ffn_up"][l].rearrange("(k p) n -> p k n", p=128)
        wd = d["w_ffn_down"][l].rearrange("(k p) n -> p k n", p=128)
        for kk in range(8):
            P.dma("pool", Wg.ap[:, kk, :], wg[:, kk, :], writes=[Wg.b])
            P.dma("pool", Wu.ap[:, kk, :], wu[:, kk, :], writes=[Wu.b])
        for kk in range(22):
            P.dma("pool", Wd.ap[:, kk, :], wd[:, kk, :], writes=[Wd.b])

    def s_outffn(self, l):
        k, P = self.k, self.P
        Wts = self.ffn_weights(l)
        mark = k.off
        self.s_outproj(l)
        P.barrier()
        k.reset(mark)
        self.s_ffn(l, Wts)

    def s_ffn(self, l, Wts):
        k, P, d, T, NT = self.k, self.P, self.din, self.T, self.NT
        sc = self.dsc
        Wg, Wu, Wd = Wts
        modv = sc["modv_d"]
        A = k.sb(D, F32, "fA"); B = k.sb(D, F32, "fB"); Gt = k.sb(D, F32, "fGt")
        t1 = k.sb(D, F32, "t1"); tmp = k.sb(D, F32, "tmp")

        def load_AB(r):
            P.dma("sp", t1.ap, d["g_pre_ffn"][l].partition_broadcast(128), writes=[t1.b])
            P.dma("sp", A.ap, modv.ap[l, r, 4 * D:5 * D].partition_broadcast(128), reads=[modv.b], writes=[A.b])
            P.dma("sp", B.ap, modv.ap[l, r, 3 * D:4 * D].partition_broadcast(128), reads=[modv.b], writes=[B.b])
            P.op("dve", I("scalar_tensor_tensor", out=A.ap, in0=A.ap, scalar=1.0, in1=t1.ap, op0=ALU.add, op1=ALU.mult), reads=[A.b, t1.b], writes=[A.b])

        def load_Gt(r):
            P.dma("sp", tmp.ap, d["g_post_ffn"][l].partition_broadcast(128), writes=[tmp.b])
            P.dma("sp", Gt.ap, modv.ap[l, r, 5 * D:6 * D].partition_broadcast(128), reads=[modv.b], writes=[Gt.b])
            P.op("dve", I("tensor_tensor", out=Gt.ap, in0=Gt.ap, in1=tmp.ap, op=ALU.mult), reads=[Gt.b, tmp.b], writes=[Gt.b])

        NB = 2
        identb = k.sb(128, BF16, "identb")
        P.dma("sp", identb.ap, d["k_identb"], writes=[identb.b])
        hblk = [[k.sb(D, F32) for _ in range(NB)] for _ in range(2)]
        junk = [k.sb(D, BF16) for _ in range(2)]
        st = [k.sb(8, F32) for _ in range(4)]
        u = [[k.sb(D, BF16) for _ in range(NB)] for _ in range(2)]
        uTblk = k.sb([8, NB * 128], BF16)
        a = k.sb([22, NB * 128], BF16)
        sg = [k.sb(NB * 128, F32) for _ in range(2)]
        ptr = [k.ps(6), k.ps(7)]
        pgs = [k.ps(4), k.ps(5)]
        pus = [k.ps(2), k.ps(3)]
        py = Tl(k.psum[:, 0:1024])
        hbd = sc["hbuf_d"]
        tiles = list(range(NT)) if l == 0 else list(range(2, NT))
        blks = [tiles[b0:b0 + NB] for b0 in range(0, len(tiles), NB)]
        state = {"ab": None, "gt": None, "it": 0}

        def norm_front(bix):
            blk = blks[bix]
            want = 1 if blk[0] < 2 else 0
            if want != state["ab"]:
                load_AB(want)
                state["ab"] = want
            for j, n in enumerate(blk):
                it = state["it"]
                state["it"] += 1
                h, s_, uu, jk = hblk[bix % 2][j], st[it % 2], u[bix % 2][j], junk[it % 2]
                P.dma("sp", h.ap, hbd.ap[n * 128:(n + 1) * 128, :], reads=[self.hb[n]], writes=[h.b])
                P.op("act", I("activation", out=jk.ap, in_=h.ap, func=AF.Square, accum_out=s_.ap[:, 0:1]), reads=[h.b], writes=[jk.b, s_.b])
                P.op("act", I("activation", out=s_.ap[:, 1:2], in_=s_.ap[:, 0:1], func=AF.Sqrt, scale=1.0 / D, bias=EPS), reads=[s_.b], writes=[s_.b])
                P.op("dve", I("reciprocal", out=s_.ap[:, 2:3], in_=s_.ap[:, 1:2]), reads=[s_.b], writes=[s_.b])
                P.op("dve", I("scalar_tensor_tensor", out=t1.ap, in0=h.ap, scalar=s_.ap[:, 2:3], in1=A.ap, op0=ALU.mult, op1=ALU.mult),
                     reads=[h.b, s_.b, A.b], writes=[t1.b])
                P.op("pool", I("tensor_tensor", out=uu.ap, in0=t1.ap, in1=B.ap, op=ALU.add), reads=[t1.b, B.b], writes=[uu.b])

        def transposes(bix):
            for j, n in enumerate(blks[bix]):
                uu, ps = u[bix % 2][j], ptr[j % 2]
                psb = ps.ap.bitcast(BF16)
                for kk in range(8):
                    P.op("pe", I("transpose", out=psb[:, kk * 128:(kk + 1) * 128], in_=uu.ap[:, kk * 128:(kk + 1) * 128], identity=identb.ap),
                         reads=[uu.b, identb.b], writes=[ps.b])
                P.op("act", I("activation", out=uTblk.ap[:, :, j * 128:(j + 1) * 128], in_=psb.rearrange("p (a b) -> p a b", a=8), func=AF.Copy), reads=[ps.b], writes=[uTblk.b])

        def gate_up(bix):
            nb = len(blks[bix])
            for ffc in range(22):
                pg, pu, sgt = pgs[ffc % 2], pus[ffc % 2], sg[ffc % 2]
                fs = slice(ffc * 128, (ffc + 1) * 128)
                for kk in range(8):
                    P.op("pe", I("matmul", pg.ap[:, 0:nb * 128], lhsT=Wg.ap[:, kk, fs], rhs=uTblk.ap[:, kk, 0:nb * 128], start=(kk == 0), stop=(kk == 7)),
                         reads=[Wg.b, uTblk.b], writes=[pg.b])
                for kk in range(8):
                    P.op("pe", I("matmul", pu.ap[:, 0:nb * 128], lhsT=Wu.ap[:, kk, fs], rhs=uTblk.ap[:, kk, 0:nb * 128], start=(kk == 0), stop=(kk == 7)),
                         reads=[Wu.b, uTblk.b], writes=[pu.b])
                P.op("act", I("activation", out=sgt.ap[:, 0:nb * 128], in_=pg.ap[:, 0:nb * 128], func=AF.Silu), reads=[pg.b], writes=[sgt.b])
                P.op("dve", I("tensor_tensor", out=a.ap[:, ffc, 0:nb * 128], in0=sgt.ap[:, 0:nb * 128], in1=pu.ap[:, 0:nb * 128], op=ALU.mult),
                     reads=[sgt.b, pu.b], writes=[a.b])

        def down_epi(bix):
            blk = blks[bix]
            want = 1 if blk[0] < 2 else 0
            if want != state["gt"]:
                load_Gt(want)
                state["gt"] = want
            for j, n in enumerate(blk):
                h = hblk[bix % 2][j]
                for half in range(2):
                    for ffc in range(22):
                        P.op("pe", I("matmul", py.ap[:, half * 512:(half + 1) * 512], lhsT=a.ap[:, ffc, j * 128:(j + 1) * 128], rhs=Wd.ap[:, ffc, half * 512:(half + 1) * 512],
                                     start=(ffc == 0), stop=(ffc == 21)), reads=[a.b, Wd.b], writes=[py.b])
                s2 = st[2 + j % 2]
                jk = junk[j % 2]
                P.op("act", I("activation", out=jk.ap, in_=py.ap, func=AF.Square, accum_out=s2.ap[:, 0:1]), reads=[py.b], writes=[jk.b, s2.b])
                P.op("act", I("activation", out=s2.ap[:, 1:2], in_=s2.ap[:, 0:1], func=AF.Sqrt, scale=1.0 / D, bias=EPS), reads=[s2.b], writes=[s2.b])
                P.op("dve", I("reciprocal", out=s2.ap[:, 2:3], in_=s2.ap[:, 1:2]), reads=[s2.b], writes=[s2.b])
                P.op("dve", I("scalar_tensor_tensor", out=tmp.ap, in0=py.ap, scalar=s2.ap[:, 2:3], in1=Gt.ap, op0=ALU.mult, op1=ALU.mult),
                     reads=[py.b, s2.b, Gt.b], writes=[tmp.b])
                P.op("pool", I("tensor_tensor", out=h.ap, in0=tmp.ap, in1=h.ap, op=ALU.add), reads=[tmp.b, h.b], writes=[h.b])
                if l == 0:
                    P.dma("pool", hbd.ap[n * 128:(n + 1) * 128, :], h.ap, reads=[h.b], writes=[self.hb[n]])
                else:
                    P.dma("pool", self.y.ap[(n - 2) * 128:(n - 1) * 128, :], h.ap, reads=[h.b], writes=[self.y.b])

        norm_front(0)
        for bix in range(len(blks)):
            transposes(bix)
            gate_up(bix)
            if bix + 1 < len(blks):
                norm_front(bix + 1)
            down_epi(bix)

    def attn_setup(self):
        k, P = self.k, self.P
        A = {}
        A["pt"] = [k.sb(512, BF16) for _ in range(3)]
        A["osb"] = [k.sb(512, F32) for _ in range(2)]
        A["om"] = [k.sb(512, BF16) for _ in range(2)]
        A["ones"] = k.sb(128, F32)
        P.op("pool", I("memset", A["ones"].ap, 1.0), writes=[A["ones"].b])
        A["ps_s"] = [k.ps(0), k.ps(1), k.ps(2)]
        A["ps_o"] = [k.ps(3), k.ps(4)]
        A["ps_b"] = k.ps(5)
        A["cnt"] = 0
        A["qcnt"] = 0
        return A

    def attend(self, A, KT, QT, kd, vfn, vb, ktiles, c0, nq, scale, dst_ap, dst_b):
        P = self.P
        qi = A["qcnt"]
        A["qcnt"] += 1
        po = A["ps_o"][qi % 2]
        osb, om = A["osb"][qi % 2], A["om"][qi % 2]
        nk = len(ktiles)
        slots = []

        def qk(j):
            i = A["cnt"]
            A["cnt"] += 1
            ps, pt = A["ps_s"][i % 3], A["pt"][i % 3]
            kt = ktiles[j]
            P.op("pe", I("matmul", ps.ap[:, 0:nq], lhsT=KT.ap[0:kd, kt * 128:(kt + 1) * 128], rhs=QT.ap[0:kd, c0:c0 + nq], start=True, stop=True),
                 reads=[KT.b, QT.b], writes=[ps.b])
            P.op("act", I("activation", out=pt.ap[:, 0:nq], in_=ps.ap[:, 0:nq], func=AF.Exp, scale=scale), reads=[ps.b], writes=[pt.b])
            slots.append(pt)

        LA = 2
        for j in range(min(LA, nk)):
            qk(j)
        for j in range(nk):
            if j + LA < nk:
                qk(j + LA)
            pt = slots[j]
            P.op("pe", I("matmul", po.ap[:, 0:nq], lhsT=vfn(ktiles[j]), rhs=pt.ap[:, 0:nq], start=(j == 0), stop=(j == nk - 1)),
                 reads=[vb, pt.b], writes=[po.b])
        P.op("act", I("activation", out=osb.ap[0:65, 0:nq], in_=po.ap[0:65, 0:nq], func=AF.Copy), reads=[po.b], writes=[osb.b])
        P.op("dve", I("reciprocal", out=osb.ap[64:65, 0:nq], in_=osb.ap[64:65, 0:nq]), reads=[osb.b], writes=[osb.b])
        pb = A["ps_b"]
        P.op("pe", I("matmul", pb.ap[0:64, 0:nq], lhsT=A["ones"].ap[64:65, 0:64], rhs=osb.ap[64:65, 0:nq], start=True, stop=True),
             reads=[A["ones"].b, osb.b], writes=[pb.b])
        P.op("dve", I("tensor_tensor", out=om.ap[0:64, 0:nq], in0=osb.ap[0:64, 0:nq], in1=pb.ap[0:64, 0:nq], op=ALU.mult),
             reads=[osb.b, pb.b], writes=[om.b])
        P.dma("sp", dst_ap, om.ap[0:64, 0:nq], reads=[om.b], writes=[dst_b])

    def s_mla(self):
        k, P, d, T, NT, SEQ = self.k, self.P, self.din, self.T, self.NT, self.SEQ
        sc = self.dsc
        qn = k.sb([2, T], BF16, "qn"); kvn = k.sb(T, BF16, "kvn")
        for kk in range(2):
            P.dma("sp", qn.ap[:, kk, :], sc["qn_d"].ap[kk], reads=[sc["qn_d"].b], writes=[qn.b])
        P.dma("sp", kvn.ap, sc["kvn_d"].ap, reads=[sc["kvn_d"].b], writes=[kvn.b])
        Wq = k.sb([2, 768], BF16, "Wq"); Wkv = k.sb(1024, BF16, "Wkv"); Wqr = k.sb([2, 256], BF16, "Wqr")
        wqv = d["mla_w_q_up"][0].rearrange("(k p) n -> p k n", p=128)
        for kk in range(2):
            P.dma("pool", Wq.ap[:, kk, :], wqv[:, kk, :], writes=[Wq.b])
        P.dma("pool", Wkv.ap, d["mla_w_kv_up"][0], writes=[Wkv.b])
        wq4 = Wq.ap.rearrange("p k (h c) -> p k h c", h=8)
        wr4 = Wqr.ap.rearrange("p k (h c) -> p k h c", h=8)
        P.op("act", I("activation", out=wr4[:, :, :, 0:16], in_=wq4[:, :, :, 80:96], func=AF.Copy, scale=-1.0), reads=[Wq.b], writes=[Wqr.b])
        P.op("act", I("activation", out=wr4[:, :, :, 16:32], in_=wq4[:, :, :, 64:80], func=AF.Copy), reads=[Wq.b], writes=[Wqr.b])
        rope = k.sb([2, T], F32, "rope")
        for i in range(2):
            P.dma("sp", rope.ap[64:96, i, :], d["k_rope32"][i], writes=[rope.b])
        KT = [k.sb(T, BF16, "KT%d" % i) for i in range(2)]
        QT = [k.sb(T, BF16, "QT%d" % i) for i in range(2)]
        for i in range(2):
            P.dma("sp", KT[i].ap[64:96, :], sc["kpe_d"].ap, reads=[sc["kpe_d"].b], writes=[KT[i].b])
        Vall = k.sb([NT, 8 * 128], BF16, "Vall")
        V4 = Vall.ap.rearrange("p n (h c) -> p n h c", h=8)
        P.op("pool", I("memset", Vall.ap, 1.0), writes=[Vall.b])
        A = self.attn_setup()
        pv = [k.ps(6), k.ps(7)]
        import os
        if os.environ.get("MSUB", "") == "a":
            return
        wkv3 = Wkv.ap.rearrange("p (h c) -> p h c", h=8)
        for t in range(NT):
            ps = pv[t % 2]
            P.op("pe", I("matmul", ps.ap.rearrange("p (h c) -> p h c", h=8), lhsT=kvn.ap[:, t * 128:(t + 1) * 128], rhs=wkv3[:, :, 64:128], start=True, stop=True),
                 reads=[kvn.b, Wkv.b], writes=[ps.b])
            P.op("dve", I("tensor_copy", out=V4[:, t, :, 0:64], in_=ps.ap.rearrange("p (h c) -> p h c", h=8)), reads=[ps.b], writes=[Vall.b])
        if os.environ.get("MSUB", "") == "b":
            return
        t1 = [k.sb(512, F32) for _ in range(2)]; t2 = [k.sb(512, F32) for _ in range(2)]
        pk, pqn, pqp, pqr = pv[0], pv[1], pv[0], pv[1]
        import os
        MSUB = os.environ.get("MSUB", "")
        blocks = self.tok_blocks()
        mix = sc["mixT_d"]
        scale = float(96 ** -0.5)
        for h in range(8):
            kt_, qt_ = KT[h % 2], QT[h % 2]
            for bi, (c0, n) in enumerate(blocks):
                P.op("pe", I("matmul", pk.ap[0:64, 0:n], lhsT=wkv3[:, h, 0:64], rhs=kvn.ap[:, c0:c0 + n], start=True, stop=True),
                     reads=[Wkv.b, kvn.b], writes=[pk.b])
                P.op("act", I("activation", out=kt_.ap[0:64, c0:c0 + n], in_=pk.ap[0:64, 0:n], func=AF.Copy), reads=[pk.b], writes=[kt_.b])
                for kk in range(2):
                    P.op("pe", I("matmul", pqn.ap[0:64, 0:n], lhsT=wq4[:, kk, h, 0:64], rhs=qn.ap[:, kk, c0:c0 + n], start=(kk == 0), stop=(kk == 1)),
                         reads=[Wq.b, qn.b], writes=[pqn.b])
                for kk in range(2):
                    P.op("pe", I("matmul", pqp.ap[64:96, 0:n], lhsT=wq4[:, kk, h, 64:96], rhs=qn.ap[:, kk, c0:c0 + n], start=(kk == 0), stop=(kk == 1)),
                         reads=[Wq.b, qn.b], writes=[pqp.b])
                P.op("dve", I("tensor_copy", out=qt_.ap[0:64, c0:c0 + n], in_=pqn.ap[0:64, 0:n]), reads=[pqn.b], writes=[qt_.b])
                a1, a2 = t1[bi % 2], t2[bi % 2]
                P.op("dve", I("tensor_tensor", out=a1.ap[64:96, 0:n], in0=pqp.ap[64:96, 0:n], in1=rope.ap[64:96, 0, c0:c0 + n], op=ALU.mult),
                     reads=[pqp.b, rope.b], writes=[a1.b])
                for kk in range(2):
                    P.op("pe", I("matmul", pqr.ap[64:96, 0:n], lhsT=wr4[:, kk, h, :], rhs=qn.ap[:, kk, c0:c0 + n], start=(kk == 0), stop=(kk == 1)),
                         reads=[Wqr.b, qn.b], writes=[pqr.b])
                P.op("dve", I("tensor_tensor", out=a2.ap[64:96, 0:n], in0=pqr.ap[64:96, 0:n], in1=rope.ap[64:96, 1, c0:c0 + n], op=ALU.mult),
                     reads=[pqr.b, rope.b], writes=[a2.b])
                P.op("pool", I("tensor_tensor", out=qt_.ap[64:96, c0:c0 + n], in0=a1.ap[64:96, 0:n], in1=a2.ap[64:96, 0:n], op=ALU.add),
                     reads=[a1.b, a2.b], writes=[qt_.b])
            if MSUB == "c":
                continue
            dst = mix.ap[4 + h // 2, (h % 2) * 64:(h % 2) * 64 + 64, :]
            vfn = lambda kt, h=h: V4[:, kt, h, :]
            for bi, (c0, n) in enumerate(blocks):
                ktl = [0, 1] if c0 < CTX else list(range(NT))
                self.attend(A, kt_, qt_, 96, vfn, Vall.b, ktl, c0, n, scale, dst[:, c0:c0 + n], mix.b)

    def s_inproj_odd(self, pre):
        k, P, d, T, NT, SEQ = self.k, self.P, self.din, self.T, self.NT, self.SEQ
        sc = self.dsc
        uT, W = pre
        Wr = k.sb([8, 640], BF16, "Wr")
        w4 = W.ap[:, :, 1536:2176].rearrange("p k (h c) -> p k h c", h=10)
        r4 = Wr.ap.rearrange("p k (h c) -> p k h c", h=10)
        P.op("act", I("activation", out=r4[:, :, :, 0:32], in_=w4[:, :, :, 32:64], func=AF.Copy, scale=-1.0), reads=[W.b], writes=[Wr.b])
        P.op("act", I("activation", out=r4[:, :, :, 32:64], in_=w4[:, :, :, 0:32], func=AF.Copy), reads=[W.b], writes=[Wr.b])
        identf = k.sb(128, F32, "identf")
        P.dma("sp", identf.ap, d["k_identf"], writes=[identf.b])
        ones = k.sb(128, F32, "ones")
        P.op("pool", I("memset", ones.ap, 1.0), writes=[ones.b])
        grow = k.sb(64, F32, "grow")
        for i, nm in enumerate(("gqa_q_norm", "gqa_k_norm")):
            g = d[nm][0]
            P.dma("sp", grow.ap[2 * i:2 * i + 1, :], g.rearrange("(a n) -> a n", a=1), writes=[grow.b])
            P.dma("sp", grow.ap[2 * i + 1:2 * i + 2, 0:32], g[32:64].rearrange("(a n) -> a n", a=1), writes=[grow.b])
            P.dma("sp", grow.ap[2 * i + 1:2 * i + 2, 32:64], g[0:32].rearrange("(a n) -> a n", a=1), writes=[grow.b])
        gcol = k.sb(4, F32, "gcol")
        pgc = k.ps(7)
        P.op("pe", I("transpose", out=pgc.ap[0:64, 0:4], in_=grow.ap[0:4, :], identity=identf.ap[0:4, 0:4]), reads=[grow.b, identf.b], writes=[pgc.b])
        P.op("dve", I("tensor_copy", out=gcol.ap[0:64], in_=pgc.ap[0:64, 0:4]), reads=[pgc.b], writes=[gcol.b])
        rope = k.sb([2, T], F32, "rope64")
        for i in range(2):
            P.dma("sp", rope.ap[0:64, i, :], d["k_rope64"][i], writes=[rope.b])
        blocks = self.tok_blocks()
        stage = [k.sb(T, BF16, "hstage%d" % i) for i in range(2)]
        pz = [k.ps(0), k.ps(1)]
        oq = sc["oq_d"]
        import os
        OSUB = os.environ.get("OSUB", "")
        if OSUB == "a":
            return
        it = 0
        for hh in range(16):
            sg = stage[hh % 2]
            for (c0, n) in blocks:
                ps = pz[it % 2]
                it += 1
                for kk in range(8):
                    P.op("pe", I("matmul", ps.ap[0:64, 0:n], lhsT=W.ap[:, kk, hh * 64:(hh + 1) * 64], rhs=uT.ap[:, kk, c0:c0 + n], start=(kk == 0), stop=(kk == 7)),
                         reads=[W.b, uT.b], writes=[ps.b])
                P.op("act", I("activation", out=sg.ap[0:64, c0:c0 + n], in_=ps.ap[0:64, 0:n], func=AF.Copy), reads=[ps.b], writes=[sg.b])
            P.dma("sp", oq.ap[hh], sg.ap[0:64, :], reads=[sg.b], writes=[oq.b])
        if OSUB == "b":
            return
        gk = sc["gk_d"]
        pr = [k.ps(2), k.ps(3)]
        pss = [k.ps(4), k.ps(5)]
        sq = [k.sb(512, F32) for _ in range(2)]; rn = [k.sb(512, F32) for _ in range(2)]
        zg = [k.sb(512, F32) for _ in range(2)]; zr = [k.sb(512, F32) for _ in range(2)]
        for hh in range(10):
            sg = stage[hh % 2]
            gi = 0 if hh < 8 else 2
            wc = 1536 + hh * 64
            for bi, (c0, n) in enumerate(blocks):
                ps, ps2, p3 = pz[it % 2], pr[it % 2], pss[it % 2]
                s2, r2, a1, a2 = sq[it % 2], rn[it % 2], zg[it % 2], zr[it % 2]
                it += 1
                for kk in range(8):
                    P.op("pe", I("matmul", ps.ap[0:64, 0:n], lhsT=W.ap[:, kk, wc:wc + 64], rhs=uT.ap[:, kk, c0:c0 + n], start=(kk == 0), stop=(kk == 7)),
                         reads=[W.b, uT.b], writes=[ps.b])
                for kk in range(8):
                    P.op("pe", I("matmul", ps2.ap[0:64, 0:n], lhsT=Wr.ap[:, kk, hh * 64:(hh + 1) * 64], rhs=uT.ap[:, kk, c0:c0 + n], start=(kk == 0), stop=(kk == 7)),
                         reads=[Wr.b, uT.b], writes=[ps2.b])
                P.op("act", I("activation", out=s2.ap[0:64, 0:n], in_=ps.ap[0:64, 0:n], func=AF.Square), reads=[ps.b], writes=[s2.b])
                P.op("pe", I("matmul", p3.ap[0:64, 0:n], lhsT=ones.ap[0:64, 0:64], rhs=s2.ap[0:64, 0:n], start=True, stop=True), reads=[ones.b, s2.b], writes=[p3.b])
                P.op("act", I("activation", out=r2.ap[0:64, 0:n], in_=p3.ap[0:64, 0:n], func=AF.Sqrt, scale=1.0 / 64, bias=EPS), reads=[p3.b], writes=[r2.b])
                P.op("dve", I("reciprocal", out=r2.ap[0:64, 0:n], in_=r2.ap[0:64, 0:n]), reads=[r2.b], writes=[r2.b])
                P.op("dve", I("scalar_tensor_tensor", out=a1.ap[0:64, 0:n], in0=ps.ap[0:64, 0:n], scalar=gcol.ap[0:64, gi:gi + 1], in1=rope.ap[0:64, 0, c0:c0 + n],
                              op0=ALU.mult, op1=ALU.mult), reads=[ps.b, gcol.b, rope.b], writes=[a1.b])
                P.op("dve", I("scalar_tensor_tensor", out=a2.ap[0:64, 0:n], in0=ps2.ap[0:64, 0:n], scalar=gcol.ap[0:64, gi + 1:gi + 2], in1=rope.ap[0:64, 1, c0:c0 + n],
                              op0=ALU.mult, op1=ALU.mult), reads=[ps2.b, gcol.b, rope.b], writes=[a2.b])
                P.op("pool", I("tensor_tensor", out=a1.ap[0:64, 0:n], in0=a1.ap[0:64, 0:n], in1=a2.ap[0:64, 0:n], op=ALU.add), reads=[a1.b, a2.b], writes=[a1.b])
                P.op("pool", I("tensor_tensor", out=sg.ap[0:64, c0:c0 + n], in0=a1.ap[0:64, 0:n], in1=r2.ap[0:64, 0:n], op=ALU.mult), reads=[a1.b, r2.b], writes=[sg.b])
            if hh < 8:
                P.dma("sp", oq.ap[16 + hh], sg.ap[0:64, :], reads=[sg.b], writes=[oq.b])
            else:
                P.dma("sp", gk.ap[hh - 8], sg.ap[0:64, :], reads=[sg.b], writes=[gk.b])
        if OSUB == "c":
            return
        ov = sc["ov_d"]
        vst = [k.sb(640, BF16) for _ in range(2)]
        pv1 = [k.ps(0), k.ps(1)]; pv2 = [k.ps(2), k.ps(3)]
        for n in range(NT):
            p1, p2, vs = pv1[n % 2], pv2[n % 2], vst[n % 2]
            for kk in range(8):
                P.op("pe", I("matmul", p1.ap, lhsT=uT.ap[:, kk, n * 128:(n + 1) * 128], rhs=W.ap[:, kk, 1024:1536], start=(kk == 0), stop=(kk == 7)),
                     reads=[W.b, uT.b], writes=[p1.b])
            for kk in range(8):
                P.op("pe", I("matmul", p2.ap[:, 0:128], lhsT=uT.ap[:, kk, n * 128:(n + 1) * 128], rhs=W.ap[:, kk, 2176:2304], start=(kk == 0), stop=(kk == 7)),
                     reads=[W.b, uT.b], writes=[p2.b])
            P.op("act", I("activation", out=vs.ap[:, 0:512], in_=p1.ap, func=AF.Copy), reads=[p1.b], writes=[vs.b])
            P.op("dve", I("tensor_copy", out=vs.ap[:, 512:640], in_=p2.ap[:, 0:128]), reads=[p2.b], writes=[vs.b])
            P.dma("sp", ov.ap[n * 128:(n + 1) * 128, :], vs.ap, reads=[vs.b], writes=[ov.b])

    def s_na(self):
        k, P, d, T, NT, SEQ = self.k, self.P, self.din, self.T, self.NT, self.SEQ
        sc = self.dsc
        nqt = SEQ // 128
        oq, ov, mix = sc["oq_d"], sc["ov_d"], sc["mixT_d"]
        QT = [k.sb(T, BF16) for _ in range(2)]; KT = [k.sb(T, BF16) for _ in range(2)]
        Vh = [k.sb([NT, 128], BF16) for _ in range(2)]
        Bs = [k.sb([5, 640], F32) for _ in range(2)]
        for i in range(2):
            P.op("pool", I("memset", Vh[i].ap, 1.0), writes=[Vh[i].b])
        sb = [k.sb(640, F32) for _ in range(2)]
        pt = [k.sb(896, BF16) for _ in range(2)]
        osb = [k.sb(512, F32) for _ in range(2)]; om = [k.sb(512, BF16) for _ in range(2)]
        ones = k.sb(128, F32)
        P.op("pool", I("memset", ones.ap, 1.0), writes=[ones.b])
        pS = [Tl(k.psum[:, 0:1024]), Tl(k.psum[:, 1024:2048])]
        pO = [k.ps(4), k.ps(5)]
        pB = k.ps(6)
        scale = 0.125
        ovv = ov.ap.rearrange("(n p) c -> p n c", p=128)
        cnt = {"it": 0}
        def na_load(h):
            q_, k_, v_, b_ = QT[h % 2], KT[h % 2], Vh[h % 2], Bs[h % 2]
            P.dma("sp", q_.ap[0:64, :], oq.ap[h], reads=[oq.b], writes=[q_.b])
            P.dma("sp", k_.ap[0:64, :], oq.ap[8 + h], reads=[oq.b], writes=[k_.b])
            P.dma("sp", v_.ap[:, :, 0:64], ovv[:, :, h * 64:(h + 1) * 64], reads=[ov.b], writes=[v_.b])
            P.dma("sp", b_.ap, d["na_bias"][:, h].rearrange("c p n -> p c n"), writes=[b_.b])

        na_load(0)
        for h in range(8):
            q_, k_, v_, b_ = QT[h % 2], KT[h % 2], Vh[h % 2], Bs[h % 2]
            if h + 1 < 8:
                na_load(h + 1)
            dst = mix.ap[h // 2, (h % 2) * 64:(h % 2) * 64 + 64, :]
            def na_qk(qt):
                i = cnt["it"]
                cnt["it"] += 1
                cls = na_class(qt, nqt)
                kt0 = int(np.clip(qt - 2, 0, nqt - 5))
                tiles = [2 + kt0 + j for j in range(5)] + [0, 1]
                ps, s_, p_ = pS[i % 2], sb[i % 2], pt[i % 2]
                qc = CTX + qt * 128
                for j, t in enumerate(tiles):
                    P.op("pe", I("matmul", ps.ap[:, j * 128:(j + 1) * 128], lhsT=k_.ap[0:64, t * 128:(t + 1) * 128], rhs=q_.ap[0:64, qc:qc + 128], start=True, stop=True),
                         reads=[k_.b, q_.b], writes=[ps.b])
                P.op("dve", I("scalar_tensor_tensor", out=s_.ap, in0=ps.ap[:, 0:640], scalar=scale, in1=b_.ap[:, cls, :], op0=ALU.mult, op1=ALU.add),
                     reads=[ps.b, b_.b], writes=[s_.b])
                P.op("act", I("activation", out=p_.ap[:, 0:640], in_=s_.ap, func=AF.Exp), reads=[s_.b], writes=[p_.b])
                P.op("act", I("activation", out=p_.ap[:, 640:896], in_=ps.ap[:, 640:896], func=AF.Exp, scale=scale), reads=[ps.b], writes=[p_.b])
                return tiles, p_

            pend = na_qk(0)
            for g0 in range(0, nqt, 4):
                gi = (h * (nqt // 4) + g0 // 4)
                po, os_, om_ = pO[gi % 2], osb[gi % 2], om[gi % 2]
                for qq in range(4):
                    qt = g0 + qq
                    tiles, p_ = pend
                    if qt + 1 < nqt:
                        pend = na_qk(qt + 1)
                    for j, t in enumerate(tiles):
                        P.op("pe", I("matmul", po.ap[:, qq * 128:(qq + 1) * 128], lhsT=v_.ap[:, t, :], rhs=p_.ap[:, j * 128:(j + 1) * 128], start=(j == 0), stop=(j == 6)),
                             reads=[v_.b, p_.b], writes=[po.b])
                c0 = CTX + g0 * 128
                P.op("act", I("activation", out=os_.ap[0:65, :], in_=po.ap[0:65, :], func=AF.Copy), reads=[po.b], writes=[os_.b])
                P.op("dve", I("reciprocal", out=os_.ap[64:65, :], in_=os_.ap[64:65, :]), reads=[os_.b], writes=[os_.b])
                P.op("pe", I("matmul", pB.ap[0:64, :], lhsT=ones.ap[64:65, 0:64], rhs=os_.ap[64:65, :], start=True, stop=True), reads=[ones.b, os_.b], writes=[pB.b])
                P.op("dve", I("tensor_tensor", out=om_.ap[0:64, :], in0=os_.ap[0:64, :], in1=pB.ap[0:64, :], op=ALU.mult), reads=[os_.b, pB.b], writes=[om_.b])
                P.dma("sp", dst[:, c0:c0 + 512], om_.ap[0:64, :], reads=[om_.b], writes=[mix.b])

    def s_gqa(self):
        k, P, d, T, NT, SEQ = self.k, self.P, self.din, self.T, self.NT, self.SEQ
        sc = self.dsc
        ffn_w = self.ffn_weights(1)
        oq, ov, mix, gk = sc["oq_d"], sc["ov_d"], sc["mixT_d"], sc["gk_d"]
        KT = [k.sb(T, BF16) for _ in range(2)]
        for i in range(2):
            P.dma("sp", KT[i].ap[0:64, :], gk.ap[i], reads=[gk.b], writes=[KT[i].b])
        V = k.sb([NT, 2 * 128], BF16)
        V4 = V.ap.rearrange("p n (h c) -> p n h c", h=2)
        P.op("pool", I("memset", V.ap, 1.0), writes=[V.b])
        ovv = ov.ap.rearrange("(n p) c -> p n c", p=128)
        for i in range(2):
            P.dma("sp", V4[:, :, i, 0:64], ovv[:, :, 512 + i * 64:512 + (i + 1) * 64], reads=[ov.b], writes=[V.b])
        QT = [k.sb(T, BF16) for _ in range(2)]
        A = self.attn_setup()
        self.ffn_weights_issue(1, ffn_w)
        blocks = [b for b in self.tok_blocks() if b[0] >= CTX]
        P.dma("sp", QT[0].ap[0:64, :], oq.ap[16], reads=[oq.b], writes=[QT[0].b])
        for h in range(8):
            q_ = QT[h % 2]
            if h + 1 < 8:
                P.dma("sp", QT[(h + 1) % 2].ap[0:64, :], oq.ap[16 + h + 1], reads=[oq.b], writes=[QT[(h + 1) % 2].b])
            kv = h // 4
            dst = mix.ap[4 + h // 2, (h % 2) * 64:(h % 2) * 64 + 64, :]
            vfn = lambda kt, kv=kv: V4[:, kt, kv, :]
            for (c0, n) in blocks:
                self.attend(A, KT[kv], q_, 64, vfn, V.b, list(range(NT)), c0, n, 0.125, dst[:, c0:c0 + n], mix.b)


def _rope_tables(SEQ, rot_dim):
    t = np.arange(SEQ)
    row = (t // 64).astype(np.float32)
    col = (t % 64).astype(np.float32)
    nf = rot_dim // 4
    inv = (np.float32(10000.0) ** (-np.arange(nf, dtype=np.float32) / nf)).astype(np.float32)
    ang = np.concatenate([row[:, None] * inv, col[:, None] * inv], -1)
    half = rot_dim // 2
    idx = np.arange(rot_dim) % half
    cos = np.ones((rot_dim, CTX + SEQ), np.float32)
    sin = np.zeros((rot_dim, CTX + SEQ), np.float32)
    cos[:, CTX:] = np.cos(ang)[:, idx].T
    sin[:, CTX:] = np.sin(ang)[:, idx].T
    return np.stack([cos, sin]).astype(np.float32)


def _na_bias(rpb, SEQ):
    rows = SEQ // 64
    nqt = rows // 2
    out = np.full((5, 8, 128, 640), NEG, np.float32)
    reps = {0: 0, 1: 1, 2: min(2, nqt - 3), 3: nqt - 2, 4: nqt - 1}
    for cls, qt in reps.items():
        kt0 = int(np.clip(qt - 2, 0, nqt - 5))
        for qi in range(128):
            r, cc = 2 * qt + qi // 64, qi % 64
            rs = int(np.clip(r - 4, 0, rows - 8))
            cs = int(np.clip(cc - 8, 0, 64 - 16))
            for kr in range(rs, rs + 8):
                j = kr // 2 - kt0
                p0 = (kr % 2) * 64
                kc = np.arange(cs, cs + 16)
                out[cls, :, p0 + kc, j * 128 + qi] = rpb[:, kr - r + 7, kc - cc + 15].T
    return out


def host_consts(SEQ):
    import ml_dtypes
    i = np.arange(128)
    tril = np.stack([(i[:, None] <= i[None, :]), (i[:, None] >= i[None, :])]).astype(np.float32)
    mstrict = np.stack([(i[None, :] < i[:, None]), (i[None, :] > i[:, None])]).astype(np.float32)
    gm = np.zeros((7, 128, 128), np.float32)
    cc, ss = i[:, None], i[None, :]
    gm[0] = (cc // 16 == ss // 16)
    for li, B in enumerate((32, 64, 128)):
        H = B // 2
        m = (cc // B == ss // B) & (cc % B >= H) & (ss % B < H)
        gm[1 + li] = m
        gm[4 + li] = m.T
    return {
        "k_gmask": gm,
        "k_identf": np.eye(128, dtype=np.float32),
        "k_identb": np.eye(128, dtype=np.float32).astype(ml_dtypes.bfloat16),
        "k_tril": tril, "k_mstrict": mstrict,
        "k_rope32": _rope_tables(SEQ, 32), "k_rope64": _rope_tables(SEQ, 64),
    }


def na_class(qt, nqt):
    if qt < 2:
        return qt
    if qt >= nqt - 2:
        return 4 - (nqt - 1 - qt)
    return 2


_CACHE = {}


def core_inputs(inputs, b, SEQ):
    m = {}
    m["x"] = np.ascontiguousarray(inputs["x"][b])
    m["ctx"] = np.ascontiguousarray(inputs["ctx"][b])
    m["c"] = np.ascontiguousarray(inputs["c"][b])
    for k_ in ("c_ctx", "w_mod", "b_mod", "g_pre_mix", "g_post_mix", "g_pre_ffn", "g_post_ffn", "w_ffn_gate", "w_ffn_up",
               "w_ffn_down", "w_in_even", "w_out_even", "gdn_conv", "gdn_a_log", "gdn_dt_bias", "gdn_out_norm", "mla_q_norm",
               "mla_w_q_up", "mla_kv_norm", "mla_w_kv_up", "w_in_odd", "w_out_odd", "gqa_q_norm", "gqa_k_norm"):
        m[k_] = np.ascontiguousarray(np.asarray(inputs[k_], dtype=np.float32))
    return m


def kernel(**inputs):
    inputs = {k_: np.asarray(v) for k_, v in inputs.items()}
    B, SEQ = inputs["x"].shape[0], inputs["x"].shape[1]
    if SEQ not in _CACHE:
        _CACHE[SEQ] = Builder(SEQ).build()
    nc = _CACHE[SEQ]
    consts = host_consts(SEQ)
    nab = _na_bias(np.asarray(inputs["na_rpb"][0], np.float32), SEQ)
    in_maps = []
    for b in range(B):
        m = core_inputs(inputs, b, SEQ)
        m.update(consts)
        m["na_bias"] = nab
        in_maps.append(m)
    res = run_bass_kernel_spmd(nc, in_maps, core_ids=list(range(B)))
    return np.stack([np.asarray(r["y"], dtype=np.float32) for r in res.results], 0)
```
